# Optimizing a Trainium2 kernel written in Bass

```python
import jax, jax.numpy as jnp
from jax import lax
import numpy as np

D_MODEL = 1024
BATCH = 4
SEQ = 8192
DEPTH = 4

N_MIXERS = 2
N_RG_LAYERS = (DEPTH + 1) // 2
N_NSA_LAYERS = DEPTH // 2
RMS_EPS = 1e-6
D_FF = 4 * D_MODEL
D_RNN = D_MODEL
RG_BLOCKS = 4
RG_BLOCK_W = D_RNN // RG_BLOCKS
CONV_W = 4
RG_C = 8.0
N_HEADS = 16
HEAD_DIM = D_MODEL // N_HEADS
N_KV_GROUPS = 4
HEADS_PER_GROUP = N_HEADS // N_KV_GROUPS
CMP_LEN = 32
CMP_STRIDE = 16
CMP_RATIO = CMP_LEN // CMP_STRIDE
SLC_LEN = 64
N_SELECT = 16
WINDOW = 512
Q_BLOCK = 128
Q_COLS = N_HEADS * HEAD_DIM
KV_COLS = N_KV_GROUPS * HEAD_DIM
GATE_COLS = 3 * N_HEADS
NSA_IN_COLS = Q_COLS + 6 * KV_COLS + GATE_COLS
ALIBI_MAX = 8.0
NEG_INF = -1e30
FORCE_BONUS = 1e4

kernel_name = 'hybrid_rglru_nsa_trunk'


def _rmsnorm(x, g):
    xf = x.astype(jnp.float32)
    y = xf * lax.rsqrt(jnp.mean(xf * xf, axis=-1, keepdims=True) + RMS_EPS)
    return (y * g.astype(jnp.float32)).astype(x.dtype)


def _masked_softmax(s, mask):
    s = jnp.where(mask, s, NEG_INF)
    m = jnp.max(s, axis=-1, keepdims=True)
    p = jnp.where(mask, jnp.exp(s - m), 0.0)
    return p / jnp.maximum(jnp.sum(p, axis=-1, keepdims=True), 1e-30)


def _alibi_slopes():
    h = jnp.arange(1, N_HEADS + 1, dtype=jnp.float32)
    return jnp.exp2(-ALIBI_MAX * h / N_HEADS).reshape(N_KV_GROUPS, HEADS_PER_GROUP)


def _sq_relu_mlp(h, w_up, w_down):
    u = jax.nn.relu(h @ w_up)
    return (u * u) @ w_down


def _causal_depthwise_conv(x, w, b):
    y = lax.conv_general_dilated(
        x, w[:, None, :].astype(x.dtype), window_strides=(1,),
        padding=[(CONV_W - 1, 0)], dimension_numbers=('NWC', 'WIO', 'NWC'),
        feature_group_count=x.shape[-1])
    return y + b


def _lru_combine(left, right):
    a_l, b_l = left
    a_r, b_r = right
    return a_l * a_r, a_r * b_l + b_r


def _rglru_mixer(h, w_in, conv_w, conv_b, w_a, b_a, w_x, b_x, lam, w_out):
    B, S, _ = h.shape
    proj = h @ w_in
    y = jax.nn.gelu(proj[..., :D_RNN], approximate=True)
    xb = _causal_depthwise_conv(proj[..., D_RNN:], conv_w, conv_b)
    xr = xb.reshape(B, S, RG_BLOCKS, RG_BLOCK_W)
    r = jax.nn.sigmoid(jnp.einsum('bsnc,ncd->bsnd', xr, w_a).reshape(B, S, D_RNN) + b_a)
    i = jax.nn.sigmoid(jnp.einsum('bsnc,ncd->bsnd', xr, w_x).reshape(B, S, D_RNN) + b_x)
    log_a = -RG_C * r.astype(jnp.float32) * jax.nn.softplus(-lam.astype(jnp.float32))
    a = jnp.exp(log_a)
    u = jnp.sqrt(-jnp.expm1(2.0 * log_a)) * (i * xb).astype(jnp.float32)
    _, hs = lax.associative_scan(_lru_combine, (a, u), axis=1)
    return (hs.astype(h.dtype) * y) @ w_out


def _compress(z, pe, w1, w2):
    B, S, G, DH = z.shape
    n_chunk = S // CMP_STRIDE
    nc = n_chunk - CMP_RATIO + 1
    chunks = z.reshape(B, n_chunk, CMP_STRIDE, G, DH)
    blocks = jnp.concatenate([chunks[:, r:r + nc] for r in range(CMP_RATIO)], axis=2)
    blocks = blocks + pe[None, None, :, None, :]
    flat = jnp.transpose(blocks, (0, 1, 3, 2, 4)).reshape(B, nc, G, CMP_LEN * DH)
    return jax.nn.gelu(flat @ w1, approximate=True) @ w2


def _cmp_to_slc_map(nc, nb):
    cs = jnp.arange(nc) * CMP_STRIDE
    ss = jnp.arange(nb) * SLC_LEN
    ov = jnp.minimum(cs[:, None] + CMP_LEN, ss[None, :] + SLC_LEN) - jnp.maximum(cs[:, None], ss[None, :])
    return (jnp.maximum(ov, 0) / CMP_STRIDE).astype(jnp.float32)


def _nsa_attend(q, gates, k_c, v_c, k_s, v_s, k_w, v_w):
    B, S = q.shape[0], q.shape[1]
    nqb = S // Q_BLOCK
    nc = k_c.shape[1]
    nb = k_s.shape[2]
    n_sel = min(N_SELECT, nb)
    G, Hg, DH = N_KV_GROUPS, HEADS_PER_GROUP, HEAD_DIM
    qx = q.reshape(B * nqb, Q_BLOCK, G, Hg, DH)
    gx = gates.reshape(B * nqb, Q_BLOCK, G, Hg, 3)
    slopes = _alibi_slopes()[:, :, None, None]
    cmp_map = _cmp_to_slc_map(nc, nb)
    cmp_end = jnp.arange(nc) * CMP_STRIDE + (CMP_LEN - 1)
    blk_ids = jnp.arange(nb)
    key_off = jnp.arange(SLC_LEN)
    win_off = jnp.arange(WINDOW + Q_BLOCK)
    gi = jnp.arange(G)[:, None, None]
    scale = HEAD_DIM ** -0.5
    zero = jnp.zeros((), jnp.int32)

    def step(args):
        idx, q_blk, g_blk = args
        b = idx // nqb
        qb = idx % nqb
        t = qb * Q_BLOCK + jnp.arange(Q_BLOCK)
        qf = q_blk.astype(jnp.float32) * scale

        kc = lax.dynamic_index_in_dim(k_c, b, 0, keepdims=False).astype(jnp.float32)
        vc = lax.dynamic_index_in_dim(v_c, b, 0, keepdims=False).astype(jnp.float32)
        d_c = t[:, None] - cmp_end[None, :]
        s_c = jnp.einsum('qghd,ngd->ghqn', qf, kc) - slopes * d_c.astype(jnp.float32)
        p_c = _masked_softmax(s_c, d_c >= 0)
        o_c = jnp.einsum('ghqn,ngd->qghd', p_c, vc)

        imp = jnp.einsum('ghqn,nj->gqj', p_c, cmp_map)
        cur = t // SLC_LEN
        forced = (blk_ids[None, :] == 0) | (blk_ids[None, :] == cur[:, None]) | (blk_ids[None, :] == cur[:, None] - 1)
        score = jnp.where(blk_ids[None, :] <= cur[:, None], imp + FORCE_BONUS * forced, -1.0)
        sel = lax.top_k(score, n_sel)[1]
        sel_ok = sel <= cur[None, :, None]

        ks = lax.dynamic_index_in_dim(k_s, b, 0, keepdims=False)
        vs = lax.dynamic_index_in_dim(v_s, b, 0, keepdims=False)
        k_sel = ks[gi, sel].astype(jnp.float32).reshape(G, Q_BLOCK, n_sel * SLC_LEN, DH)
        v_sel = vs[gi, sel].astype(jnp.float32).reshape(G, Q_BLOCK, n_sel * SLC_LEN, DH)
        pos = (sel[..., None] * SLC_LEN + key_off).reshape(G, Q_BLOCK, n_sel * SLC_LEN)
        d_s = t[None, :, None] - pos
        m_s = (d_s >= 0) & jnp.repeat(sel_ok, SLC_LEN, axis=-1)
        s_s = jnp.einsum('qghd,gqkd->ghqk', qf, k_sel) - slopes * d_s[:, None].astype(jnp.float32)
        p_s = _masked_softmax(s_s, m_s[:, None])
        o_s = jnp.einsum('ghqk,gqkd->qghd', p_s, v_sel)

        kw = lax.dynamic_slice(k_w, (b, qb * Q_BLOCK, zero, zero), (1, WINDOW + Q_BLOCK, G, DH))[0].astype(jnp.float32)
        vw = lax.dynamic_slice(v_w, (b, qb * Q_BLOCK, zero, zero), (1, WINDOW + Q_BLOCK, G, DH))[0].astype(jnp.float32)
        s_pos = qb * Q_BLOCK - WINDOW + win_off
        d_w = t[:, None] - s_pos[None, :]
        m_w = (d_w >= 0) & (d_w < WINDOW) & (s_pos[None, :] >= 0)
        s_w = jnp.einsum('qghd,kgd->ghqk', qf, kw) - slopes * d_w.astype(jnp.float32)
        p_w = _masked_softmax(s_w, m_w)
        o_w = jnp.einsum('ghqk,kgd->qghd', p_w, vw)

        g = g_blk.astype(jnp.float32)
        o = g[..., 0:1] * o_c + g[..., 1:2] * o_s + g[..., 2:3] * o_w
        return o.reshape(Q_BLOCK, N_HEADS * HEAD_DIM)

    out = lax.map(step, (jnp.arange(B * nqb, dtype=jnp.int32), qx, gx))
    return out.reshape(B, S, N_HEADS * HEAD_DIM)


def _nsa_mixer(h, w_in, b_gate, pe_k, pe_v, w1_k, w2_k, w1_v, w2_v, w_out):
    B, S, _ = h.shape
    proj = h @ w_in
    q = proj[..., :Q_COLS].reshape(B, S, N_KV_GROUPS, HEADS_PER_GROUP, HEAD_DIM)
    kv = proj[..., Q_COLS:Q_COLS + 6 * KV_COLS].reshape(B, S, 6, N_KV_GROUPS, HEAD_DIM)
    gates = jax.nn.sigmoid(proj[..., Q_COLS + 6 * KV_COLS:] + b_gate).reshape(B, S, N_HEADS, 3)
    k_c = _compress(kv[:, :, 0], pe_k, w1_k, w2_k)
    v_c = _compress(kv[:, :, 1], pe_v, w1_v, w2_v)
    nb = S // SLC_LEN
    k_s = jnp.transpose(kv[:, :, 2].reshape(B, nb, SLC_LEN, N_KV_GROUPS, HEAD_DIM), (0, 3, 1, 2, 4))
    v_s = jnp.transpose(kv[:, :, 3].reshape(B, nb, SLC_LEN, N_KV_GROUPS, HEAD_DIM), (0, 3, 1, 2, 4))
    pad = ((0, 0), (WINDOW, 0), (0, 0), (0, 0))
    k_w = jnp.pad(kv[:, :, 4], pad)
    v_w = jnp.pad(kv[:, :, 5], pad)
    o = _nsa_attend(q, gates, k_c, v_c, k_s, v_s, k_w, v_w)
    return o.astype(h.dtype) @ w_out


def setup_inputs(seed: int = 0) -> dict:
    key = jax.random.key(seed)
    ks = jax.random.split(key, 24)
    f32 = jnp.float32
    nrm = lambda k, shape, s: jax.random.normal(k, shape, f32) * s
    u = jax.random.uniform(ks[11], (N_RG_LAYERS, D_RNN), f32, minval=0.9, maxval=0.999)
    sig = u ** (1.0 / RG_C)
    return {
        'x': nrm(ks[0], (BATCH, SEQ, D_MODEL), 1.0),
        'norm_mix': 1.0 + nrm(ks[1], (DEPTH, D_MODEL), 0.02),
        'norm_ffn': 1.0 + nrm(ks[2], (DEPTH, D_MODEL), 0.02),
        'norm_final': 1.0 + nrm(ks[3], (D_MODEL,), 0.02),
        'rg_w_in': nrm(ks[4], (N_RG_LAYERS, D_MODEL, 2 * D_RNN), D_MODEL ** -0.5),
        'rg_conv_w': nrm(ks[5], (N_RG_LAYERS, CONV_W, D_RNN), CONV_W ** -0.5),
        'rg_conv_b': nrm(ks[6], (N_RG_LAYERS, D_RNN), 0.01),
        'rg_w_a': nrm(ks[7], (N_RG_LAYERS, RG_BLOCKS, RG_BLOCK_W, RG_BLOCK_W), RG_BLOCK_W ** -0.5),
        'rg_b_a': nrm(ks[8], (N_RG_LAYERS, D_RNN), 0.01),
        'rg_w_x': nrm(ks[9], (N_RG_LAYERS, RG_BLOCKS, RG_BLOCK_W, RG_BLOCK_W), RG_BLOCK_W ** -0.5),
        'rg_b_x': nrm(ks[10], (N_RG_LAYERS, D_RNN), 0.01),
        'rg_lambda': jnp.log(sig) - jnp.log1p(-sig),
        'rg_w_out': nrm(ks[12], (N_RG_LAYERS, D_RNN, D_MODEL), D_RNN ** -0.5),
        'nsa_w_in': nrm(ks[13], (N_NSA_LAYERS, D_MODEL, NSA_IN_COLS), D_MODEL ** -0.5),
        'nsa_b_gate': nrm(ks[14], (N_NSA_LAYERS, GATE_COLS), 0.01),
        'nsa_pe_k': nrm(ks[15], (N_NSA_LAYERS, CMP_LEN, HEAD_DIM), 0.02),
        'nsa_pe_v': nrm(ks[16], (N_NSA_LAYERS, CMP_LEN, HEAD_DIM), 0.02),
        'nsa_w1_k': nrm(ks[17], (N_NSA_LAYERS, CMP_LEN * HEAD_DIM, HEAD_DIM), (CMP_LEN * HEAD_DIM) ** -0.5),
        'nsa_w2_k': nrm(ks[18], (N_NSA_LAYERS, HEAD_DIM, HEAD_DIM), HEAD_DIM ** -0.5),
        'nsa_w1_v': nrm(ks[19], (N_NSA_LAYERS, CMP_LEN * HEAD_DIM, HEAD_DIM), (CMP_LEN * HEAD_DIM) ** -0.5),
        'nsa_w2_v': nrm(ks[20], (N_NSA_LAYERS, HEAD_DIM, HEAD_DIM), HEAD_DIM ** -0.5),
        'nsa_w_out': nrm(ks[21], (N_NSA_LAYERS, N_HEADS * HEAD_DIM, D_MODEL), (N_HEADS * HEAD_DIM) ** -0.5),
        'mlp_w_up': nrm(ks[22], (DEPTH, D_MODEL, D_FF), D_MODEL ** -0.5),
        'mlp_w_down': nrm(ks[23], (DEPTH, D_FF, D_MODEL), D_FF ** -0.5),
    }


def reference(x, norm_mix, norm_ffn, norm_final,
              rg_w_in, rg_conv_w, rg_conv_b, rg_w_a, rg_b_a, rg_w_x, rg_b_x, rg_lambda, rg_w_out,
              nsa_w_in, nsa_b_gate, nsa_pe_k, nsa_pe_v, nsa_w1_k, nsa_w2_k, nsa_w1_v, nsa_w2_v, nsa_w_out,
              mlp_w_up, mlp_w_down):
    for i in range(DEPTH):
        h = _rmsnorm(x, norm_mix[i])
        j = i // N_MIXERS
        if i % N_MIXERS == 0:
            x = x + _rglru_mixer(h, rg_w_in[j], rg_conv_w[j], rg_conv_b[j], rg_w_a[j], rg_b_a[j],
                                 rg_w_x[j], rg_b_x[j], rg_lambda[j], rg_w_out[j])
        else:
            x = x + _nsa_mixer(h, nsa_w_in[j], nsa_b_gate[j], nsa_pe_k[j], nsa_pe_v[j],
                               nsa_w1_k[j], nsa_w2_k[j], nsa_w1_v[j], nsa_w2_v[j], nsa_w_out[j])
        h = _rmsnorm(x, norm_ffn[i])
        x = x + _sq_relu_mlp(h, mlp_w_up[i], mlp_w_down[i])
    return _rmsnorm(x, norm_final)
```

```python
import numpy as np
from contextlib import ExitStack
import concourse.bass as bass
import concourse.mybir as mybir
from concourse.bass_utils import run_bass_kernel_spmd

F32 = mybir.dt.float32
BF16 = mybir.dt.bfloat16
AF = mybir.ActivationFunctionType
ALU = mybir.AluOpType

ENGS = ('sp', 'pe', 'act', 'dve', 'pool')
SAME_ENGINE_SYNC = True

D = 1024
S = 8192
B = 4
NCORES = 8
TPC = 4096
TB = 256
NBLK = TPC // TB
DFF = 4096
EPS = 1e-6
NSA_COLS = 2608


class Prog:
    def __init__(self):
        self.nc = bass.Bass("TRN2", target_bir_lowering=False)
        self.es = ExitStack()
        self.ops = {e: [] for e in ENGS}
        self.cnt = {e: 0 for e in ENGS}
        self.lastw = {}
        self.readers = {}
        self.waited = {e: {} for e in ENGS}
        self.dmasem = {}
        self.semnames = ['c_' + e for e in ENGS if e != 'sp']
        self.nuniq = 0
        self.floor = {}
        self.pes = None

    def phase_begin(self):
        self.pes = ExitStack()

    def phase_end(self):
        for e in ENGS:
            if e != 'sp' and self.cnt[e] > 0:
                self.floor['c_' + e] = self.cnt[e]
        for name, c in self.dmasem.values():
            if c > 0:
                self.floor[name] = c
        self.pes.close()
        self.pes = None

    def sb(self, shape, dt, name=None):
        self.nuniq += 1
        return (self.pes or self.es).enter_context(self.nc.sbuf_tensor(name or f"sb{self.nuniq}", list(shape), dt))

    def ps(self, shape, dt, name=None):
        self.nuniq += 1
        return (self.pes or self.es).enter_context(self.nc.psum_tensor(name or f"ps{self.nuniq}", list(shape), dt))

    def dram(self, name, shape, dt, kind):
        return self.nc.dram_tensor(name, list(shape), dt, kind=kind).ap()

    def _deps(self, reads, writes):
        deps = {}

        def add(s, v):
            if deps.get(s, 0) < v:
                deps[s] = v
        for k in reads:
            t = self.lastw.get(k)
            if t is not None:
                add(*t)
        for k in writes:
            t = self.lastw.get(k)
            if t is not None:
                add(*t)
            for s, v in self.readers.get(k, {}).items():
                add(s, v)
        return deps

    def _commit(self, tok, reads, writes):
        s, v = tok
        for k in reads:
            r = self.readers.setdefault(k, {})
            if r.get(s, 0) < v:
                r[s] = v
        for k in writes:
            self.lastw[k] = tok
            self.readers[k] = {}

    def _waits(self, eng, deps):
        ws = []
        own = 'c_' + eng
        for s, v in self.floor.items():
            if deps.get(s, 0) < v:
                deps[s] = v
        for s, v in deps.items():
            if s == own and (eng == 'pe' or not SAME_ENGINE_SYNC):
                continue
            if self.waited[eng].get(s, 0) >= v:
                continue
            self.waited[eng][s] = v
            ws.append((s, v))
        return ws

    def op(self, eng, fn, reads=(), writes=()):
        deps = self._deps(reads, writes)
        ws = self._waits(eng, deps)
        self.cnt[eng] += 1
        tok = ('c_' + eng, self.cnt[eng])
        self.ops[eng].append((ws, fn, (tok[0], 1)))
        self._commit(tok, reads, writes)
        return tok

    def dma(self, q, out, in_, reads=(), writes=(), semkey=None):
        semkey = semkey if semkey is not None else writes[0]
        if semkey not in self.dmasem:
            name = f"d{len(self.dmasem)}"
            self.dmasem[semkey] = [name, 0]
            self.semnames.append(name)
        ent = self.dmasem[semkey]
        deps = self._deps(reads, writes)
        ws = self._waits(q, deps)
        ent[1] += 16
        tok = (ent[0], ent[1])
        self.ops[q].append((ws, lambda e: e.dma_start(out=out, in_=in_), (ent[0], 16)))
        self._commit(tok, reads, writes)
        return tok

    def build(self):
        nc = self.nc
        final = {}
        for e in ENGS:
            if e != 'sp' and self.cnt[e] > 0:
                final['c_' + e] = self.cnt[e]
        for name, c in self.dmasem.values():
            final[name] = c
        fws = list(final.items())
        with ExitStack() as es:
            sems = {n: es.enter_context(nc.semaphore(n)) for n in self.semnames}
            ops = self.ops

            def replay(name, e):
                for ws, fn, inc in ops[name]:
                    for s, v in ws:
                        e.wait_ge(sems[s], v)
                    fn(e).then_inc(sems[inc[0]], inc[1])
                if name == 'sp':
                    for s, v in fws:
                        e.wait_ge(sems[s], v)
            with nc.Block() as block:
                @block.sync
                def _(e):
                    replay('sp', e)

                @block.tensor
                def _(e):
                    replay('pe', e)

                @block.scalar
                def _(e):
                    replay('act', e)

                @block.vector
                def _(e):
                    replay('dve', e)

                @block.gpsimd
                def _(e):
                    replay('pool', e)
        self.es.close()
        return nc


class PsumRot:
    def __init__(self, P, n, prefix):
        self.tiles = [P.ps([128, 512], F32) for _ in range(n)]
        self.keys = [(prefix, i) for i in range(n)]
        self.i = 0

    def next(self):
        t, k = self.tiles[self.i], self.keys[self.i]
        self.i = (self.i + 1) % len(self.tiles)
        return t, k


def build_token_prog(has_post, has_mlp, nxt):
    P = Prog()
    A = {}
    x_in = P.dram("x_in", [NBLK, 128, 8, TB], F32, "ExternalInput")
    write_x = has_post or has_mlp
    if write_x and nxt != 'final':
        x_out = P.dram("x_out", [NBLK, 128, 8, TB], F32, "ExternalOutput")
    elif write_x:
        x_out = P.dram("x_scr", [NBLK, 128, 8, TB], F32, "Internal")
    else:
        x_out = x_in
    if has_post:
        A['mix'] = P.dram("mix", [D, TPC], F32, "ExternalInput")
        A['w_out'] = P.dram("w_out", [D, D], F32, "ExternalInput")
    if has_mlp:
        A['g_ffn'] = P.dram("g_ffn", [128, 8], F32, "ExternalInput")
        A['w_up'] = P.dram("w_up", [D, DFF], F32, "ExternalInput")
        A['w_down'] = P.dram("w_down", [DFF, D], F32, "ExternalInput")
    A['g_nxt'] = P.dram("g_nxt", [128, 8], F32, "ExternalInput")
    if nxt == 'nsa':
        A['b_gate'] = P.dram("b_gate", [48, 1], F32, "ExternalInput")
    if nxt != 'final':
        ncols = 2048 if nxt == 'rg' else NSA_COLS
        A['w_in'] = P.dram("w_in", [D, ncols], F32, "ExternalInput")
        A['proj'] = P.dram("proj", [ncols, TPC], F32, "ExternalOutput")
    else:
        A['y_out'] = P.dram("y_out", [NBLK, 128, 8, TB], F32, "ExternalOutput")
    emit_token(P, NBLK, x_in, x_out, has_post, has_mlp, nxt, A)
    return P.build()


def emit_token(P, NBLK, x_in, x_out, has_post, has_mlp, nxt, A):
    write_x = has_post or has_mlp
    mix, w_out = A.get('mix'), A.get('w_out')
    g_ffn, w_up, w_down = A.get('g_ffn'), A.get('w_up'), A.get('w_down')
    g_nxt, b_gate, w_in, proj, y_out = A.get('g_nxt'), A.get('b_gate'), A.get('w_in'), A.get('proj'), A.get('y_out')
    ncols = 2048 if nxt == 'rg' else NSA_COLS
    P.phase_begin()
    WA = P.sb([128, 8 * DFF], BF16)
    WB = P.sb([128, 32 * D], BF16)
    WO = P.sb([128, 8 * D], BF16)
    xbuf = [P.sb([128, 8, TB], F32) for _ in range(2)]
    mbuf = [P.sb([128, 8, TB], BF16) for _ in range(2)]
    hb = P.sb([128, 8, TB], BF16)
    sq = P.sb([128, 8, TB], BF16)
    rs = P.sb([128, TB], F32)
    hb2 = P.sb([128, 8, TB], BF16)
    u2 = P.sb([128, 32, TB], BF16)
    rt = [P.sb([128, TB], F32) for _ in range(2)]
    stg = [P.sb([128, 4, TB], F32) for _ in range(2)]
    st = [stg[i % 2][:, i // 2, :] for i in range(4)]
    ones_bf = P.sb([128, 128], BF16)
    gf = P.sb([128, 8], F32)
    gn = P.sb([128, 8], F32)
    P.eps_tile = P.sb([128, 1], F32)
    bg = P.sb([48, 1], F32)
    psr = PsumRot(P, 6, 'ps')
    psn = P.ps([128, 512], F32)

    P.op('pool', lambda e: e.memset(ones_bf[:], 1.0), writes=['ones'])
    P.op('pool', lambda e: e.memset(P.eps_tile[:], EPS), writes=['epsc'])
    if has_mlp:
        P.dma('sp', gf[:], g_ffn, writes=['gvec_f'])
    P.dma('sp', gn[:], g_nxt, writes=['gvec_n'])
    if nxt == 'nsa':
        P.dma('sp', bg[:], b_gate, writes=['bg'])

    WAv = WA[:].rearrange("p (c f) -> p c f", c=8)
    WBv = WB[:].rearrange("p (c f) -> p c f", c=32)
    WOv = WO[:].rearrange("p (c f) -> p c f", c=8)
    if has_post:
        wsrc = w_out.rearrange("(c p) f -> p c f", p=128)
        for c in range(0, 8, 4):
            P.dma('pool', WOv[:, c:c + 4, :], wsrc[:, c:c + 4, :], writes=[('WO', c)])
        wo_keys = [('WO', 0), ('WO', 4)]
    if has_mlp:
        wsrc = w_up.rearrange("(c p) f -> p c f", p=128)
        for c in range(8):
            P.dma('pool', WAv[:, c, :], wsrc[:, c, :], writes=[('WA', c)])
        wsrc = w_down.rearrange("(c p) f -> p c f", p=128)
        for c in range(0, 32, 4):
            P.dma('pool', WBv[:, c:c + 4, :], wsrc[:, c:c + 4, :], writes=[('WB', c)])

    mixv = mix.rearrange("(c p) t -> p c t", p=128) if has_post else None

    def xkeys(slot):
        return [('xb', slot, c) for c in range(8)]

    if write_x:
        hbs = [hb, hb2]
        sqs = [sq, sq]
        rss = [rs, rs]

        def load(blk):
            sl = blk % 2
            P.dma('sp', xbuf[sl][:], x_in[blk], writes=xkeys(sl), semkey=('xbsem', sl))
            if has_post:
                P.dma('pool', mbuf[sl][:], mixv[:, :, blk * TB:(blk + 1) * TB], writes=[('mb', sl)])

        def stageA(blk):
            slot = blk % 2
            xb = xbuf[slot]
            xk = xkeys(slot)
            if has_post:
                mb = mbuf[slot]
                for cc in range(8):
                    pt, pk = psr.next()
                    for kc in range(8):
                        P.op('pe', lambda e, pt=pt, kc=kc, cc=cc, mb=mb: e.matmul(
                            pt[:, :TB], lhsT=WOv[:, kc, cc * 128:(cc + 1) * 128], rhs=mb[:, kc, :], start=(kc == 0), stop=(kc == 7)),
                            reads=[('mb', slot)] + wo_keys, writes=[pk])
                    P.op('dve', lambda e, pt=pt, cc=cc, xb=xb: e.tensor_tensor(out=xb[:, cc, :], in0=pt[:, :TB], in1=xb[:, cc, :], op=ALU.add),
                         reads=[pk, xk[cc]], writes=[xk[cc]])
            if has_mlp:
                hk = [('hb', slot, c) for c in range(8)]
                emit_norm_k(P, xb, xk, gf, 'gvec_f', hbs[slot], hk, ones_bf, sqs[slot], psn, rss[slot], sfx=0, sqk=0)

        def stageB(blk):
            slot = blk % 2
            hk = [('hb', slot, c) for c in range(8)]
            hcur = hbs[slot]
            for fc in range(32):
                pt, pk = psr.next()
                for kc in range(8):
                    P.op('pe', lambda e, pt=pt, kc=kc, fc=fc, hcur=hcur: e.matmul(
                        pt[:, :TB], lhsT=WAv[:, kc, fc * 128:(fc + 1) * 128], rhs=hcur[:, kc, :], start=(kc == 0), stop=(kc == 7)),
                        reads=[hk[kc], ('WA', kc)], writes=[pk])
                r = rt[fc % 2]
                P.op('act', lambda e, pt=pt, r=r: e.activation(out=r[:], in_=pt[:, :TB], func=AF.Relu),
                     reads=[pk], writes=[('rt', fc % 2)])
                eng = 'dve' if fc % 2 == 0 else 'pool'
                P.op(eng, lambda e, r=r, fc=fc: e.tensor_tensor(out=u2[:, fc, :], in0=r[:], in1=r[:], op=ALU.mult),
                     reads=[('rt', fc % 2)], writes=[('u2', fc)])

        def stageC(blk):
            slot = blk % 2
            xb = xbuf[slot]
            xk = xkeys(slot)
            if has_mlp:
                for cc in range(8):
                    pt, pk = psr.next()
                    for fc in range(32):
                        P.op('pe', lambda e, pt=pt, fc=fc, cc=cc: e.matmul(
                            pt[:, :TB], lhsT=WBv[:, fc, cc * 128:(cc + 1) * 128], rhs=u2[:, fc, :], start=(fc == 0), stop=(fc == 31)),
                            reads=[('u2', fc), ('WB', (fc // 4) * 4)], writes=[pk])
                    P.op('dve', lambda e, pt=pt, cc=cc, xb=xb: e.tensor_tensor(out=xb[:, cc, :], in0=pt[:, :TB], in1=xb[:, cc, :], op=ALU.add),
                         reads=[pk, xk[cc]], writes=[xk[cc]])
            P.dma('sp', x_out[blk], xb[:], reads=xk, writes=[('xout', blk)], semkey=('xosem', slot))

        load(0)
        if NBLK > 1:
            load(1)
        stageA(0)
        for blk in range(NBLK):
            if has_mlp:
                stageB(blk)
            if blk + 1 < NBLK:
                stageA(blk + 1)
            stageC(blk)
            if blk + 2 < NBLK:
                load(blk + 2)

    if nxt != 'final':
        nch = (ncols + 127) // 128
        WIv = WA[:, 0:8 * ncols].rearrange("p (c f) -> p c f", c=8)
        wsrc = w_in.rearrange("(c p) f -> p c f", p=128)
        for c in range(8):
            P.dma('pool', WIv[:, c, :], wsrc[:, c, :], writes=[('WA', c)])
    for blk in range(NBLK):
        slot = blk % 2
        xb = xbuf[slot]
        xk = xkeys(slot)
        P.dma('sp', xb[:], x_out[blk], reads=[('xout', blk)] if write_x else [], writes=xk, semkey=('xbsem', slot))
        hk = [('hb', c) for c in range(8)]
        if nxt == 'final':
            emit_norm_k(P, xb, xk, gn, 'gvec_n', None, [('st', c % 4) for c in range(8)], ones_bf, sq, psn, rs,
                        outs=lambda c: st[c % 4],
                        after=lambda c: P.dma('sp', y_out[blk][:, c, :], st[c % 4], reads=[('st', c % 4)], writes=[('yout', blk, c)],
                                              semkey=('stsem', c % 4)))
            continue
        emit_norm_k(P, xb, xk, gn, 'gvec_n', hb, hk, ones_bf, sq, psn, rs)
        for cc in range(nch):
            m = min(128, ncols - cc * 128)
            pt, pk = psr.next()
            for kc in range(8):
                P.op('pe', lambda e, pt=pt, kc=kc, cc=cc, m=m: e.matmul(
                    pt[0:m, :TB], lhsT=WIv[:, kc, cc * 128:cc * 128 + m], rhs=hb[:, kc, :], start=(kc == 0), stop=(kc == 7)),
                    reads=[hk[kc], ('WA', kc)], writes=[pk])
            gi = (blk * 8 + cc // 4) % 2
            s = stg[gi][:, cc % 4, :]
            sk = ('stg', gi, cc % 4)
            if nxt == 'rg' and cc < 8:
                P.op('act', lambda e, pt=pt, s=s: e.activation(out=s, in_=pt[:, :TB], func=AF.Gelu_apprx_tanh), reads=[pk], writes=[sk])
            elif nxt == 'nsa' and m < 128:
                P.op('act', lambda e, pt=pt, s=s, m=m: e.activation(out=s[0:m, :], in_=pt[0:m, :TB], func=AF.Sigmoid, bias=bg[:]),
                     reads=[pk, 'bg'], writes=[sk])
            else:
                P.op('dve', lambda e, pt=pt, s=s: e.tensor_copy(s, pt[:, :TB]), reads=[pk], writes=[sk])
            if m < 128:
                P.dma('sp', proj[cc * 128:cc * 128 + m, blk * TB:(blk + 1) * TB], s[0:m, :], reads=[sk], writes=[('proj', blk, cc)],
                      semkey=('stsem', gi))
            elif cc % 4 == 3:
                c0 = cc - 3
                P.dma('sp', proj[c0 * 128:(c0 + 4) * 128, blk * TB:(blk + 1) * TB].rearrange("(c p) t -> p c t", p=128), stg[gi][:],
                      reads=[('stg', gi, k) for k in range(4)], writes=[('proj', blk, cc)], semkey=('stsem', gi))
    P.phase_end()


def emit_norm_k(P, xb, xk, g_sb, gkey, hb, hk, ones_bf, sq, psn, rs, sfx=0, sqk=0, outs=None, after=None):
    P.op('act', lambda e: e.activation(out=sq[:], in_=xb[:], func=AF.Square), reads=xk, writes=[('sq', sqk)])
    for c in range(8):
        P.op('pe', lambda e, c=c: e.matmul(psn[:, :TB], lhsT=ones_bf[:], rhs=sq[:, c, :], start=(c == 0), stop=(c == 7)),
             reads=[('sq', sqk), 'ones'], writes=['psn'])
    P.op('act', lambda e: e.activation(out=rs[:], in_=psn[:, :TB], func=AF.Sqrt, bias=P.eps_tile[:], scale=1.0 / D),
         reads=['psn', 'epsc'], writes=[('rs', sfx)])
    P.op('dve', lambda e: e.reciprocal(rs[:], rs[:]), reads=[('rs', sfx)], writes=[('rs', sfx)])
    for c in range(8):
        dst = outs(c) if outs is not None else hb[:, c, :]
        P.op('dve', lambda e, c=c, dst=dst: e.scalar_tensor_tensor(out=dst, in0=xb[:, c, :], scalar=g_sb[:, c:c + 1], in1=rs[:],
                                                                 op0=ALU.mult, op1=ALU.mult),
             reads=[xk[c], ('rs', sfx), gkey], writes=[hk[c]])
        if after is not None:
            after(c)


SEG = 2048
NSEG = S // SEG


def build_scan_prog():
    P = Prog()
    ysrc = P.dram("ysrc", [2, 256, S], F32, "ExternalInput")
    xpsrc = P.dram("xpsrc", [2, 256, S], F32, "ExternalInput")
    cpar = P.dram("cpar", [2, 128, 2, 8], F32, "ExternalInput")
    wa = P.dram("wa", [2, 256, 256], F32, "ExternalInput")
    wx = P.dram("wx", [2, 256, 256], F32, "ExternalInput")
    mixo = P.dram("mixo", [2, 256, S], F32, "ExternalOutput")
    emit_scan(P, 2, lambda u, j, a, b: ysrc[u, j * 128:(j + 1) * 128, a:b], lambda u, j, a, b: xpsrc[u, j * 128:(j + 1) * 128, a:b],
              cpar, wa, wx, lambda u, j, a, b: mixo[u, j * 128:(j + 1) * 128, a:b])
    return P.build()


def emit_scan(P, NU, ysrc, xpsrc, cpar, wa, wx, mixo):
    P.phase_begin()
    cp = P.sb([128, NU, 2, 8], F32)
    cst = P.sb([128, NU, 2, 4], F32)
    wab = P.sb([128, NU, 2, 256], BF16)
    wxb = P.sb([128, NU, 2, 256], BF16)
    xp = [P.sb([128, 3 + SEG], F32) for _ in range(2)]
    xb = [P.sb([128, SEG], F32) for _ in range(2)]
    xbb = [P.sb([128, SEG], BF16) for _ in range(2)]
    rt = P.sb([128, SEG], F32)
    it = P.sb([128, SEG], F32)
    at = P.sb([128, SEG], F32)
    mt = P.sb([128, SEG], F32)
    ut = P.sb([128, SEG], F32)
    ht = P.sb([128, SEG], F32)
    yt = P.sb([128, SEG], F32)
    ot = P.sb([128, SEG], F32)
    hcar = P.sb([128, 2], F32)
    psr = PsumRot(P, 6, 'ps')

    for u in range(NU):
        P.dma('sp', cp[:, u], cpar[u], writes=[('cp', u)])
        P.dma('pool', wab[:, u], wa[u].rearrange("(j p) o -> p j o", p=128), writes=[('wab', u)])
        P.dma('pool', wxb[:, u], wx[u].rearrange("(j p) o -> p j o", p=128), writes=[('wxb', u)])
        for j in range(2):
            lam = cp[:, u, j, 7:8]
            P.op('act', lambda e, u=u, j=j, lam=lam: e.activation(out=cst[:, u, j, 0:1], in_=lam, func=AF.Exp, scale=-1.0),
                 reads=[('cp', u)], writes=[('cst', u, j)])
            P.op('act', lambda e, u=u, j=j: e.activation(out=cst[:, u, j, 1:2], in_=cst[:, u, j, 0:1], func=AF.Ln, bias=1.0),
                 reads=[('cst', u, j)], writes=[('cst', u, j)])
            P.op('dve', lambda e, u=u, j=j: e.tensor_scalar(out=cst[:, u, j, 2:3], in0=cst[:, u, j, 1:2], scalar1=-8.0, scalar2=None, op0=ALU.mult),
                 reads=[('cst', u, j)], writes=[('cst', u, j)])
            P.op('dve', lambda e, u=u, j=j: e.tensor_scalar(out=cst[:, u, j, 3:4], in0=cst[:, u, j, 1:2], scalar1=-16.0, scalar2=None, op0=ALU.mult),
                 reads=[('cst', u, j)], writes=[('cst', u, j)])

    for u in range(NU):
        for s in range(NSEG):
            t0 = s * SEG
            for j in range(2):
                rows = slice(j * 128, (j + 1) * 128)
                if s == 0:
                    P.op('pool', lambda e, j=j: e.memset(xp[j][:, 0:3], 0.0), writes=[('xp', j)])
                    P.dma('sp', xp[j][:, 3:3 + SEG], xpsrc(u, j, 0, SEG), writes=[('xp', j)], semkey=('xpsem', j))
                else:
                    P.dma('sp', xp[j][:, :], xpsrc(u, j, t0 - 3, t0 + SEG), writes=[('xp', j)], semkey=('xpsem', j))
                cw = lambda k, u=u, j=j: cp[:, u, j, k:k + 1]
                P.op('dve', lambda e, j=j, cw=cw: e.tensor_scalar(out=xb[j][:], in0=xp[j][:, 3:3 + SEG], scalar1=cw(3), scalar2=cw(4),
                                                                 op0=ALU.mult, op1=ALU.add),
                     reads=[('xp', j), ('cp', u)], writes=[('xb', j)])
                for k in (2, 1, 0):
                    P.op('dve', lambda e, j=j, k=k, cw=cw: e.scalar_tensor_tensor(out=xb[j][:], in0=xp[j][:, k:k + SEG], scalar=cw(k), in1=xb[j][:],
                                                                                 op0=ALU.mult, op1=ALU.add),
                         reads=[('xp', j), ('cp', u), ('xb', j)], writes=[('xb', j)])
                P.op('act', lambda e, j=j: e.copy(out=xbb[j][:], in_=xb[j][:]), reads=[('xb', j)], writes=[('xbb', j)])
            for j in range(2):
                rows = slice(j * 128, (j + 1) * 128)
                P.dma('sp', yt[:], ysrc(u, j, t0, t0 + SEG), writes=['yt'])
                for (wt, wkey, dst, dkey, bk) in ((wab, 'wab', rt, 'rt', 5), (wxb, 'wxb', it, 'it', 6)):
                    for tb in range(SEG // 512):
                        pt, pk = psr.next()
                        for jin in range(2):
                            P.op('pe', lambda e, pt=pt, wt=wt, jin=jin, j=j, tb=tb, u=u: e.matmul(
                                pt[:], lhsT=wt[:, u, jin, j * 128:(j + 1) * 128], rhs=xbb[jin][:, tb * 512:(tb + 1) * 512],
                                start=(jin == 0), stop=(jin == 1)),
                                reads=[('xbb', jin), (wkey, u)], writes=[pk])
                        P.op('act', lambda e, pt=pt, dst=dst, tb=tb, u=u, j=j, bk=bk: e.activation(
                            out=dst[:, tb * 512:(tb + 1) * 512], in_=pt[:], func=AF.Sigmoid, bias=cp[:, u, j, bk:bk + 1]),
                            reads=[pk, ('cp', u)], writes=[dkey])
                P.op('act', lambda e, u=u, j=j: e.activation(out=at[:], in_=rt[:], func=AF.Exp, scale=cst[:, u, j, 2:3]),
                     reads=['rt', ('cst', u, j)], writes=['at'])
                P.op('act', lambda e, u=u, j=j: e.activation(out=mt[:], in_=rt[:], func=AF.Exp, scale=cst[:, u, j, 3:4]),
                     reads=['rt', ('cst', u, j)], writes=['mt'])
                P.op('dve', lambda e: e.tensor_scalar(out=mt[:], in0=mt[:], scalar1=-1.0, scalar2=1.0, op0=ALU.mult, op1=ALU.add),
                     reads=['mt'], writes=['mt'])
                P.op('dve', lambda e: e.tensor_scalar_max(out=mt[:], in0=mt[:], scalar1=1e-20), reads=['mt'], writes=['mt'])
                P.op('act', lambda e: e.activation(out=mt[:], in_=mt[:], func=AF.Sqrt), reads=['mt'], writes=['mt'])
                P.op('pool', lambda e, j=j: e.tensor_tensor(out=ut[:], in0=it[:], in1=xb[j][:], op=ALU.mult),
                     reads=['it', ('xb', j)], writes=['ut'])
                P.op('dve', lambda e: e.tensor_tensor(out=ut[:], in0=ut[:], in1=mt[:], op=ALU.mult), reads=['ut', 'mt'], writes=['ut'])
                init = 0.0 if s == 0 else hcar[:, j:j + 1]
                P.op('dve', lambda e, init=init: e.tensor_tensor_scan(ht[:], at[:], ut[:], init, ALU.mult, ALU.add),
                     reads=['at', 'ut', ('hcar', j)], writes=['ht'])
                P.op('dve', lambda e, j=j: e.tensor_copy(hcar[:, j:j + 1], ht[:, SEG - 1:SEG]), reads=['ht'], writes=[('hcar', j)])
                P.op('pool', lambda e: e.tensor_tensor(out=ot[:], in0=ht[:], in1=yt[:], op=ALU.mult), reads=['ht', 'yt'], writes=['ot'])
                P.dma('sp', mixo(u, j, t0, t0 + SEG), ot[:], reads=['ot'], writes=[('mixo', u, j, s)], semkey=('osem',))
    P.phase_end()


_PROGS = {}


def _prog(key, fn):
    if key not in _PROGS:
        _PROGS[key] = fn()
    return _PROGS[key]


def _run(nc, in_maps):
    res = run_bass_kernel_spmd(nc, in_maps, core_ids=list(range(NCORES)))
    return res.results


def to_xl(xc):
    return np.ascontiguousarray(xc.reshape(NBLK, TB, 8, 128).transpose(0, 3, 2, 1))


def from_xl(xl):
    return np.ascontiguousarray(xl.transpose(0, 3, 2, 1).reshape(TPC, D))


def gvec(g):
    return np.ascontiguousarray(g.reshape(8, 128).T)


def run_token(xls, has_post, has_mlp, nxt, mixs=None, w_out=None, g_ffn=None, w_up=None, w_down=None, g_nxt=None, w_in=None, b_gate=None):
    nc = _prog(('tok', has_post, has_mlp, nxt), lambda: build_token_prog(has_post, has_mlp, nxt))
    maps = []
    for c in range(NCORES):
        m = {"x_in": xls[c], "g_nxt": gvec(g_nxt)}
        if has_post:
            m["mix"] = mixs[c]
            m["w_out"] = w_out
        if has_mlp:
            m["g_ffn"] = gvec(g_ffn)
            m["w_up"] = w_up
            m["w_down"] = w_down
        if nxt != 'final':
            m["w_in"] = w_in
        if nxt == 'nsa':
            m["b_gate"] = np.ascontiguousarray(b_gate.reshape(48, 1))
        maps.append(m)
    return _run(nc, maps)


def run_scan(projs, conv_w, conv_b, w_a, b_a, w_x, b_x, lam):
    nc = _prog(('scan',), build_scan_prog)
    maps = []
    for c in range(NCORES):
        b, hh = c // 2, c % 2
        ys, xs, cps, was, wxs = [], [], [], [], []
        for u in range(2):
            n = 2 * hh + u
            ch = slice(n * 256, (n + 1) * 256)
            ys.append(np.concatenate([projs[2 * b][ch], projs[2 * b + 1][ch]], axis=1))
            ch2 = slice(1024 + n * 256, 1024 + (n + 1) * 256)
            xs.append(np.concatenate([projs[2 * b][ch2], projs[2 * b + 1][ch2]], axis=1))
            par = np.stack([conv_w[0, ch], conv_w[1, ch], conv_w[2, ch], conv_w[3, ch], conv_b[ch], b_a[ch], b_x[ch], lam[ch]], axis=-1)
            cps.append(par.reshape(2, 128, 8).transpose(1, 0, 2))
            was.append(w_a[n])
            wxs.append(w_x[n])
        maps.append({"ysrc": np.ascontiguousarray(np.stack(ys)), "xpsrc": np.ascontiguousarray(np.stack(xs)),
                     "cpar": np.ascontiguousarray(np.stack(cps)), "wa": np.ascontiguousarray(np.stack(was)),
                     "wx": np.ascontiguousarray(np.stack(wxs))})
    res = _run(nc, maps)
    mixs = []
    for c in range(NCORES):
        b, hh = c // 2, c % 2
        parts = []
        for n in range(4):
            src = res[2 * b + n // 2]["mixo"][n % 2]
            parts.append(src[:, hh * TPC:(hh + 1) * TPC])
        mixs.append(np.ascontiguousarray(np.concatenate(parts, axis=0)))
    return mixs


QC = 512
NQC = S // QC
NCMP = 511
SCALE = 0.125


def att_tables(g):
    slopes = np.exp2(-8.0 * (np.arange(1, 17, dtype=np.float64)) / 16.0).reshape(4, 4)[g]
    tq = np.arange(QC, dtype=np.float64)
    aug = (-slopes[:, None] * tq[None, :] / SCALE).astype(np.float32)
    kk = np.arange(128, dtype=np.float64)
    dl = (np.arange(67, dtype=np.float64) - 63.0) * 128.0
    bias_sw = (slopes[None, :, None] * (dl[None, None, :] + kk[:, None, None])).astype(np.float32)
    kt = np.arange(4, dtype=np.float64)
    qc = np.arange(NQC, dtype=np.float64)
    pos = 16.0 * (128.0 * kt[None, :, None] + kk[:, None, None]) + 31.0 - QC * qc[None, None, :]
    bias_c = (slopes[None, :, None, None] * pos[:, None, :, :]).astype(np.float32)
    return aug, bias_sw, bias_c


def att_consts():
    n = np.arange(512)
    j = np.arange(128)
    ov = np.minimum(n[:, None] * 16 + 32, j[None, :] * 64 + 64) - np.maximum(n[:, None] * 16, j[None, :] * 64)
    cm = (np.maximum(ov, 0) / 16.0).astype(np.float32)
    cm[511] = 0.0
    cmpmap = np.ascontiguousarray(cm.reshape(4, 128, 128).transpose(1, 0, 2))
    jj = np.arange(128)[:, None, None]
    ktt = np.arange(64)[None, :, None]
    kk = np.arange(128)[None, None, :]
    eexp = (jj == 2 * ktt + kk // 64).astype(np.float32)
    kq = np.arange(128)[:, None, None]
    tq = np.arange(QC)[None, None, :]
    dw = (np.arange(8)[None, :, None] - 4) * 128
    d = tq - kq - dw
    winmask = ((d >= 0) & (d < 512)).astype(np.float32)
    dc = np.arange(4)[None, :, None] * 128
    causal = ((tq - kq - dc) >= 0).astype(np.float32)
    tt = np.arange(128)[:, None]
    w = np.arange(256)[None, :]
    jr = w - 126
    c = (tt >= 64).astype(np.int64)
    valid = jr <= c
    forced = (jr == c) | (jr == c - 1)
    tb = (300.0 - w) * 1e-35
    validW = valid.astype(np.float32)
    addW = np.where(valid, 1e4 * forced + tb, -1.0).astype(np.float32)
    ident = np.eye(128, dtype=np.float32)
    return dict(cmpmap=cmpmap, eexp=eexp, winmask=winmask, causal=causal, validW=validW, addW=addW, ident=ident)


def build_att_prog(dbg=False, nunits=2, nqc=NQC):
    P = Prog()
    A = {}
    qT = P.dram("qT", [2, 4, 64, S], F32, "ExternalInput")
    kT = P.dram("kT", [2, 3, 64, S], F32, "ExternalInput")
    vcT = P.dram("vcT", [2, 64, S], F32, "ExternalInput")
    vtok = P.dram("vtok", [2, 2, S, 64], F32, "ExternalInput")
    gates = P.dram("gates", [2, S, 12], F32, "ExternalInput")
    for nm, shp in (("w1k", [2048, 64]), ("w2k", [64, 64]), ("w1v", [2048, 64]), ("w2v", [64, 64]), ("pek", [128, 16]), ("pev", [128, 16]),
                    ("augrow", [2, 1, 4, QC]), ("bias_sw", [2, 128, 4, 67]), ("bias_c", [2, 128, 4, 4, NQC]), ("cmpmap", [128, 4, 128]),
                    ("eexp", [128, 64, 128]), ("winmask", [128, 8, QC]), ("causal", [128, 4, QC]), ("validW", [128, 256]), ("addW", [128, 256]),
                    ("ident", [128, 128])):
        A[nm] = P.dram(nm, shp, F32, "ExternalInput")
    o_d = P.dram("o", [2, 3, S, 256] if dbg else [2, S, 256], F32, "ExternalOutput")
    A['q_src'] = lambda u, q0: qT[u, :, :, q0:q0 + QC].rearrange("h d t -> d h t")
    A['k_src'] = lambda u, i: kT[u, i]
    A['vc_src'] = lambda u: vcT[u]
    A['vtok_src'] = lambda u, i: vtok[u, i]
    A['gates_src'] = lambda u, q0: gates[u, q0:q0 + QC, :]
    A['o_dst'] = (lambda u, br, q0: o_d[u, br, q0:q0 + QC, :]) if dbg else (lambda u, br, q0: o_d[u, q0:q0 + QC, :])
    emit_att(P, A, dbg=dbg, nunits=nunits, nqc=nqc, fused=False)
    return P.build()


def emit_att(P, A, dbg=False, nunits=2, nqc=NQC, fused=False, groups=None):
    P.phase_begin()
    ALIBI_CUT = 100.0

    def dcut(u, h):
        if groups is None:
            return 1e30
        return ALIBI_CUT / (2.0 ** (-(4 * groups[u] + h + 1) / 2.0))

    NU = nunits
    w1k, w2k, w1v, w2v, pek, pev = A['w1k'], A['w2k'], A['w1v'], A['w2v'], A['pek'], A['pev']
    augrow, bias_sw_d, bias_c_d = A['augrow'], A['bias_sw'], A['bias_c']
    cmpmap_d, eexp_d, winmask_d, causal_d = A['cmpmap'], A['eexp'], A['winmask'], A['causal']
    validW_d, addW_d, ident_d = A['validW'], A['addW'], A['ident']

    zk = P.sb([64, S], BF16)
    zv = P.sb([64, S], BF16)
    ksA = P.sb([65, S], BF16)
    kwA = P.sb([65, S], BF16)
    vsA = P.sb([128, 64, 65], BF16)
    vwA = P.sb([128, 64, 65], BF16)
    kcA = P.sb([65, 512], BF16)
    vcA = P.sb([128, 4, 65], BF16)
    eexp = P.sb([128, 64, 128], BF16)
    cmpb = P.sb([128, 4, 128], BF16)
    winm = P.sb([128, 8, QC], BF16)
    caus = P.sb([128, 4, QC], BF16)
    validW = P.sb([128, 256], F32)
    addW = P.sb([128, 256], F32)
    identf = P.sb([128, 128], F32)
    bsw = P.sb([128, NU, 4, 67], F32)
    bc = P.sb([128, NU, 4, 4, NQC], F32)
    w1d = [P.sb([64, 32, 64], BF16) for _ in range(2)]
    w1p = [P.sb([128, 16, 64], BF16) for _ in range(2)]
    w2b = [P.sb([64, 64], BF16) for _ in range(2)]
    pef = [P.sb([128, 16], BF16) for _ in range(2)]
    cb = P.sb([64, 2], F32)
    h1 = P.sb([64, 512], BF16)
    qbuf = [P.sb([65, 4, QC], BF16) for _ in range(2)]
    gt = [P.sb([128, 4, 12], F32) for _ in range(2)]
    acc = [[P.sb([128, 4, 256], F32) for _ in range(3 if dbg else 1)] for _ in range(2)]
    PTc = [P.sb([128, QC], BF16) for _ in range(4)]
    NPT = 12
    PT = [P.sb([128, QC], BF16) for _ in range(NPT)]
    PTm = [P.sb([128, QC], BF16) for _ in range(NPT)]
    msb = [P.sb([128, QC], BF16) for _ in range(3)]
    osb = [P.sb([65, QC], F32) for _ in range(2)]
    rl = [P.sb([128, 4], F32) for _ in range(2)]
    ff = [P.sb([128, 4], F32) for _ in range(2)]
    impacc = P.sb([128, 4, 128], F32)
    sc = P.sb([128, 128], F32)
    sc2 = P.sb([128, 128], F32)
    t8 = P.sb([128, 16], F32)
    selT = P.sb([128, QC], BF16)
    pso = [P.ps([128, 512], F32) for _ in range(4)]
    pss = PsumRot(P, 2, 'pss')
    psM = P.ps([128, 512], F32)
    pss3 = PsumRot(P, 0, 'pss3')
    pss3.tiles = pss.tiles + [psM]
    pss3.keys = pss.keys + ['psM']
    SK = 7
    psx = P.ps([128, 512], F32)
    if fused:
        vstage = P.sb([64, 2048], F32)
        gtf = [P.sb([12, QC], F32) for _ in range(2)]
        oT = [P.sb([128, QC], F32) for _ in range(2)]

    P.dma('pool', eexp[:], eexp_d, writes=['eexp'])
    P.dma('pool', cmpb[:], cmpmap_d, writes=['cmpb'])
    P.dma('pool', winm[:], winmask_d, writes=['winm'])
    P.dma('pool', caus[:], causal_d, writes=['caus'])
    P.dma('sp', validW[:], validW_d, writes=['validW'])
    P.dma('sp', addW[:], addW_d, writes=['addW'])
    P.dma('sp', identf[:], ident_d, writes=['identf'])
    for u in range(NU):
        P.dma('sp', bsw[:, u], bias_sw_d[u], writes=[('bsw', u)])
        P.dma('sp', bc[:, u], bias_c_d[u], writes=[('bc', u)])
    for i, (w1, w2, pe) in enumerate(((w1k, w2k, pek), (w1v, w2v, pev))):
        P.dma('pool', w1d[i][:], w1.rearrange("(l d) o -> d l o", d=64), writes=[('w1d', i)])
        P.dma('pool', w1p[i][:], w1.rearrange("(j p) o -> p j o", p=128), writes=[('w1p', i)])
        P.dma('pool', w2b[i][:], w2, writes=[('w2b', i)])
        P.dma('pool', pef[i][:], pe, writes=[('pef', i)])
    P.op('pool', lambda e: e.memset(ksA[64:65, :], 1.0), writes=['ksA1'])
    P.op('pool', lambda e: e.memset(kwA[64:65, :], 1.0), writes=['kwA1'])
    P.op('pool', lambda e: e.memset(kcA[64:65, :], 1.0), writes=['kcA1'])
    P.op('pool', lambda e: e.memset(vsA[:, :, 64:65], 1.0), writes=['vsA1'])
    P.op('pool', lambda e: e.memset(vwA[:, :, 64:65], 1.0), writes=['vwA1'])
    P.op('pool', lambda e: e.memset(vcA[:, :, 64:65], 1.0), writes=['vcA1'])
    P.op('pool', lambda e: e.memset(h1[:], 0.0), writes=['h1'])
    for i in range(2):
        for j in range(16):
            P.op('pe', lambda e, i=i, j=j: e.matmul(psx[0:64, i:i + 1], lhsT=w1p[i][:, j, :], rhs=pef[i][:, j:j + 1], start=(j == 0), stop=(j == 15)),
                 reads=[('w1p', i), ('pef', i)], writes=['psx'])
        P.op('act', lambda e, i=i: e.copy(out=cb[:, i:i + 1], in_=psx[0:64, i:i + 1]), reads=['psx'], writes=[('cb', i)])

    cnt = {'pt': 0, 'ptm': 0, 'msb': 0, 'osb': 0}

    def do_unit(u):
        P.dma('pool', zk[:], A['k_src'](u, 0), writes=['zk'])
        P.dma('pool', zv[:], A['vc_src'](u), writes=['zv'])
        P.dma('pool', ksA[0:64, :], A['k_src'](u, 1), writes=['ksA'])
        P.dma('pool', kwA[0:64, :], A['k_src'](u, 2), writes=['kwA'])
        if not fused:
            P.dma('pool', vsA[:, :, 0:64], A['vtok_src'](u, 0).rearrange("(kt p) d -> p kt d", p=128), writes=['vsA'])
            P.dma('pool', vwA[:, :, 0:64], A['vtok_src'](u, 1).rearrange("(kt p) d -> p kt d", p=128), writes=['vwA'])
        else:
            for i, (vA, vkey) in enumerate(((vsA, 'vsA'), (vwA, 'vwA'))):
                for pc in range(4):
                    P.dma('sp', vstage[:], A['vT_src'](u, i)[:, pc * 2048:(pc + 1) * 2048], writes=['vstage'])
                    for half in range(2):
                        for t8i in range(8):
                            c0 = half * 1024 + t8i * 128
                            P.op('pe', lambda e, t8i=t8i, c0=c0: e.transpose(psx[:, t8i * 64:(t8i + 1) * 64], vstage[0:64, c0:c0 + 128], identf[0:64, 0:64]),
                                 reads=['vstage', 'identf'], writes=['psx'])
                        kt0 = pc * 16 + half * 8
                        P.op('act', lambda e, vA=vA, kt0=kt0: e.copy(out=vA[:, kt0:kt0 + 8, 0:64], in_=psx[:].rearrange("p (a d) -> p a d", d=64)),
                             reads=['psx'], writes=[vkey])
        for slot in range(2):
            P.dma('pool', qbuf[slot][64:65, :, :], augrow[u], writes=[('qaug', slot)])
        for i, z in enumerate((zk, zv)):
            zkey = 'zk' if i == 0 else 'zv'
            pt, pk = pss.next()
            for l in range(32):
                P.op('pe', lambda e, pt=pt, i=i, l=l, z=z: e.matmul(pt[0:64, 0:NCMP], lhsT=w1d[i][:, l, :], rhs=z[:, l:l + 16 * (NCMP - 1) + 1:16],
                                                                  start=(l == 0), stop=(l == 31)),
                     reads=[zkey, ('w1d', i)], writes=[pk])
            P.op('act', lambda e, pt=pt, i=i: e.activation(out=h1[:, 0:NCMP], in_=pt[0:64, 0:NCMP], func=AF.Gelu_apprx_tanh, bias=cb[:, i:i + 1]),
                 reads=[pk, ('cb', i)], writes=['h1'])
            if i == 0:
                pt2, pk2 = pss.next()
                P.op('pe', lambda e, pt2=pt2: e.matmul(pt2[0:64, 0:NCMP], lhsT=w2b[0][:], rhs=h1[:, 0:NCMP], start=True, stop=True),
                     reads=['h1', ('w2b', 0)], writes=[pk2])
                P.op('act', lambda e, pt2=pt2: e.copy(out=kcA[0:64, 0:NCMP], in_=pt2[0:64, 0:NCMP]), reads=[pk2], writes=['kcA'])
            else:
                for kt in range(4):
                    nk = 127 if kt == 3 else 128
                    P.op('pe', lambda e, kt=kt, nk=nk: e.matmul(psx[0:nk, 0:64], lhsT=h1[:, kt * 128:kt * 128 + nk], rhs=w2b[1][:], start=True, stop=True),
                         reads=['h1', ('w2b', 1)], writes=['psx'])
                    P.op('act', lambda e, kt=kt, nk=nk: e.copy(out=vcA[0:nk, kt, 0:64], in_=psx[0:nk, 0:64]), reads=['psx'], writes=['vcA'])

    def do_chunk(u, qc):
        if True:
            q0 = qc * QC
            slot = qc % 2
            qa = qbuf[slot]
            P.dma('pool', qa[0:64, :, :], A['q_src'](u, q0), writes=[('qa', slot)])
            if not fused:
                P.dma('sp', gt[slot][:], A['gates_src'](u, q0).rearrange("(ts p) k -> p ts k", p=128), writes=[('gt', slot)])
            else:
                P.dma('sp', gtf[slot][:], A['gT_src'](u, q0), writes=[('gtf', slot)])
                for ts in range(4):
                    P.op('pe', lambda e, ts=ts: e.transpose(psx[:, ts * 12:(ts + 1) * 12], gtf[slot][0:12, ts * 128:(ts + 1) * 128], identf[0:12, 0:12]),
                         reads=[('gtf', slot), 'identf'], writes=['psx'])
                P.op('act', lambda e: e.copy(out=gt[slot][:], in_=psx[:, 0:48].rearrange("p (a k) -> p a k", k=12)), reads=['psx'], writes=[('gt', slot)])
            qkeys = [('qa', slot), ('qaug', slot)]

            def epilogue(h, br, with_imp=False):
                r = cnt['osb'] % 2
                cnt['osb'] += 1
                ob = osb[r]
                P.op('act', lambda e, ob=ob, h=h: e.copy(out=ob[:], in_=pso[h][0:65, :]), reads=[('pso', h)], writes=[('osb', r)])
                for ts in range(4):
                    P.op('pe', lambda e, ob=ob, ts=ts: e.transpose(psx[:, ts * 65:(ts + 1) * 65], ob[0:65, ts * 128:(ts + 1) * 128], identf[0:65, 0:65]),
                         reads=[('osb', r), 'identf'], writes=['psx'])
                rr, fr = rl[r], ff[r]
                P.op('dve', lambda e, rr=rr: e.tensor_scalar_max(out=rr[:], in0=psx[:, 64:64 + 65 * 4:65], scalar1=1e-30), reads=['psx'], writes=[('rl', r)])
                P.op('dve', lambda e, rr=rr: e.reciprocal(rr[:], rr[:]), reads=[('rl', r)], writes=[('rl', r)])
                gi = h * 3 + br
                P.op('dve', lambda e, rr=rr, fr=fr, gi=gi: e.tensor_tensor(out=fr[:], in0=rr[:], in1=gt[slot][:, :, gi], op=ALU.mult),
                     reads=[('rl', r), ('gt', slot)], writes=[('ff', r)])
                for ts in range(4):
                    dst = acc[slot][br if dbg else 0][:, ts, h * 64:(h + 1) * 64]
                    src = psx[:, ts * 65:ts * 65 + 64]
                    if br == 0 or dbg:
                        P.op('dve', lambda e, dst=dst, src=src, fr=fr, ts=ts: e.tensor_scalar(out=dst, in0=src, scalar1=fr[:, ts:ts + 1], scalar2=None, op0=ALU.mult),
                             reads=['psx', ('ff', r)], writes=[('acc', slot, br if dbg else 0, h)])
                    else:
                        P.op('dve', lambda e, dst=dst, src=src, fr=fr, ts=ts: e.scalar_tensor_tensor(out=dst, in0=src, scalar=fr[:, ts:ts + 1], in1=dst,
                                                                                                  op0=ALU.mult, op1=ALU.add),
                             reads=['psx', ('ff', r), ('acc', slot, 0, h)], writes=[('acc', slot, 0, h)])
                if with_imp:
                    for ts in range(4):
                        dst = impacc[:, ts, :]
                        src = psM[:, ts * 128:(ts + 1) * 128]
                        if h == 0:
                            P.op('dve', lambda e, dst=dst, src=src, rr=rr, ts=ts: e.tensor_scalar(out=dst, in0=src, scalar1=rr[:, ts:ts + 1], scalar2=None, op0=ALU.mult),
                                 reads=['psM', ('rl', r)], writes=['impacc'])
                        else:
                            P.op('dve', lambda e, dst=dst, src=src, rr=rr, ts=ts: e.scalar_tensor_tensor(out=dst, in0=src, scalar=rr[:, ts:ts + 1], in1=dst,
                                                                                                      op0=ALU.mult, op1=ALU.add),
                                 reads=['psM', ('rl', r), 'impacc'], writes=['impacc'])

            ktc = min(3, (32 * qc + 30) // 128)
            for h in range(4):
                k0h = 0
                while k0h < ktc and q0 - (16 * (128 * k0h + 127) + 31) > dcut(u, h):
                    k0h += 1
                for kt in range(k0h, ktc + 1):
                    nk = 127 if kt == 3 else 128
                    pt, pk = pss.next()
                    P.op('pe', lambda e, pt=pt, kt=kt, nk=nk, h=h: e.matmul(pt[0:nk, :], lhsT=kcA[0:65, kt * 128:kt * 128 + nk], rhs=qa[0:65, h, :], start=True, stop=True),
                         reads=['kcA', 'kcA1'] + qkeys, writes=[pk])
                    P.op('act', lambda e, pt=pt, kt=kt, nk=nk, h=h: e.activation(out=PTc[kt][0:nk, :], in_=pt[0:nk, :], func=AF.Exp,
                                                                              bias=bc[0:nk, u, h, kt, qc:qc + 1], scale=SCALE),
                         reads=[pk, ('bc', u)], writes=[('PTc', kt)])
                    if q0 - 16 * (128 * kt + nk - 1) - 31 < 0:
                        P.op('pool', lambda e, kt=kt, nk=nk: e.affine_select(out=PTc[kt][0:nk, :], in_=PTc[kt][0:nk, :], pattern=[[1, QC]],
                                                                           compare_op=ALU.is_ge, fill=0.0, base=q0 - 2048 * kt - 31, channel_multiplier=-16),
                             reads=[('PTc', kt)], writes=[('PTc', kt)])
                for kt in range(k0h, ktc + 1):
                    nk = 127 if kt == 3 else 128
                    P.op('pe', lambda e, kt=kt, nk=nk, h=h, k0h=k0h: e.matmul(pso[h][0:65, :], lhsT=vcA[0:nk, kt, 0:65], rhs=PTc[kt][0:nk, :],
                                                                  start=(kt == k0h), stop=(kt == ktc)),
                         reads=[('PTc', kt), 'vcA', 'vcA1'], writes=[('pso', h)])
                for ts in range(4):
                    for kt in range(k0h, ktc + 1):
                        nk = 127 if kt == 3 else 128
                        P.op('pe', lambda e, kt=kt, nk=nk, ts=ts, k0h=k0h: e.matmul(psM[:, ts * 128:(ts + 1) * 128], lhsT=PTc[kt][0:nk, ts * 128:(ts + 1) * 128],
                                                                        rhs=cmpb[0:nk, kt, :], start=(kt == k0h), stop=(kt == ktc)),
                             reads=[('PTc', kt), 'cmpb'], writes=['psM'])
                epilogue(h, 0, with_imp=True)

            for ts in range(4):
                w0 = 126 - 2 * (4 * qc + ts)
                P.op('dve', lambda e, ts=ts, w0=w0: e.tensor_tensor(out=sc[:], in0=impacc[:, ts, :], in1=validW[:, w0:w0 + 128], op=ALU.mult),
                     reads=['impacc', 'validW'], writes=['sc'])
                P.op('dve', lambda e, w0=w0: e.tensor_tensor(out=sc[:], in0=sc[:], in1=addW[:, w0:w0 + 128], op=ALU.add),
                     reads=['sc', 'addW'], writes=['sc'])
                P.op('dve', lambda e: e.tensor_scalar_add(out=sc[:, 0:1], in0=sc[:, 0:1], scalar1=1e4), reads=['sc'], writes=['sc'])
                P.op('dve', lambda e: e.max(t8[:, 0:8], sc[:]), reads=['sc'], writes=['t8'])
                P.op('dve', lambda e: e.match_replace(sc2[:], t8[:, 0:8], sc[:], -1e30), reads=['sc', 't8'], writes=['sc2'])
                P.op('dve', lambda e: e.max(t8[:, 8:16], sc2[:]), reads=['sc2'], writes=['t8'])
                P.op('dve', lambda e: e.tensor_scalar(out=sc2[:], in0=sc[:], scalar1=t8[:, 15:16], scalar2=None, op0=ALU.is_ge), reads=['sc', 't8'], writes=['sc2'])
                P.op('dve', lambda e, w0=w0: e.tensor_tensor(out=sc2[:], in0=sc2[:], in1=validW[:, w0:w0 + 128], op=ALU.mult), reads=['sc2', 'validW'], writes=['sc2'])
                P.op('pe', lambda e, ts=ts: e.transpose(psx[:, ts * 128:(ts + 1) * 128], sc2[:], identf[:]), reads=['sc2', 'identf'], writes=['psx'])
            P.op('act', lambda e: e.copy(out=selT[:], in_=psx[:]), reads=['psx'], writes=['selT'])

            kts = 4 * qc + 3

            def expand(kt):
                mi = cnt['msb'] % 3
                cnt['msb'] += 1
                mt_ = msb[mi]
                P.op('pe', lambda e, kt=kt: e.matmul(psx[:], lhsT=eexp[:, kt, :], rhs=selT[:], start=True, stop=True), reads=['eexp', 'selT'], writes=['psx'])
                P.op('act', lambda e, mt_=mt_: e.copy(out=mt_[:], in_=psx[:]), reads=['psx'], writes=[('msb', mi)])
                return mi
            pend = []

            def flush(n):
                while len(pend) > n:
                    (kt_, h_, pi_, vA_, vkeys_, first_, last_) = pend.pop(0)
                    P.op('pe', lambda e, pi_=pi_, kt_=kt_, h_=h_, vA_=vA_, first_=first_, last_=last_: e.matmul(
                        pso[h_][0:65, :], lhsT=vA_[:, kt_, 0:65], rhs=PTm[pi_][:], start=first_, stop=last_),
                        reads=[('PTm', pi_)] + vkeys_, writes=[('pso', h_)])
            kmin = [0] * 4
            for h in range(4):
                while kmin[h] < kts and q0 - (128 * kmin[h] + 127) > dcut(u, h):
                    kmin[h] += 1
            ktlo = min(kmin)
            mi_next = expand(ktlo)
            for kt in range(ktlo, kts + 1):
                dl = 128 * kt - q0
                mi = mi_next
                mt_ = msb[mi]
                hs = [h for h in range(4) if kt >= kmin[h]]
                for h in hs:
                    pt, pk = pss3.next()
                    P.op('pe', lambda e, pt=pt, kt=kt, h=h: e.matmul(pt[:], lhsT=ksA[0:65, kt * 128:(kt + 1) * 128], rhs=qa[0:65, h, :], start=True, stop=True),
                         reads=['ksA', 'ksA1'] + qkeys, writes=[pk])
                    if h == hs[0] and kt < kts:
                        mi_next = expand(kt + 1)
                    pi = cnt['pt'] % NPT
                    cnt['pt'] += 1
                    P.op('act', lambda e, pt=pt, pi=pi, h=h, dl=dl: e.activation(out=PT[pi][:], in_=pt[:], func=AF.Exp, bias=bsw[:, u, h, dl // 128 + 63:dl // 128 + 64], scale=SCALE),
                         reads=[pk, ('bsw', u)], writes=[('PT', pi)])
                    if dl >= 0:
                        P.op('pool', lambda e, pi=pi, dl=dl: e.affine_select(out=PT[pi][:], in_=PT[pi][:], pattern=[[1, QC]], compare_op=ALU.is_ge,
                                                                           fill=0.0, base=-dl, channel_multiplier=-1),
                             reads=[('PT', pi)], writes=[('PT', pi)])
                        eng = 'dve'
                    else:
                        eng = 'dve' if h % 2 == 0 else 'pool'
                    P.op(eng, lambda e, pi=pi, mt_=mt_: e.tensor_tensor(out=PTm[pi][:], in0=PT[pi][:], in1=mt_[:], op=ALU.mult),
                         reads=[('PT', pi), ('msb', mi)], writes=[('PTm', pi)])
                    pend.append((kt, h, pi, vsA, ['vsA', 'vsA1'], kt == kmin[h], kt == kts))
                    flush(SK)
            flush(0)
            for h in range(4):
                epilogue(h, 1)

            kt0w = max(0, 4 * qc - 4)
            kminw = [max(kt0w, kmin[h]) for h in range(4)]
            for kt in range(kt0w, kts + 1):
                dl = 128 * kt - q0
                for h in range(4):
                    if kt < kminw[h]:
                        continue
                    pt, pk = pss3.next()
                    P.op('pe', lambda e, pt=pt, kt=kt, h=h: e.matmul(pt[:], lhsT=kwA[0:65, kt * 128:(kt + 1) * 128], rhs=qa[0:65, h, :], start=True, stop=True),
                         reads=['kwA', 'kwA1'] + qkeys, writes=[pk])
                    pi = cnt['pt'] % NPT
                    cnt['pt'] += 1
                    P.op('act', lambda e, pt=pt, pi=pi, h=h, dl=dl: e.activation(out=PT[pi][:], in_=pt[:], func=AF.Exp, bias=bsw[:, u, h, dl // 128 + 63:dl // 128 + 64], scale=SCALE),
                         reads=[pk, ('bsw', u)], writes=[('PT', pi)])
                    eng = 'pool' if h % 2 == 0 else 'pool'
                    if dl >= 0:
                        P.op(eng, lambda e, pi=pi, dl=dl: e.affine_select(out=PTm[pi][:], in_=PT[pi][:], pattern=[[1, QC]], compare_op=ALU.is_ge,
                                                                        fill=0.0, base=-dl, channel_multiplier=-1),
                             reads=[('PT', pi)], writes=[('PTm', pi)])
                    else:
                        P.op(eng, lambda e, pi=pi, dl=dl: e.affine_select(out=PTm[pi][:], in_=PT[pi][:], pattern=[[-1, QC]], compare_op=ALU.is_ge,
                                                                        fill=0.0, base=dl + 511, channel_multiplier=1),
                             reads=[('PT', pi)], writes=[('PTm', pi)])
                    pend.append((kt, h, pi, vwA, ['vwA', 'vwA1'], kt == kminw[h], kt == kts))
                    flush(SK)
            flush(0)
            for h in range(4):
                epilogue(h, 2)
            for br in range(3 if dbg else 1):
                if not fused:
                    dst = A['o_dst'](u, br, q0).rearrange("(ts p) f -> p ts f", p=128)
                    P.dma('sp', dst, acc[slot][br][:], reads=[('acc', slot, br, h) for h in range(4)],
                          writes=[('o', u, qc, br)], semkey=('osem', slot, br))
                else:
                    for fb in range(2):
                        for ts in range(4):
                            P.op('pe', lambda e, ts=ts, fb=fb, br=br: e.transpose(psx[:, ts * 128:(ts + 1) * 128], acc[slot][br][:, ts, fb * 128:(fb + 1) * 128], identf[:]),
                                 reads=[('acc', slot, br, 2 * fb), ('acc', slot, br, 2 * fb + 1), 'identf'], writes=['psx'])
                        P.op('act', lambda e, fb=fb: e.copy(out=oT[fb][:], in_=psx[:]), reads=['psx'], writes=[('oT', fb)])
                        P.dma('sp', A['oT_dst'](u, fb, q0), oT[fb][:], reads=[('oT', fb)], writes=[('o', u, qc, fb)], semkey=('osem', fb))

    for u in range(nunits):
        do_unit(u)
        for qc in range(nqc):
            do_chunk(u, qc)
    P.phase_end()


def run_att(projs, w1k, w2k, w1v, w2v, pe_k, pe_v):
    nc = _prog(('att',), build_att_prog)
    consts = att_consts()
    maps = []
    for c in range(NCORES):
        b, hh = c // 2, c % 2
        full = np.concatenate([projs[2 * b], projs[2 * b + 1]], axis=1)
        qs, ks, vcs, vts, gs, augs, bsws, bcs = [], [], [], [], [], [], [], []
        for u in range(2):
            g = 2 * hh + u
            qs.append(full[g * 256:(g + 1) * 256].reshape(4, 64, S))
            kv = lambda i: full[1024 + i * 256 + g * 64:1024 + i * 256 + (g + 1) * 64]
            ks.append(np.stack([kv(0), kv(2), kv(4)]))
            vcs.append(kv(1))
            vts.append(np.stack([kv(3).T, kv(5).T]))
            gs.append(full[2560 + g * 12:2560 + (g + 1) * 12].T)
            aug, bsw, bc = att_tables(g)
            augs.append(aug[None])
            bsws.append(bsw)
            bcs.append(bc)
        m = {"qT": np.ascontiguousarray(np.stack(qs)), "kT": np.ascontiguousarray(np.stack(ks)), "vcT": np.ascontiguousarray(np.stack(vcs)),
             "vtok": np.ascontiguousarray(np.stack(vts)), "gates": np.ascontiguousarray(np.stack(gs)),
             "w1k": w1k, "w2k": w2k, "w1v": w1v, "w2v": w2v,
             "pek": np.ascontiguousarray(pe_k.reshape(16, 128).T), "pev": np.ascontiguousarray(pe_v.reshape(16, 128).T),
             "augrow": np.ascontiguousarray(np.stack(augs)), "bias_sw": np.ascontiguousarray(np.stack(bsws)), "bias_c": np.ascontiguousarray(np.stack(bcs))}
        m.update(consts)
        maps.append(m)
    res = _run(nc, maps)
    mixs = []
    for c in range(NCORES):
        b, hh = c // 2, c % 2
        parts = []
        for g in range(4):
            src = res[2 * b + g // 2]["o"][g % 2]
            parts.append(src[hh * TPC:(hh + 1) * TPC].T)
        mixs.append(np.ascontiguousarray(np.concatenate(parts, axis=0)))
    return mixs


def kernel_unfused(x, norm_mix, norm_ffn, norm_final,
           rg_w_in, rg_conv_w, rg_conv_b, rg_w_a, rg_b_a, rg_w_x, rg_b_x, rg_lambda, rg_w_out,
           nsa_w_in, nsa_b_gate, nsa_pe_k, nsa_pe_v, nsa_w1_k, nsa_w2_k, nsa_w1_v, nsa_w2_v, nsa_w_out,
           mlp_w_up, mlp_w_down):
    f = lambda a: np.ascontiguousarray(np.asarray(a, dtype=np.float32))
    x = f(x)
    xls = [to_xl(x[c // 2, (c % 2) * TPC:(c % 2 + 1) * TPC]) for c in range(NCORES)]
    r = run_token(xls, False, False, 'rg', g_nxt=f(norm_mix[0]), w_in=f(rg_w_in[0]))
    projs = [r[c]['proj'] for c in range(NCORES)]
    for i in range(4):
        j = i // 2
        if i % 2 == 0:
            mixs = run_scan(projs, f(rg_conv_w[j]), f(rg_conv_b[j]), f(rg_w_a[j]), f(rg_b_a[j]), f(rg_w_x[j]), f(rg_b_x[j]), f(rg_lambda[j]))
            w_out = f(rg_w_out[j])
        else:
            mixs = run_att(projs, f(nsa_w1_k[j]), f(nsa_w2_k[j]), f(nsa_w1_v[j]), f(nsa_w2_v[j]), f(nsa_pe_k[j]), f(nsa_pe_v[j]))
            w_out = f(nsa_w_out[j])
        if i == 3:
            r = run_token(xls, True, True, 'final', mixs=mixs, w_out=w_out, g_ffn=f(norm_ffn[i]), w_up=f(mlp_w_up[i]), w_down=f(mlp_w_down[i]),
                          g_nxt=f(norm_final))
            break
        if i % 2 == 0:
            r = run_token(xls, True, True, 'nsa', mixs=mixs, w_out=w_out, g_ffn=f(norm_ffn[i]), w_up=f(mlp_w_up[i]), w_down=f(mlp_w_down[i]),
                          g_nxt=f(norm_mix[i + 1]), w_in=f(nsa_w_in[(i + 1) // 2]), b_gate=f(nsa_b_gate[(i + 1) // 2]))
        else:
            r = run_token(xls, True, True, 'rg', mixs=mixs, w_out=w_out, g_ffn=f(norm_ffn[i]), w_up=f(mlp_w_up[i]), w_down=f(mlp_w_down[i]),
                          g_nxt=f(norm_mix[i + 1]), w_in=f(rg_w_in[(i + 1) // 2]))
        xls = [r[c]['x_out'] for c in range(NCORES)]
        projs = [r[c]['proj'] for c in range(NCORES)]
    out = np.empty((B, S, D), np.float32)
    for c in range(NCORES):
        out[c // 2, (c % 2) * TPC:(c % 2 + 1) * TPC] = from_xl(r[c]['y_out'])
    return out


NBLK_F = S // TB


def build_fused_prog():
    P = Prog()
    x_in = P.dram("x_in", [NBLK_F, 128, 8, TB], F32, "ExternalInput")
    y_out = P.dram("y_out", [NBLK_F, 128, 8, TB], F32, "ExternalOutput")
    gmix = P.dram("gmix", [4, 128, 8], F32, "ExternalInput")
    gffn = P.dram("gffn", [4, 128, 8], F32, "ExternalInput")
    gfin = P.dram("gfin", [128, 8], F32, "ExternalInput")
    rg_w_in = P.dram("rg_w_in", [2, D, 2048], F32, "ExternalInput")
    rg_cpar = P.dram("rg_cpar", [2, 4, 128, 2, 8], F32, "ExternalInput")
    rg_w_a = P.dram("rg_w_a", [2, 4, 256, 256], F32, "ExternalInput")
    rg_w_x = P.dram("rg_w_x", [2, 4, 256, 256], F32, "ExternalInput")
    rg_w_out = P.dram("rg_w_out", [2, D, D], F32, "ExternalInput")
    nsa_w_in = P.dram("nsa_w_in", [2, D, NSA_COLS], F32, "ExternalInput")
    nsa_b_gate = P.dram("nsa_b_gate", [2, 48, 1], F32, "ExternalInput")
    nsa_pek = P.dram("nsa_pek", [2, 128, 16], F32, "ExternalInput")
    nsa_pev = P.dram("nsa_pev", [2, 128, 16], F32, "ExternalInput")
    nsa_w1_k = P.dram("nsa_w1_k", [2, 2048, 64], F32, "ExternalInput")
    nsa_w2_k = P.dram("nsa_w2_k", [2, 64, 64], F32, "ExternalInput")
    nsa_w1_v = P.dram("nsa_w1_v", [2, 2048, 64], F32, "ExternalInput")
    nsa_w2_v = P.dram("nsa_w2_v", [2, 64, 64], F32, "ExternalInput")
    nsa_w_out = P.dram("nsa_w_out", [2, D, D], F32, "ExternalInput")
    mlp_w_up = P.dram("mlp_w_up", [4, D, DFF], F32, "ExternalInput")
    mlp_w_down = P.dram("mlp_w_down", [4, DFF, D], F32, "ExternalInput")
    T = {}
    for nm, shp in (("augrow", [4, 1, 4, QC]), ("bias_sw", [4, 128, 4, 67]), ("bias_c", [4, 128, 4, 4, NQC]), ("cmpmap", [128, 4, 128]),
                    ("eexp", [128, 64, 128]), ("winmask", [128, 8, QC]), ("causal", [128, 4, QC]), ("validW", [128, 256]), ("addW", [128, 256]),
                    ("ident", [128, 128])):
        T[nm] = P.dram(nm, shp, F32, "ExternalInput")
    X = P.dram("X_scr", [NBLK_F, 128, 8, TB], F32, "Internal")
    PROJ = P.dram("PROJ_scr", [NSA_COLS, S], F32, "Internal")
    MIX = P.dram("MIX_scr", [D, S], F32, "Internal")

    def scan_phase(j):
        emit_scan(P, 4,
                  lambda u, jj, a, b: PROJ[u * 256 + jj * 128:u * 256 + (jj + 1) * 128, a:b],
                  lambda u, jj, a, b: PROJ[1024 + u * 256 + jj * 128:1024 + u * 256 + (jj + 1) * 128, a:b],
                  rg_cpar[j], rg_w_a[j], rg_w_x[j],
                  lambda u, jj, a, b: MIX[u * 256 + jj * 128:u * 256 + (jj + 1) * 128, a:b])

    def att_phase(j):
        A = dict(T)
        A.update(w1k=nsa_w1_k[j], w2k=nsa_w2_k[j], w1v=nsa_w1_v[j], w2v=nsa_w2_v[j], pek=nsa_pek[j], pev=nsa_pev[j])
        A['q_src'] = lambda u, q0: PROJ[u * 256:(u + 1) * 256, q0:q0 + QC].rearrange("(h d) t -> d h t", d=64)
        A['k_src'] = lambda u, i: PROJ[1024 + (2 * i) * 256 + u * 64:1024 + (2 * i) * 256 + (u + 1) * 64, :]
        A['vc_src'] = lambda u: PROJ[1024 + 256 + u * 64:1024 + 256 + (u + 1) * 64, :]
        A['vT_src'] = lambda u, i: PROJ[1024 + (3 + 2 * i) * 256 + u * 64:1024 + (3 + 2 * i) * 256 + (u + 1) * 64, :]
        A['gT_src'] = lambda u, q0: PROJ[2560 + u * 12:2560 + (u + 1) * 12, q0:q0 + QC]
        A['oT_dst'] = lambda u, fb, q0: MIX[u * 256 + fb * 128:u * 256 + (fb + 1) * 128, q0:q0 + QC]
        emit_att(P, A, dbg=False, nunits=4, nqc=NQC, fused=True, groups=[0, 1, 2, 3])

    emit_token(P, NBLK_F, x_in, x_in, False, False, 'rg', dict(g_nxt=gmix[0], w_in=rg_w_in[0], proj=PROJ))
    xsrc = x_in
    for i in range(4):
        j = i // 2
        if i % 2 == 0:
            scan_phase(j)
            w_out = rg_w_out[j]
        else:
            att_phase(j)
            w_out = nsa_w_out[j]
        A = dict(mix=MIX, w_out=w_out, g_ffn=gffn[i], w_up=mlp_w_up[i], w_down=mlp_w_down[i])
        if i == 3:
            A.update(g_nxt=gfin, y_out=y_out)
            emit_token(P, NBLK_F, xsrc, X, True, True, 'final', A)
        elif i % 2 == 0:
            A.update(g_nxt=gmix[i + 1], w_in=nsa_w_in[(i + 1) // 2], b_gate=nsa_b_gate[(i + 1) // 2], proj=PROJ)
            emit_token(P, NBLK_F, xsrc, X, True, True, 'nsa', A)
        else:
            A.update(g_nxt=gmix[i + 1], w_in=rg_w_in[(i + 1) // 2], proj=PROJ)
            emit_token(P, NBLK_F, xsrc, X, True, True, 'rg', A)
        xsrc = X
    return P.build()


def kernel(x, norm_mix, norm_ffn, norm_final,
                 rg_w_in, rg_conv_w, rg_conv_b, rg_w_a, rg_b_a, rg_w_x, rg_b_x, rg_lambda, rg_w_out,
                 nsa_w_in, nsa_b_gate, nsa_pe_k, nsa_pe_v, nsa_w1_k, nsa_w2_k, nsa_w1_v, nsa_w2_v, nsa_w_out,
                 mlp_w_up, mlp_w_down):
    f = lambda a: np.ascontiguousarray(np.asarray(a, dtype=np.float32))
    x = f(x)
    nc = _prog(('fused',), build_fused_prog)
    common = {
        "gmix": np.stack([gvec(f(norm_mix[i])) for i in range(4)]),
        "gffn": np.stack([gvec(f(norm_ffn[i])) for i in range(4)]),
        "gfin": gvec(f(norm_final)),
        "rg_w_in": f(rg_w_in), "rg_w_a": f(rg_w_a), "rg_w_x": f(rg_w_x), "rg_w_out": f(rg_w_out),
        "nsa_w_in": f(nsa_w_in), "nsa_b_gate": f(nsa_b_gate).reshape(2, 48, 1),
        "nsa_pek": np.ascontiguousarray(f(nsa_pe_k).reshape(2, 16, 128).transpose(0, 2, 1)),
        "nsa_pev": np.ascontiguousarray(f(nsa_pe_v).reshape(2, 16, 128).transpose(0, 2, 1)),
        "nsa_w1_k": f(nsa_w1_k), "nsa_w2_k": f(nsa_w2_k), "nsa_w1_v": f(nsa_w1_v), "nsa_w2_v": f(nsa_w2_v), "nsa_w_out": f(nsa_w_out),
        "mlp_w_up": f(mlp_w_up), "mlp_w_down": f(mlp_w_down),
    }
    cps = []
    for j in range(2):
        par = np.stack([f(rg_conv_w[j])[0], f(rg_conv_w[j])[1], f(rg_conv_w[j])[2], f(rg_conv_w[j])[3], f(rg_conv_b[j]), f(rg_b_a[j]), f(rg_b_x[j]),
                        f(rg_lambda[j])], axis=-1)
        cps.append(par.reshape(4, 2, 128, 8).transpose(0, 2, 1, 3))
    common["rg_cpar"] = np.ascontiguousarray(np.stack(cps))
    tabs = [att_tables(g) for g in range(4)]
    common["augrow"] = np.ascontiguousarray(np.stack([t[0][None] for t in tabs]))
    common["bias_sw"] = np.ascontiguousarray(np.stack([t[1] for t in tabs]))
    common["bias_c"] = np.ascontiguousarray(np.stack([t[2] for t in tabs]))
    common.update(att_consts())
    maps = []
    for c in range(NCORES):
        m = dict(common)
        xb = x[c % B]
        m["x_in"] = np.ascontiguousarray(xb.reshape(NBLK_F, TB, 8, 128).transpose(0, 3, 2, 1))
        maps.append(m)
    res = _run(nc, maps)
    out = np.empty((B, S, D), np.float32)
    for b in range(B):
        out[b] = res[b]["y_out"].transpose(0, 3, 2, 1).reshape(S, D)
    return out
```

```python
import numpy as np
from contextlib import ExitStack
import concourse.bass as bass
import concourse.mybir as mybir
from concourse.bass_utils import run_bass_kernel_spmd

F32 = mybir.dt.float32
BF16 = mybir.dt.bfloat16
AF = mybir.ActivationFunctionType
ALU = mybir.AluOpType

ENGS = ('sp', 'pe', 'act', 'dve', 'pool')
SAME_ENGINE_SYNC = True

D = 1024
S = 8192
B = 4
NCORES = 8
TPC = 4096
TB = 256
NBLK = TPC // TB
DFF = 4096
EPS = 1e-6
NSA_COLS = 2608


class Prog:
    def __init__(self):
        self.nc = bass.Bass("TRN2", target_bir_lowering=False)
        self.es = ExitStack()
        self.ops = {e: [] for e in ENGS}
        self.cnt = {e: 0 for e in ENGS}
        self.lastw = {}
        self.readers = {}
        self.waited = {e: {} for e in ENGS}
        self.dmasem = {}
        self.semnames = ['c_' + e for e in ENGS if e != 'sp']
        self.nuniq = 0
        self.floor = {}
        self.pes = None

    def phase_begin(self):
        self.pes = ExitStack()

    def phase_end(self):
        for e in ENGS:
            if e != 'sp' and self.cnt[e] > 0:
                self.floor['c_' + e] = self.cnt[e]
        for name, c in self.dmasem.values():
            if c > 0:
                self.floor[name] = c
        self.pes.close()
        self.pes = None

    def sb(self, shape, dt, name=None):
        self.nuniq += 1
        return (self.pes or self.es).enter_context(self.nc.sbuf_tensor(name or f"sb{self.nuniq}", list(shape), dt))

    def ps(self, shape, dt, name=None):
        self.nuniq += 1
        return (self.pes or self.es).enter_context(self.nc.psum_tensor(name or f"ps{self.nuniq}", list(shape), dt))

    def dram(self, name, shape, dt, kind):
        return self.nc.dram_tensor(name, list(shape), dt, kind=kind).ap()

    def _deps(self, reads, writes):
        deps = {}

        def add(s, v):
            if deps.get(s, 0) < v:
                deps[s] = v
        for k in reads:
            t = self.lastw.get(k)
            if t is not None:
                add(*t)
        for k in writes:
            t = self.lastw.get(k)
            if t is not None:
                add(*t)
            for s, v in self.readers.get(k, {}).items():
                add(s, v)
        return deps

    def _commit(self, tok, reads, writes):
        s, v = tok
        for k in reads:
            r = self.readers.setdefault(k, {})
            if r.get(s, 0) < v:
                r[s] = v
        for k in writes:
            self.lastw[k] = tok
            self.readers[k] = {}

    def _waits(self, eng, deps):
        ws = []
        own = 'c_' + eng
        for s, v in self.floor.items():
            if deps.get(s, 0) < v:
                deps[s] = v
        for s, v in deps.items():
            if s == own and (eng == 'pe' or not SAME_ENGINE_SYNC):
                continue
            if self.waited[eng].get(s, 0) >= v:
                continue
            self.waited[eng][s] = v
            ws.append((s, v))
        return ws

    def op(self, eng, fn, reads=(), writes=()):
        deps = self._deps(reads, writes)
        ws = self._waits(eng, deps)
        self.cnt[eng] += 1
        tok = ('c_' + eng, self.cnt[eng])
        self.ops[eng].append((ws, fn, (tok[0], 1)))
        self._commit(tok, reads, writes)
        return tok

    def dma(self, q, out, in_, reads=(), writes=(), semkey=None):
        semkey = semkey if semkey is not None else writes[0]
        if semkey not in self.dmasem:
            name = f"d{len(self.dmasem)}"
            self.dmasem[semkey] = [name, 0]
            self.semnames.append(name)
        ent = self.dmasem[semkey]
        deps = self._deps(reads, writes)
        ws = self._waits(q, deps)
        ent[1] += 16
        tok = (ent[0], ent[1])
        self.ops[q].append((ws, lambda e: e.dma_start(out=out, in_=in_), (ent[0], 16)))
        self._commit(tok, reads, writes)
        return tok

    def build(self):
        nc = self.nc
        final = {}
        for e in ENGS:
            if e != 'sp' and self.cnt[e] > 0:
                final['c_' + e] = self.cnt[e]
        for name, c in self.dmasem.values():
            final[name] = c
        fws = list(final.items())
        with ExitStack() as es:
            sems = {n: es.enter_context(nc.semaphore(n)) for n in self.semnames}
            ops = self.ops

            def replay(name, e):
                for ws, fn, inc in ops[name]:
                    for s, v in ws:
                        e.wait_ge(sems[s], v)
                    fn(e).then_inc(sems[inc[0]], inc[1])
                if name == 'sp':
                    for s, v in fws:
                        e.wait_ge(sems[s], v)
            with nc.Block() as block:
                @block.sync
                def _(e):
                    replay('sp', e)

                @block.tensor
                def _(e):
                    replay('pe', e)

                @block.scalar
                def _(e):
                    replay('act', e)

                @block.vector
                def _(e):
                    replay('dve', e)

                @block.gpsimd
                def _(e):
                    replay('pool', e)
        self.es.close()
        return nc


class PsumRot:
    def __init__(self, P, n, prefix):
        self.tiles = [P.ps([128, 512], F32) for _ in range(n)]
        self.keys = [(prefix, i) for i in range(n)]
        self.i = 0

    def next(self):
        t, k = self.tiles[self.i], self.keys[self.i]
        self.i = (self.i + 1) % len(self.tiles)
        return t, k


def build_token_prog(has_post, has_mlp, nxt):
    P = Prog()
    A = {}
    x_in = P.dram("x_in", [NBLK, 128, 8, TB], F32, "ExternalInput")
    write_x = has_post or has_mlp
    if write_x and nxt != 'final':
        x_out = P.dram("x_out", [NBLK, 128, 8, TB], F32, "ExternalOutput")
    elif write_x:
        x_out = P.dram("x_scr", [NBLK, 128, 8, TB], F32, "Internal")
    else:
        x_out = x_in
    if has_post:
        A['mix'] = P.dram("mix", [D, TPC], F32, "ExternalInput")
        A['w_out'] = P.dram("w_out", [D, D], F32, "ExternalInput")
    if has_mlp:
        A['g_ffn'] = P.dram("g_ffn", [128, 8], F32, "ExternalInput")
        A['w_up'] = P.dram("w_up", [D, DFF], F32, "ExternalInput")
        A['w_down'] = P.dram("w_down", [DFF, D], F32, "ExternalInput")
    A['g_nxt'] = P.dram("g_nxt", [128, 8], F32, "ExternalInput")
    if nxt == 'nsa':
        A['b_gate'] = P.dram("b_gate", [48, 1], F32, "ExternalInput")
    if nxt != 'final':
        ncols = 2048 if nxt == 'rg' else NSA_COLS
        A['w_in'] = P.dram("w_in", [D, ncols], F32, "ExternalInput")
        A['proj'] = P.dram("proj", [ncols, TPC], F32, "ExternalOutput")
    else:
        A['y_out'] = P.dram("y_out", [NBLK, 128, 8, TB], F32, "ExternalOutput")
    emit_token(P, NBLK, x_in, x_out, has_post, has_mlp, nxt, A)
    return P.build()


def emit_token(P, NBLK, x_in, x_out, has_post, has_mlp, nxt, A):
    write_x = has_post or has_mlp
    mix, w_out = A.get('mix'), A.get('w_out')
    g_ffn, w_up, w_down = A.get('g_ffn'), A.get('w_up'), A.get('w_down')
    g_nxt, b_gate, w_in, proj, y_out = A.get('g_nxt'), A.get('b_gate'), A.get('w_in'), A.get('proj'), A.get('y_out')
    ncols = 2048 if nxt == 'rg' else NSA_COLS
    P.phase_begin()
    WA = P.sb([128, 8 * DFF], BF16)
    WB = P.sb([128, 32 * D], BF16)
    WO = P.sb([128, 8 * D], BF16)
    xbuf = [P.sb([128, 8, TB], F32) for _ in range(2)]
    mbuf = [P.sb([128, 8, TB], BF16) for _ in range(2)]
    hb = P.sb([128, 8, TB], BF16)
    sq = P.sb([128, 8, TB], BF16)
    rs = P.sb([128, TB], F32)
    hb2 = P.sb([128, 8, TB], BF16)
    rs2 = P.sb([128, TB], F32)
    u2 = P.sb([128, 32, TB], BF16)
    rt = [P.sb([128, TB], F32) for _ in range(2)]
    st = [P.sb([128, TB], F32) for _ in range(4)]
    ones_bf = P.sb([128, 128], BF16)
    gf = P.sb([128, 8], F32)
    gn = P.sb([128, 8], F32)
    P.eps_tile = P.sb([128, 1], F32)
    bg = P.sb([48, 1], F32)
    psr = PsumRot(P, 6, 'ps')
    psn = P.ps([128, 512], F32)

    P.op('pool', lambda e: e.memset(ones_bf[:], 1.0), writes=['ones'])
    P.op('pool', lambda e: e.memset(P.eps_tile[:], EPS), writes=['epsc'])
    if has_mlp:
        P.dma('sp', gf[:], g_ffn, writes=['gvec_f'])
    P.dma('sp', gn[:], g_nxt, writes=['gvec_n'])
    if nxt == 'nsa':
        P.dma('sp', bg[:], b_gate, writes=['bg'])

    WAv = WA[:].rearrange("p (c f) -> p c f", c=8)
    WBv = WB[:].rearrange("p (c f) -> p c f", c=32)
    WOv = WO[:].rearrange("p (c f) -> p c f", c=8)
    if has_post:
        wsrc = w_out.rearrange("(c p) f -> p c f", p=128)
        for c in range(0, 8, 4):
            P.dma('pool', WOv[:, c:c + 4, :], wsrc[:, c:c + 4, :], writes=[('WO', c)])
        wo_keys = [('WO', 0), ('WO', 4)]
    if has_mlp:
        wsrc = w_up.rearrange("(c p) f -> p c f", p=128)
        for c in range(8):
            P.dma('pool', WAv[:, c, :], wsrc[:, c, :], writes=[('WA', c)])
        wsrc = w_down.rearrange("(c p) f -> p c f", p=128)
        for c in range(0, 32, 4):
            P.dma('pool', WBv[:, c:c + 4, :], wsrc[:, c:c + 4, :], writes=[('WB', c)])

    mixv = mix.rearrange("(c p) t -> p c t", p=128) if has_post else None

    def xkeys(slot):
        return [('xb', slot, c) for c in range(8)]

    if write_x:
        hbs = [hb, hb2]
        sqs = [sq, sq]
        rss = [rs, rs2]

        def load(blk):
            sl = blk % 2
            P.dma('sp', xbuf[sl][:], x_in[blk], writes=xkeys(sl), semkey=('xbsem', sl))
            if has_post:
                P.dma('pool', mbuf[sl][:], mixv[:, :, blk * TB:(blk + 1) * TB], writes=[('mb', sl)])

        def stageA(blk):
            slot = blk % 2
            xb = xbuf[slot]
            xk = xkeys(slot)
            if has_post:
                mb = mbuf[slot]
                for cc in range(8):
                    pt, pk = psr.next()
                    for kc in range(8):
                        P.op('pe', lambda e, pt=pt, kc=kc, cc=cc, mb=mb: e.matmul(
                            pt[:, :TB], lhsT=WOv[:, kc, cc * 128:(cc + 1) * 128], rhs=mb[:, kc, :], start=(kc == 0), stop=(kc == 7)),
                            reads=[('mb', slot)] + wo_keys, writes=[pk])
                    P.op('dve', lambda e, pt=pt, cc=cc, xb=xb: e.tensor_tensor(out=xb[:, cc, :], in0=pt[:, :TB], in1=xb[:, cc, :], op=ALU.add),
                         reads=[pk, xk[cc]], writes=[xk[cc]])
            if has_mlp:
                hk = [('hb', slot, c) for c in range(8)]
                emit_norm_k(P, xb, xk, gf, 'gvec_f', hbs[slot], hk, ones_bf, sqs[slot], psn, rss[slot], sfx=slot, sqk=0)

        def stageB(blk):
            slot = blk % 2
            hk = [('hb', slot, c) for c in range(8)]
            hcur = hbs[slot]
            for fc in range(32):
                pt, pk = psr.next()
                for kc in range(8):
                    P.op('pe', lambda e, pt=pt, kc=kc, fc=fc, hcur=hcur: e.matmul(
                        pt[:, :TB], lhsT=WAv[:, kc, fc * 128:(fc + 1) * 128], rhs=hcur[:, kc, :], start=(kc == 0), stop=(kc == 7)),
                        reads=[hk[kc], ('WA', kc)], writes=[pk])
                r = rt[fc % 2]
                P.op('act', lambda e, pt=pt, r=r: e.activation(out=r[:], in_=pt[:, :TB], func=AF.Relu),
                     reads=[pk], writes=[('rt', fc % 2)])
                eng = 'dve' if fc % 2 == 0 else 'pool'
                P.op(eng, lambda e, r=r, fc=fc: e.tensor_tensor(out=u2[:, fc, :], in0=r[:], in1=r[:], op=ALU.mult),
                     reads=[('rt', fc % 2)], writes=[('u2', fc)])

        def stageC(blk):
            slot = blk % 2
            xb = xbuf[slot]
            xk = xkeys(slot)
            if has_mlp:
                for cc in range(8):
                    pt, pk = psr.next()
                    for fc in range(32):
                        P.op('pe', lambda e, pt=pt, fc=fc, cc=cc: e.matmul(
                            pt[:, :TB], lhsT=WBv[:, fc, cc * 128:(cc + 1) * 128], rhs=u2[:, fc, :], start=(fc == 0), stop=(fc == 31)),
                            reads=[('u2', fc), ('WB', (fc // 4) * 4)], writes=[pk])
                    P.op('dve', lambda e, pt=pt, cc=cc, xb=xb: e.tensor_tensor(out=xb[:, cc, :], in0=pt[:, :TB], in1=xb[:, cc, :], op=ALU.add),
                         reads=[pk, xk[cc]], writes=[xk[cc]])
            P.dma('sp', x_out[blk], xb[:], reads=xk, writes=[('xout', blk)], semkey=('xosem', slot))

        load(0)
        if NBLK > 1:
            load(1)
        stageA(0)
        for blk in range(NBLK):
            if has_mlp:
                stageB(blk)
            if blk + 1 < NBLK:
                stageA(blk + 1)
            stageC(blk)
            if blk + 2 < NBLK:
                load(blk + 2)

    if nxt != 'final':
        nch = (ncols + 127) // 128
        WIv = WA[:, 0:8 * ncols].rearrange("p (c f) -> p c f", c=8)
        wsrc = w_in.rearrange("(c p) f -> p c f", p=128)
        for c in range(8):
            P.dma('pool', WIv[:, c, :], wsrc[:, c, :], writes=[('WA', c)])
    hbs3 = [hb, hb2]

    def p3_load(blk):
        sl = blk % 2
        P.dma('sp', xbuf[sl][:], x_out[blk], reads=[('xout', blk)] if write_x else [], writes=xkeys(sl), semkey=('xbsem', sl))

    def p3_norm(blk):
        sl = blk % 2
        emit_norm_k(P, xbuf[sl], xkeys(sl), gn, 'gvec_n', hbs3[sl], [('hb', sl, c) for c in range(8)], ones_bf, sq, psn, rs)

    if nxt == 'final':
        for blk in range(NBLK):
            slot = blk % 2
            xb = xbuf[slot]
            xk = xkeys(slot)
            p3_load(blk)
            emit_norm_k(P, xb, xk, gn, 'gvec_n', None, [('st', c % 4) for c in range(8)], ones_bf, sq, psn, rs,
                        outs=lambda c: st[c % 4][:],
                        after=lambda c: P.dma('sp', y_out[blk][:, c, :], st[c % 4][:], reads=[('st', c % 4)], writes=[('yout', blk, c)],
                                              semkey=('stsem', c % 4)))
    else:
        p3_load(0)
        if NBLK > 1:
            p3_load(1)
        p3_norm(0)
        for blk in range(NBLK):
            slot = blk % 2
            hcur = hbs3[slot]
            hk = [('hb', slot, c) for c in range(8)]
            for cc in range(nch):
                if cc == nch // 2 and blk + 1 < NBLK:
                    p3_norm(blk + 1)
                m = min(128, ncols - cc * 128)
                pt, pk = psr.next()
                for kc in range(8):
                    P.op('pe', lambda e, pt=pt, kc=kc, cc=cc, m=m, hcur=hcur: e.matmul(
                        pt[0:m, :TB], lhsT=WIv[:, kc, cc * 128:cc * 128 + m], rhs=hcur[:, kc, :], start=(kc == 0), stop=(kc == 7)),
                        reads=[hk[kc], ('WA', kc)], writes=[pk])
                s_ = st[cc % 4]
                sk = ('st', cc % 4)
                if nxt == 'rg' and cc < 8:
                    P.op('act', lambda e, pt=pt, s_=s_: e.activation(out=s_[:], in_=pt[:, :TB], func=AF.Gelu_apprx_tanh), reads=[pk], writes=[sk])
                elif nxt == 'nsa' and m < 128:
                    P.op('act', lambda e, pt=pt, s_=s_, m=m: e.activation(out=s_[0:m, :], in_=pt[0:m, :TB], func=AF.Sigmoid, bias=bg[:]),
                         reads=[pk, 'bg'], writes=[sk])
                else:
                    P.op('dve', lambda e, pt=pt, s_=s_: e.tensor_copy(s_[:], pt[:, :TB]), reads=[pk], writes=[sk])
                P.dma('sp', proj[cc * 128:cc * 128 + m, blk * TB:(blk + 1) * TB], s_[0:m, :], reads=[sk], writes=[('proj', blk, cc)],
                      semkey=('stsem', cc % 4))
            if blk + 2 < NBLK:
                p3_load(blk + 2)
    P.phase_end()


def emit_norm_k(P, xb, xk, g_sb, gkey, hb, hk, ones_bf, sq, psn, rs, sfx=0, sqk=0, outs=None, after=None):
    P.op('act', lambda e: e.activation(out=sq[:], in_=xb[:], func=AF.Square), reads=xk, writes=[('sq', sqk)])
    for c in range(8):
        P.op('pe', lambda e, c=c: e.matmul(psn[:, :TB], lhsT=ones_bf[:], rhs=sq[:, c, :], start=(c == 0), stop=(c == 7)),
             reads=[('sq', sqk), 'ones'], writes=['psn'])
    P.op('act', lambda e: e.activation(out=rs[:], in_=psn[:, :TB], func=AF.Sqrt, bias=P.eps_tile[:], scale=1.0 / D),
         reads=['psn', 'epsc'], writes=[('rs', sfx)])
    P.op('dve', lambda e: e.reciprocal(rs[:], rs[:]), reads=[('rs', sfx)], writes=[('rs', sfx)])
    for c in range(8):
        dst = outs(c) if outs is not None else hb[:, c, :]
        P.op('dve', lambda e, c=c, dst=dst: e.scalar_tensor_tensor(out=dst, in0=xb[:, c, :], scalar=g_sb[:, c:c + 1], in1=rs[:],
                                                                 op0=ALU.mult, op1=ALU.mult),
             reads=[xk[c], ('rs', sfx), gkey], writes=[hk[c]])
        if after is not None:
            after(c)


SEG = 2048
NSEG = S // SEG


def build_scan_prog():
    P = Prog()
    ysrc = P.dram("ysrc", [2, 256, S], F32, "ExternalInput")
    xpsrc = P.dram("xpsrc", [2, 256, S], F32, "ExternalInput")
    cpar = P.dram("cpar", [2, 128, 2, 8], F32, "ExternalInput")
    wa = P.dram("wa", [2, 256, 256], F32, "ExternalInput")
    wx = P.dram("wx", [2, 256, 256], F32, "ExternalInput")
    mixo = P.dram("mixo", [2, 256, S], F32, "ExternalOutput")
    emit_scan(P, 2, lambda u, j, a, b: ysrc[u, j * 128:(j + 1) * 128, a:b], lambda u, j, a, b: xpsrc[u, j * 128:(j + 1) * 128, a:b],
              cpar, wa, wx, lambda u, j, a, b: mixo[u, j * 128:(j + 1) * 128, a:b])
    return P.build()


def emit_scan(P, NU, ysrc, xpsrc, cpar, wa, wx, mixo):
    P.phase_begin()
    cp = P.sb([128, NU, 2, 8], F32)
    cst = P.sb([128, NU, 2, 4], F32)
    wab = P.sb([128, NU, 2, 256], BF16)
    wxb = P.sb([128, NU, 2, 256], BF16)
    xp = [P.sb([128, 3 + SEG], F32) for _ in range(2)]
    xb = [P.sb([128, SEG], F32) for _ in range(2)]
    xbb = [P.sb([128, SEG], BF16) for _ in range(2)]
    rt = P.sb([128, SEG], F32)
    it = P.sb([128, SEG], F32)
    at = P.sb([128, SEG], F32)
    mt = P.sb([128, SEG], F32)
    ut = P.sb([128, SEG], F32)
    ht = P.sb([128, SEG], F32)
    yt = P.sb([128, SEG], F32)
    ot = P.sb([128, SEG], F32)
    hcar = P.sb([128, 2], F32)
    psr = PsumRot(P, 6, 'ps')

    for u in range(NU):
        P.dma('sp', cp[:, u], cpar[u], writes=[('cp', u)])
        P.dma('pool', wab[:, u], wa[u].rearrange("(j p) o -> p j o", p=128), writes=[('wab', u)])
        P.dma('pool', wxb[:, u], wx[u].rearrange("(j p) o -> p j o", p=128), writes=[('wxb', u)])
        for j in range(2):
            lam = cp[:, u, j, 7:8]
            P.op('act', lambda e, u=u, j=j, lam=lam: e.activation(out=cst[:, u, j, 0:1], in_=lam, func=AF.Exp, scale=-1.0),
                 reads=[('cp', u)], writes=[('cst', u, j)])
            P.op('act', lambda e, u=u, j=j: e.activation(out=cst[:, u, j, 1:2], in_=cst[:, u, j, 0:1], func=AF.Ln, bias=1.0),
                 reads=[('cst', u, j)], writes=[('cst', u, j)])
            P.op('dve', lambda e, u=u, j=j: e.tensor_scalar(out=cst[:, u, j, 2:3], in0=cst[:, u, j, 1:2], scalar1=-8.0, scalar2=None, op0=ALU.mult),
                 reads=[('cst', u, j)], writes=[('cst', u, j)])
            P.op('dve', lambda e, u=u, j=j: e.tensor_scalar(out=cst[:, u, j, 3:4], in0=cst[:, u, j, 1:2], scalar1=-16.0, scalar2=None, op0=ALU.mult),
                 reads=[('cst', u, j)], writes=[('cst', u, j)])

    for u in range(NU):
        for s in range(NSEG):
            t0 = s * SEG
            for j in range(2):
                rows = slice(j * 128, (j + 1) * 128)
                if s == 0:
                    P.op('pool', lambda e, j=j: e.memset(xp[j][:, 0:3], 0.0), writes=[('xp', j)])
                    P.dma('sp', xp[j][:, 3:3 + SEG], xpsrc(u, j, 0, SEG), writes=[('xp', j)], semkey=('xpsem', j))
                else:
                    P.dma('sp', xp[j][:, :], xpsrc(u, j, t0 - 3, t0 + SEG), writes=[('xp', j)], semkey=('xpsem', j))
                cw = lambda k, u=u, j=j: cp[:, u, j, k:k + 1]
                P.op('dve', lambda e, j=j, cw=cw: e.tensor_scalar(out=xb[j][:], in0=xp[j][:, 3:3 + SEG], scalar1=cw(3), scalar2=cw(4),
                                                                 op0=ALU.mult, op1=ALU.add),
                     reads=[('xp', j), ('cp', u)], writes=[('xb', j)])
                for k in (2, 1, 0):
                    P.op('dve', lambda e, j=j, k=k, cw=cw: e.scalar_tensor_tensor(out=xb[j][:], in0=xp[j][:, k:k + SEG], scalar=cw(k), in1=xb[j][:],
                                                                                 op0=ALU.mult, op1=ALU.add),
                         reads=[('xp', j), ('cp', u), ('xb', j)], writes=[('xb', j)])
                P.op('act', lambda e, j=j: e.copy(out=xbb[j][:], in_=xb[j][:]), reads=[('xb', j)], writes=[('xbb', j)])
            for j in range(2):
                rows = slice(j * 128, (j + 1) * 128)
                P.dma('sp', yt[:], ysrc(u, j, t0, t0 + SEG), writes=['yt'])
                for (wt, wkey, dst, dkey, bk) in ((wab, 'wab', rt, 'rt', 5), (wxb, 'wxb', it, 'it', 6)):
                    for tb in range(SEG // 512):
                        pt, pk = psr.next()
                        for jin in range(2):
                            P.op('pe', lambda e, pt=pt, wt=wt, jin=jin, j=j, tb=tb, u=u: e.matmul(
                                pt[:], lhsT=wt[:, u, jin, j * 128:(j + 1) * 128], rhs=xbb[jin][:, tb * 512:(tb + 1) * 512],
                                start=(jin == 0), stop=(jin == 1)),
                                reads=[('xbb', jin), (wkey, u)], writes=[pk])
                        P.op('act', lambda e, pt=pt, dst=dst, tb=tb, u=u, j=j, bk=bk: e.activation(
                            out=dst[:, tb * 512:(tb + 1) * 512], in_=pt[:], func=AF.Sigmoid, bias=cp[:, u, j, bk:bk + 1]),
                            reads=[pk, ('cp', u)], writes=[dkey])
                P.op('act', lambda e, u=u, j=j: e.activation(out=at[:], in_=rt[:], func=AF.Exp, scale=cst[:, u, j, 2:3]),
                     reads=['rt', ('cst', u, j)], writes=['at'])
                P.op('act', lambda e, u=u, j=j: e.activation(out=mt[:], in_=rt[:], func=AF.Exp, scale=cst[:, u, j, 3:4]),
                     reads=['rt', ('cst', u, j)], writes=['mt'])
                P.op('dve', lambda e: e.tensor_scalar(out=mt[:], in0=mt[:], scalar1=-1.0, scalar2=1.0, op0=ALU.mult, op1=ALU.add),
                     reads=['mt'], writes=['mt'])
                P.op('dve', lambda e: e.tensor_scalar_max(out=mt[:], in0=mt[:], scalar1=1e-20), reads=['mt'], writes=['mt'])
                P.op('act', lambda e: e.activation(out=mt[:], in_=mt[:], func=AF.Sqrt), reads=['mt'], writes=['mt'])
                P.op('pool', lambda e, j=j: e.tensor_tensor(out=ut[:], in0=it[:], in1=xb[j][:], op=ALU.mult),
                     reads=['it', ('xb', j)], writes=['ut'])
                P.op('dve', lambda e: e.tensor_tensor(out=ut[:], in0=ut[:], in1=mt[:], op=ALU.mult), reads=['ut', 'mt'], writes=['ut'])
                init = 0.0 if s == 0 else hcar[:, j:j + 1]
                P.op('dve', lambda e, init=init: e.tensor_tensor_scan(ht[:], at[:], ut[:], init, ALU.mult, ALU.add),
                     reads=['at', 'ut', ('hcar', j)], writes=['ht'])
                P.op('dve', lambda e, j=j: e.tensor_copy(hcar[:, j:j + 1], ht[:, SEG - 1:SEG]), reads=['ht'], writes=[('hcar', j)])
                P.op('pool', lambda e: e.tensor_tensor(out=ot[:], in0=ht[:], in1=yt[:], op=ALU.mult), reads=['ht', 'yt'], writes=['ot'])
                P.dma('sp', mixo(u, j, t0, t0 + SEG), ot[:], reads=['ot'], writes=[('mixo', u, j, s)], semkey=('osem',))
    P.phase_end()


_PROGS = {}


def _prog(key, fn):
    if key not in _PROGS:
        _PROGS[key] = fn()
    return _PROGS[key]


def _run(nc, in_maps):
    res = run_bass_kernel_spmd(nc, in_maps, core_ids=list(range(NCORES)))
    return res.results


def to_xl(xc):
    return np.ascontiguousarray(xc.reshape(NBLK, TB, 8, 128).transpose(0, 3, 2, 1))


def from_xl(xl):
    return np.ascontiguousarray(xl.transpose(0, 3, 2, 1).reshape(TPC, D))


def gvec(g):
    return np.ascontiguousarray(g.reshape(8, 128).T)


def run_token(xls, has_post, has_mlp, nxt, mixs=None, w_out=None, g_ffn=None, w_up=None, w_down=None, g_nxt=None, w_in=None, b_gate=None):
    nc = _prog(('tok', has_post, has_mlp, nxt), lambda: build_token_prog(has_post, has_mlp, nxt))
    maps = []
    for c in range(NCORES):
        m = {"x_in": xls[c], "g_nxt": gvec(g_nxt)}
        if has_post:
            m["mix"] = mixs[c]
            m["w_out"] = w_out
        if has_mlp:
            m["g_ffn"] = gvec(g_ffn)
            m["w_up"] = w_up
            m["w_down"] = w_down
        if nxt != 'final':
            m["w_in"] = w_in
        if nxt == 'nsa':
            m["b_gate"] = np.ascontiguousarray(b_gate.reshape(48, 1))
        maps.append(m)
    return _run(nc, maps)


def run_scan(projs, conv_w, conv_b, w_a, b_a, w_x, b_x, lam):
    nc = _prog(('scan',), build_scan_prog)
    maps = []
    for c in range(NCORES):
        b, hh = c // 2, c % 2
        ys, xs, cps, was, wxs = [], [], [], [], []
        for u in range(2):
            n = 2 * hh + u
            ch = slice(n * 256, (n + 1) * 256)
            ys.append(np.concatenate([projs[2 * b][ch], projs[2 * b + 1][ch]], axis=1))
            ch2 = slice(1024 + n * 256, 1024 + (n + 1) * 256)
            xs.append(np.concatenate([projs[2 * b][ch2], projs[2 * b + 1][ch2]], axis=1))
            par = np.stack([conv_w[0, ch], conv_w[1, ch], conv_w[2, ch], conv_w[3, ch], conv_b[ch], b_a[ch], b_x[ch], lam[ch]], axis=-1)
            cps.append(par.reshape(2, 128, 8).transpose(1, 0, 2))
            was.append(w_a[n])
            wxs.append(w_x[n])
        maps.append({"ysrc": np.ascontiguousarray(np.stack(ys)), "xpsrc": np.ascontiguousarray(np.stack(xs)),
                     "cpar": np.ascontiguousarray(np.stack(cps)), "wa": np.ascontiguousarray(np.stack(was)),
                     "wx": np.ascontiguousarray(np.stack(wxs))})
    res = _run(nc, maps)
    mixs = []
    for c in range(NCORES):
        b, hh = c // 2, c % 2
        parts = []
        for n in range(4):
            src = res[2 * b + n // 2]["mixo"][n % 2]
            parts.append(src[:, hh * TPC:(hh + 1) * TPC])
        mixs.append(np.ascontiguousarray(np.concatenate(parts, axis=0)))
    return mixs


QC = 512
NQC = S // QC
NCMP = 511
SCALE = 0.125


def att_tables(g):
    slopes = np.exp2(-8.0 * (np.arange(1, 17, dtype=np.float64)) / 16.0).reshape(4, 4)[g]
    tq = np.arange(QC, dtype=np.float64)
    aug = (-slopes[:, None] * tq[None, :] / SCALE).astype(np.float32)
    kk = np.arange(128, dtype=np.float64)
    dl = (np.arange(67, dtype=np.float64) - 63.0) * 128.0
    bias_sw = (slopes[None, :, None] * (dl[None, None, :] + kk[:, None, None])).astype(np.float32)
    kt = np.arange(4, dtype=np.float64)
    qc = np.arange(NQC, dtype=np.float64)
    pos = 16.0 * (128.0 * kt[None, :, None] + kk[:, None, None]) + 31.0 - QC * qc[None, None, :]
    bias_c = (slopes[None, :, None, None] * pos[:, None, :, :]).astype(np.float32)
    return aug, bias_sw, bias_c


def att_consts():
    n = np.arange(512)
    j = np.arange(128)
    ov = np.minimum(n[:, None] * 16 + 32, j[None, :] * 64 + 64) - np.maximum(n[:, None] * 16, j[None, :] * 64)
    cm = (np.maximum(ov, 0) / 16.0).astype(np.float32)
    cm[511] = 0.0
    cmpmap = np.ascontiguousarray(cm.reshape(4, 128, 128).transpose(1, 0, 2))
    jj = np.arange(128)[:, None, None]
    ktt = np.arange(64)[None, :, None]
    kk = np.arange(128)[None, None, :]
    eexp = (jj == 2 * ktt + kk // 64).astype(np.float32)
    kq = np.arange(128)[:, None, None]
    tq = np.arange(QC)[None, None, :]
    dw = (np.arange(8)[None, :, None] - 4) * 128
    d = tq - kq - dw
    winmask = ((d >= 0) & (d < 512)).astype(np.float32)
    dc = np.arange(4)[None, :, None] * 128
    causal = ((tq - kq - dc) >= 0).astype(np.float32)
    tt = np.arange(128)[:, None]
    w = np.arange(256)[None, :]
    jr = w - 126
    c = (tt >= 64).astype(np.int64)
    valid = jr <= c
    forced = (jr == c) | (jr == c - 1)
    tb = (300.0 - w) * 1e-35
    validW = valid.astype(np.float32)
    addW = np.where(valid, 1e4 * forced + tb, -1.0).astype(np.float32)
    ident = np.eye(128, dtype=np.float32)
    return dict(cmpmap=cmpmap, eexp=eexp, winmask=winmask, causal=causal, validW=validW, addW=addW, ident=ident)


def build_att_prog(dbg=False, nunits=2, nqc=NQC):
    P = Prog()
    A = {}
    qT = P.dram("qT", [2, 4, 64, S], F32, "ExternalInput")
    kT = P.dram("kT", [2, 3, 64, S], F32, "ExternalInput")
    vcT = P.dram("vcT", [2, 64, S], F32, "ExternalInput")
    vtok = P.dram("vtok", [2, 2, S, 64], F32, "ExternalInput")
    gates = P.dram("gates", [2, S, 12], F32, "ExternalInput")
    for nm, shp in (("w1k", [2048, 64]), ("w2k", [64, 64]), ("w1v", [2048, 64]), ("w2v", [64, 64]), ("pek", [128, 16]), ("pev", [128, 16]),
                    ("augrow", [2, 1, 4, QC]), ("bias_sw", [2, 128, 4, 67]), ("bias_c", [2, 128, 4, 4, NQC]), ("cmpmap", [128, 4, 128]),
                    ("eexp", [128, 64, 128]), ("winmask", [128, 8, QC]), ("causal", [128, 4, QC]), ("validW", [128, 256]), ("addW", [128, 256]),
                    ("ident", [128, 128])):
        A[nm] = P.dram(nm, shp, F32, "ExternalInput")
    o_d = P.dram("o", [2, 3, S, 256] if dbg else [2, S, 256], F32, "ExternalOutput")
    A['q_src'] = lambda u, q0: qT[u, :, :, q0:q0 + QC].rearrange("h d t -> d h t")
    A['k_src'] = lambda u, i: kT[u, i]
    A['vc_src'] = lambda u: vcT[u]
    A['vtok_src'] = lambda u, i: vtok[u, i]
    A['gates_src'] = lambda u, q0: gates[u, q0:q0 + QC, :]
    A['o_dst'] = (lambda u, br, q0: o_d[u, br, q0:q0 + QC, :]) if dbg else (lambda u, br, q0: o_d[u, q0:q0 + QC, :])
    emit_att(P, A, dbg=dbg, nunits=nunits, nqc=nqc, fused=False)
    return P.build()


def emit_att(P, A, dbg=False, nunits=2, nqc=NQC, fused=False, groups=None):
    P.phase_begin()
    ALIBI_CUT = 100.0

    def dcut(u, h):
        if groups is None:
            return 1e30
        return ALIBI_CUT / (2.0 ** (-(4 * groups[u] + h + 1) / 2.0))

    NU = nunits
    w1k, w2k, w1v, w2v, pek, pev = A['w1k'], A['w2k'], A['w1v'], A['w2v'], A['pek'], A['pev']
    augrow, bias_sw_d, bias_c_d = A['augrow'], A['bias_sw'], A['bias_c']
    cmpmap_d, eexp_d, winmask_d, causal_d = A['cmpmap'], A['eexp'], A['winmask'], A['causal']
    validW_d, addW_d, ident_d = A['validW'], A['addW'], A['ident']

    zk = P.sb([64, S], BF16)
    zv = P.sb([64, S], BF16)
    ksA = P.sb([65, S], BF16)
    kwA = P.sb([65, S], BF16)
    vsA = P.sb([128, 64, 65], BF16)
    vwA = P.sb([128, 64, 65], BF16)
    kcA = P.sb([65, 512], BF16)
    vcA = P.sb([128, 4, 65], BF16)
    eexp = P.sb([128, 64, 128], BF16)
    cmpb = P.sb([128, 4, 128], BF16)
    winm = P.sb([128, 8, QC], BF16)
    caus = P.sb([128, 4, QC], BF16)
    validW = P.sb([128, 256], F32)
    addW = P.sb([128, 256], F32)
    identf = P.sb([128, 128], F32)
    bsw = P.sb([128, NU, 4, 67], F32)
    bc = P.sb([128, NU, 4, 4, NQC], F32)
    w1d = [P.sb([64, 32, 64], BF16) for _ in range(2)]
    w1p = [P.sb([128, 16, 64], BF16) for _ in range(2)]
    w2b = [P.sb([64, 64], BF16) for _ in range(2)]
    pef = [P.sb([128, 16], BF16) for _ in range(2)]
    cb = P.sb([64, 2], F32)
    h1 = P.sb([64, 512], BF16)
    qbuf = [P.sb([65, 4, QC], BF16) for _ in range(2)]
    gt = [P.sb([128, 4, 12], F32) for _ in range(2)]
    acc = [[P.sb([128, 4, 256], F32) for _ in range(3 if dbg else 1)] for _ in range(2)]
    PTc = [P.sb([128, QC], BF16) for _ in range(4)]
    NPT = 12
    PT = [P.sb([128, QC], BF16) for _ in range(NPT)]
    PTm = [P.sb([128, QC], BF16) for _ in range(NPT)]
    msb = [P.sb([128, QC], BF16) for _ in range(3)]
    osb = [P.sb([65, QC], F32) for _ in range(2)]
    rl = [P.sb([128, 4], F32) for _ in range(2)]
    ff = [P.sb([128, 4], F32) for _ in range(2)]
    impacc = P.sb([128, 4, 128], F32)
    sc = P.sb([128, 128], F32)
    sc2 = P.sb([128, 128], F32)
    t8 = P.sb([128, 16], F32)
    selT = P.sb([128, QC], BF16)
    pso = [P.ps([128, 512], F32) for _ in range(4)]
    pss = PsumRot(P, 2, 'pss')
    psM = P.ps([128, 512], F32)
    pss3 = PsumRot(P, 0, 'pss3')
    pss3.tiles = pss.tiles + [psM]
    pss3.keys = pss.keys + ['psM']
    SK = 7
    psx = P.ps([128, 512], F32)
    if fused:
        vstage = P.sb([64, 2048], F32)
        gtf = [P.sb([12, QC], F32) for _ in range(2)]
        oT = [P.sb([128, QC], F32) for _ in range(2)]

    P.dma('pool', eexp[:], eexp_d, writes=['eexp'])
    P.dma('pool', cmpb[:], cmpmap_d, writes=['cmpb'])
    P.dma('pool', winm[:], winmask_d, writes=['winm'])
    P.dma('pool', caus[:], causal_d, writes=['caus'])
    P.dma('sp', validW[:], validW_d, writes=['validW'])
    P.dma('sp', addW[:], addW_d, writes=['addW'])
    P.dma('sp', identf[:], ident_d, writes=['identf'])
    for u in range(NU):
        P.dma('sp', bsw[:, u], bias_sw_d[u], writes=[('bsw', u)])
        P.dma('sp', bc[:, u], bias_c_d[u], writes=[('bc', u)])
    for i, (w1, w2, pe) in enumerate(((w1k, w2k, pek), (w1v, w2v, pev))):
        P.dma('pool', w1d[i][:], w1.rearrange("(l d) o -> d l o", d=64), writes=[('w1d', i)])
        P.dma('pool', w1p[i][:], w1.rearrange("(j p) o -> p j o", p=128), writes=[('w1p', i)])
        P.dma('pool', w2b[i][:], w2, writes=[('w2b', i)])
        P.dma('pool', pef[i][:], pe, writes=[('pef', i)])
    P.op('pool', lambda e: e.memset(ksA[64:65, :], 1.0), writes=['ksA1'])
    P.op('pool', lambda e: e.memset(kwA[64:65, :], 1.0), writes=['kwA1'])
    P.op('pool', lambda e: e.memset(kcA[64:65, :], 1.0), writes=['kcA1'])
    P.op('pool', lambda e: e.memset(vsA[:, :, 64:65], 1.0), writes=['vsA1'])
    P.op('pool', lambda e: e.memset(vwA[:, :, 64:65], 1.0), writes=['vwA1'])
    P.op('pool', lambda e: e.memset(vcA[:, :, 64:65], 1.0), writes=['vcA1'])
    P.op('pool', lambda e: e.memset(h1[:], 0.0), writes=['h1'])
    for i in range(2):
        for j in range(16):
            P.op('pe', lambda e, i=i, j=j: e.matmul(psx[0:64, i:i + 1], lhsT=w1p[i][:, j, :], rhs=pef[i][:, j:j + 1], start=(j == 0), stop=(j == 15)),
                 reads=[('w1p', i), ('pef', i)], writes=['psx'])
        P.op('act', lambda e, i=i: e.copy(out=cb[:, i:i + 1], in_=psx[0:64, i:i + 1]), reads=['psx'], writes=[('cb', i)])

    cnt = {'pt': 0, 'ptm': 0, 'msb': 0, 'osb': 0}

    def do_unit(u):
        P.dma('pool', zk[:], A['k_src'](u, 0), writes=['zk'])
        P.dma('pool', zv[:], A['vc_src'](u), writes=['zv'])
        P.dma('pool', ksA[0:64, :], A['k_src'](u, 1), writes=['ksA'])
        P.dma('pool', kwA[0:64, :], A['k_src'](u, 2), writes=['kwA'])
        if not fused:
            P.dma('pool', vsA[:, :, 0:64], A['vtok_src'](u, 0).rearrange("(kt p) d -> p kt d", p=128), writes=['vsA'])
            P.dma('pool', vwA[:, :, 0:64], A['vtok_src'](u, 1).rearrange("(kt p) d -> p kt d", p=128), writes=['vwA'])
        else:
            for i, (vA, vkey) in enumerate(((vsA, 'vsA'), (vwA, 'vwA'))):
                for pc in range(4):
                    P.dma('sp', vstage[:], A['vT_src'](u, i)[:, pc * 2048:(pc + 1) * 2048], writes=['vstage'])
                    for half in range(2):
                        for t8i in range(8):
                            c0 = half * 1024 + t8i * 128
                            P.op('pe', lambda e, t8i=t8i, c0=c0: e.transpose(psx[:, t8i * 64:(t8i + 1) * 64], vstage[0:64, c0:c0 + 128], identf[0:64, 0:64]),
                                 reads=['vstage', 'identf'], writes=['psx'])
                        kt0 = pc * 16 + half * 8
                        P.op('act', lambda e, vA=vA, kt0=kt0: e.copy(out=vA[:, kt0:kt0 + 8, 0:64], in_=psx[:].rearrange("p (a d) -> p a d", d=64)),
                             reads=['psx'], writes=[vkey])
        for slot in range(2):
            P.dma('pool', qbuf[slot][64:65, :, :], augrow[u], writes=[('qaug', slot)])
        for i, z in enumerate((zk, zv)):
            zkey = 'zk' if i == 0 else 'zv'
            pt, pk = pss.next()
            for l in range(32):
                P.op('pe', lambda e, pt=pt, i=i, l=l, z=z: e.matmul(pt[0:64, 0:NCMP], lhsT=w1d[i][:, l, :], rhs=z[:, l:l + 16 * (NCMP - 1) + 1:16],
                                                                  start=(l == 0), stop=(l == 31)),
                     reads=[zkey, ('w1d', i)], writes=[pk])
            P.op('act', lambda e, pt=pt, i=i: e.activation(out=h1[:, 0:NCMP], in_=pt[0:64, 0:NCMP], func=AF.Gelu_apprx_tanh, bias=cb[:, i:i + 1]),
                 reads=[pk, ('cb', i)], writes=['h1'])
            if i == 0:
                pt2, pk2 = pss.next()
                P.op('pe', lambda e, pt2=pt2: e.matmul(pt2[0:64, 0:NCMP], lhsT=w2b[0][:], rhs=h1[:, 0:NCMP], start=True, stop=True),
                     reads=['h1', ('w2b', 0)], writes=[pk2])
                P.op('act', lambda e, pt2=pt2: e.copy(out=kcA[0:64, 0:NCMP], in_=pt2[0:64, 0:NCMP]), reads=[pk2], writes=['kcA'])
            else:
                for kt in range(4):
                    nk = 127 if kt == 3 else 128
                    P.op('pe', lambda e, kt=kt, nk=nk: e.matmul(psx[0:nk, 0:64], lhsT=h1[:, kt * 128:kt * 128 + nk], rhs=w2b[1][:], start=True, stop=True),
                         reads=['h1', ('w2b', 1)], writes=['psx'])
                    P.op('act', lambda e, kt=kt, nk=nk: e.copy(out=vcA[0:nk, kt, 0:64], in_=psx[0:nk, 0:64]), reads=['psx'], writes=['vcA'])

    def do_chunk(u, qc):
        if True:
            q0 = qc * QC
            slot = qc % 2
            qa = qbuf[slot]
            P.dma('pool', qa[0:64, :, :], A['q_src'](u, q0), writes=[('qa', slot)])
            if not fused:
                P.dma('sp', gt[slot][:], A['gates_src'](u, q0).rearrange("(ts p) k -> p ts k", p=128), writes=[('gt', slot)])
            else:
                P.dma('sp', gtf[slot][:], A['gT_src'](u, q0), writes=[('gtf', slot)])
                for ts in range(4):
                    P.op('pe', lambda e, ts=ts: e.transpose(psx[:, ts * 12:(ts + 1) * 12], gtf[slot][0:12, ts * 128:(ts + 1) * 128], identf[0:12, 0:12]),
                         reads=[('gtf', slot), 'identf'], writes=['psx'])
                P.op('act', lambda e: e.copy(out=gt[slot][:], in_=psx[:, 0:48].rearrange("p (a k) -> p a k", k=12)), reads=['psx'], writes=[('gt', slot)])
            qkeys = [('qa', slot), ('qaug', slot)]

            def epilogue(h, br, with_imp=False):
                r = cnt['osb'] % 2
                cnt['osb'] += 1
                ob = osb[r]
                P.op('act', lambda e, ob=ob, h=h: e.copy(out=ob[:], in_=pso[h][0:65, :]), reads=[('pso', h)], writes=[('osb', r)])
                for ts in range(4):
                    P.op('pe', lambda e, ob=ob, ts=ts: e.transpose(psx[:, ts * 65:(ts + 1) * 65], ob[0:65, ts * 128:(ts + 1) * 128], identf[0:65, 0:65]),
                         reads=[('osb', r), 'identf'], writes=['psx'])
                rr, fr = rl[r], ff[r]
                P.op('dve', lambda e, rr=rr: e.tensor_scalar_max(out=rr[:], in0=psx[:, 64:64 + 65 * 4:65], scalar1=1e-30), reads=['psx'], writes=[('rl', r)])
                P.op('dve', lambda e, rr=rr: e.reciprocal(rr[:], rr[:]), reads=[('rl', r)], writes=[('rl', r)])
                gi = h * 3 + br
                P.op('dve', lambda e, rr=rr, fr=fr, gi=gi: e.tensor_tensor(out=fr[:], in0=rr[:], in1=gt[slot][:, :, gi], op=ALU.mult),
                     reads=[('rl', r), ('gt', slot)], writes=[('ff', r)])
                for ts in range(4):
                    dst = acc[slot][br if dbg else 0][:, ts, h * 64:(h + 1) * 64]
                    src = psx[:, ts * 65:ts * 65 + 64]
                    if br == 0 or dbg:
                        P.op('dve', lambda e, dst=dst, src=src, fr=fr, ts=ts: e.tensor_scalar(out=dst, in0=src, scalar1=fr[:, ts:ts + 1], scalar2=None, op0=ALU.mult),
                             reads=['psx', ('ff', r)], writes=[('acc', slot, br if dbg else 0, h)])
                    else:
                        P.op('dve', lambda e, dst=dst, src=src, fr=fr, ts=ts: e.scalar_tensor_tensor(out=dst, in0=src, scalar=fr[:, ts:ts + 1], in1=dst,
                                                                                                  op0=ALU.mult, op1=ALU.add),
                             reads=['psx', ('ff', r), ('acc', slot, 0, h)], writes=[('acc', slot, 0, h)])
                if with_imp:
                    for ts in range(4):
                        dst = impacc[:, ts, :]
                        src = psM[:, ts * 128:(ts + 1) * 128]
                        if h == 0:
                            P.op('dve', lambda e, dst=dst, src=src, rr=rr, ts=ts: e.tensor_scalar(out=dst, in0=src, scalar1=rr[:, ts:ts + 1], scalar2=None, op0=ALU.mult),
                                 reads=['psM', ('rl', r)], writes=['impacc'])
                        else:
                            P.op('dve', lambda e, dst=dst, src=src, rr=rr, ts=ts: e.scalar_tensor_tensor(out=dst, in0=src, scalar=rr[:, ts:ts + 1], in1=dst,
                                                                                                      op0=ALU.mult, op1=ALU.add),
                                 reads=['psM', ('rl', r), 'impacc'], writes=['impacc'])

            ktc = min(3, (32 * qc + 30) // 128)
            for h in range(4):
                k0h = 0
                while k0h < ktc and q0 - (16 * (128 * k0h + 127) + 31) > dcut(u, h):
                    k0h += 1
                for kt in range(k0h, ktc + 1):
                    nk = 127 if kt == 3 else 128
                    pt, pk = pss.next()
                    P.op('pe', lambda e, pt=pt, kt=kt, nk=nk, h=h: e.matmul(pt[0:nk, :], lhsT=kcA[0:65, kt * 128:kt * 128 + nk], rhs=qa[0:65, h, :], start=True, stop=True),
                         reads=['kcA', 'kcA1'] + qkeys, writes=[pk])
                    P.op('act', lambda e, pt=pt, kt=kt, nk=nk, h=h: e.activation(out=PTc[kt][0:nk, :], in_=pt[0:nk, :], func=AF.Exp,
                                                                              bias=bc[0:nk, u, h, kt, qc:qc + 1], scale=SCALE),
                         reads=[pk, ('bc', u)], writes=[('PTc', kt)])
                    if q0 - 16 * (128 * kt + nk - 1) - 31 < 0:
                        P.op('pool', lambda e, kt=kt, nk=nk: e.affine_select(out=PTc[kt][0:nk, :], in_=PTc[kt][0:nk, :], pattern=[[1, QC]],
                                                                           compare_op=ALU.is_ge, fill=0.0, base=q0 - 2048 * kt - 31, channel_multiplier=-16),
                             reads=[('PTc', kt)], writes=[('PTc', kt)])
                for kt in range(k0h, ktc + 1):
                    nk = 127 if kt == 3 else 128
                    P.op('pe', lambda e, kt=kt, nk=nk, h=h, k0h=k0h: e.matmul(pso[h][0:65, :], lhsT=vcA[0:nk, kt, 0:65], rhs=PTc[kt][0:nk, :],
                                                                  start=(kt == k0h), stop=(kt == ktc)),
                         reads=[('PTc', kt), 'vcA', 'vcA1'], writes=[('pso', h)])
                for ts in range(4):
                    for kt in range(k0h, ktc + 1):
                        nk = 127 if kt == 3 else 128
                        P.op('pe', lambda e, kt=kt, nk=nk, ts=ts, k0h=k0h: e.matmul(psM[:, ts * 128:(ts + 1) * 128], lhsT=PTc[kt][0:nk, ts * 128:(ts + 1) * 128],
                                                                        rhs=cmpb[0:nk, kt, :], start=(kt == k0h), stop=(kt == ktc)),
                             reads=[('PTc', kt), 'cmpb'], writes=['psM'])
                epilogue(h, 0, with_imp=True)

            for ts in range(4):
                w0 = 126 - 2 * (4 * qc + ts)
                P.op('dve', lambda e, ts=ts, w0=w0: e.tensor_tensor(out=sc[:], in0=impacc[:, ts, :], in1=validW[:, w0:w0 + 128], op=ALU.mult),
                     reads=['impacc', 'validW'], writes=['sc'])
                P.op('dve', lambda e, w0=w0: e.tensor_tensor(out=sc[:], in0=sc[:], in1=addW[:, w0:w0 + 128], op=ALU.add),
                     reads=['sc', 'addW'], writes=['sc'])
                P.op('dve', lambda e: e.tensor_scalar_add(out=sc[:, 0:1], in0=sc[:, 0:1], scalar1=1e4), reads=['sc'], writes=['sc'])
                P.op('dve', lambda e: e.max(t8[:, 0:8], sc[:]), reads=['sc'], writes=['t8'])
                P.op('dve', lambda e: e.match_replace(sc2[:], t8[:, 0:8], sc[:], -1e30), reads=['sc', 't8'], writes=['sc2'])
                P.op('dve', lambda e: e.max(t8[:, 8:16], sc2[:]), reads=['sc2'], writes=['t8'])
                P.op('dve', lambda e: e.tensor_scalar(out=sc2[:], in0=sc[:], scalar1=t8[:, 15:16], scalar2=None, op0=ALU.is_ge), reads=['sc', 't8'], writes=['sc2'])
                P.op('dve', lambda e, w0=w0: e.tensor_tensor(out=sc2[:], in0=sc2[:], in1=validW[:, w0:w0 + 128], op=ALU.mult), reads=['sc2', 'validW'], writes=['sc2'])
                P.op('pe', lambda e, ts=ts: e.transpose(psx[:, ts * 128:(ts + 1) * 128], sc2[:], identf[:]), reads=['sc2', 'identf'], writes=['psx'])
            P.op('act', lambda e: e.copy(out=selT[:], in_=psx[:]), reads=['psx'], writes=['selT'])

            kts = 4 * qc + 3

            def expand(kt):
                mi = cnt['msb'] % 3
                cnt['msb'] += 1
                mt_ = msb[mi]
                P.op('pe', lambda e, kt=kt: e.matmul(psx[:], lhsT=eexp[:, kt, :], rhs=selT[:], start=True, stop=True), reads=['eexp', 'selT'], writes=['psx'])
                P.op('act', lambda e, mt_=mt_: e.copy(out=mt_[:], in_=psx[:]), reads=['psx'], writes=[('msb', mi)])
                return mi
            pend = []

            def flush(n):
                while len(pend) > n:
                    (kt_, h_, pi_, vA_, vkeys_, first_, last_) = pend.pop(0)
                    P.op('pe', lambda e, pi_=pi_, kt_=kt_, h_=h_, vA_=vA_, first_=first_, last_=last_: e.matmul(
                        pso[h_][0:65, :], lhsT=vA_[:, kt_, 0:65], rhs=PTm[pi_][:], start=first_, stop=last_),
                        reads=[('PTm', pi_)] + vkeys_, writes=[('pso', h_)])
            kmin = [0] * 4
            for h in range(4):
                while kmin[h] < kts and q0 - (128 * kmin[h] + 127) > dcut(u, h):
                    kmin[h] += 1
            ktlo = min(kmin)
            mi_next = expand(ktlo)
            for kt in range(ktlo, kts + 1):
                dl = 128 * kt - q0
                mi = mi_next
                mt_ = msb[mi]
                hs = [h for h in range(4) if kt >= kmin[h]]
                for h in hs:
                    pt, pk = pss3.next()
                    P.op('pe', lambda e, pt=pt, kt=kt, h=h: e.matmul(pt[:], lhsT=ksA[0:65, kt * 128:(kt + 1) * 128], rhs=qa[0:65, h, :], start=True, stop=True),
                         reads=['ksA', 'ksA1'] + qkeys, writes=[pk])
                    if h == hs[0] and kt < kts:
                        mi_next = expand(kt + 1)
                    pi = cnt['pt'] % NPT
                    cnt['pt'] += 1
                    P.op('act', lambda e, pt=pt, pi=pi, h=h, dl=dl: e.activation(out=PT[pi][:], in_=pt[:], func=AF.Exp, bias=bsw[:, u, h, dl // 128 + 63:dl // 128 + 64], scale=SCALE),
                         reads=[pk, ('bsw', u)], writes=[('PT', pi)])
                    if dl >= 0:
                        P.op('pool', lambda e, pi=pi, dl=dl: e.affine_select(out=PT[pi][:], in_=PT[pi][:], pattern=[[1, QC]], compare_op=ALU.is_ge,
                                                                           fill=0.0, base=-dl, channel_multiplier=-1),
                             reads=[('PT', pi)], writes=[('PT', pi)])
                        eng = 'dve'
                    else:
                        eng = 'dve' if h % 2 == 0 else 'pool'
                    P.op(eng, lambda e, pi=pi, mt_=mt_: e.tensor_tensor(out=PTm[pi][:], in0=PT[pi][:], in1=mt_[:], op=ALU.mult),
                         reads=[('PT', pi), ('msb', mi)], writes=[('PTm', pi)])
                    pend.append((kt, h, pi, vsA, ['vsA', 'vsA1'], kt == kmin[h], kt == kts))
                    flush(SK)
            flush(0)
            for h in range(4):
                epilogue(h, 1)

            kt0w = max(0, 4 * qc - 4)
            kminw = [max(kt0w, kmin[h]) for h in range(4)]
            for kt in range(kt0w, kts + 1):
                dl = 128 * kt - q0
                for h in range(4):
                    if kt < kminw[h]:
                        continue
                    pt, pk = pss3.next()
                    P.op('pe', lambda e, pt=pt, kt=kt, h=h: e.matmul(pt[:], lhsT=kwA[0:65, kt * 128:(kt + 1) * 128], rhs=qa[0:65, h, :], start=True, stop=True),
                         reads=['kwA', 'kwA1'] + qkeys, writes=[pk])
                    pi = cnt['pt'] % NPT
                    cnt['pt'] += 1
                    P.op('act', lambda e, pt=pt, pi=pi, h=h, dl=dl: e.activation(out=PT[pi][:], in_=pt[:], func=AF.Exp, bias=bsw[:, u, h, dl // 128 + 63:dl // 128 + 64], scale=SCALE),
                         reads=[pk, ('bsw', u)], writes=[('PT', pi)])
                    eng = 'pool' if h % 2 == 0 else 'pool'
                    if dl >= 0:
                        P.op(eng, lambda e, pi=pi, dl=dl: e.affine_select(out=PTm[pi][:], in_=PT[pi][:], pattern=[[1, QC]], compare_op=ALU.is_ge,
                                                                        fill=0.0, base=-dl, channel_multiplier=-1),
                             reads=[('PT', pi)], writes=[('PTm', pi)])
                    else:
                        P.op(eng, lambda e, pi=pi, dl=dl: e.affine_select(out=PTm[pi][:], in_=PT[pi][:], pattern=[[-1, QC]], compare_op=ALU.is_ge,
                                                                        fill=0.0, base=dl + 511, channel_multiplier=1),
                             reads=[('PT', pi)], writes=[('PTm', pi)])
                    pend.append((kt, h, pi, vwA, ['vwA', 'vwA1'], kt == kminw[h], kt == kts))
                    flush(SK)
            flush(0)
            for h in range(4):
                epilogue(h, 2)
            for br in range(3 if dbg else 1):
                if not fused:
                    dst = A['o_dst'](u, br, q0).rearrange("(ts p) f -> p ts f", p=128)
                    P.dma('sp', dst, acc[slot][br][:], reads=[('acc', slot, br, h) for h in range(4)],
                          writes=[('o', u, qc, br)], semkey=('osem', slot, br))
                else:
                    for fb in range(2):
                        for ts in range(4):
                            P.op('pe', lambda e, ts=ts, fb=fb, br=br: e.transpose(psx[:, ts * 128:(ts + 1) * 128], acc[slot][br][:, ts, fb * 128:(fb + 1) * 128], identf[:]),
                                 reads=[('acc', slot, br, 2 * fb), ('acc', slot, br, 2 * fb + 1), 'identf'], writes=['psx'])
                        P.op('act', lambda e, fb=fb: e.copy(out=oT[fb][:], in_=psx[:]), reads=['psx'], writes=[('oT', fb)])
                        P.dma('sp', A['oT_dst'](u, fb, q0), oT[fb][:], reads=[('oT', fb)], writes=[('o', u, qc, fb)], semkey=('osem', fb))

    for u in range(nunits):
        do_unit(u)
        for qc in range(nqc):
            do_chunk(u, qc)
    P.phase_end()


def run_att(projs, w1k, w2k, w1v, w2v, pe_k, pe_v):
    nc = _prog(('att',), build_att_prog)
    consts = att_consts()
    maps = []
    for c in range(NCORES):
        b, hh = c // 2, c % 2
        full = np.concatenate([projs[2 * b], projs[2 * b + 1]], axis=1)
        qs, ks, vcs, vts, gs, augs, bsws, bcs = [], [], [], [], [], [], [], []
        for u in range(2):
            g = 2 * hh + u
            qs.append(full[g * 256:(g + 1) * 256].reshape(4, 64, S))
            kv = lambda i: full[1024 + i * 256 + g * 64:1024 + i * 256 + (g + 1) * 64]
            ks.append(np.stack([kv(0), kv(2), kv(4)]))
            vcs.append(kv(1))
            vts.append(np.stack([kv(3).T, kv(5).T]))
            gs.append(full[2560 + g * 12:2560 + (g + 1) * 12].T)
            aug, bsw, bc = att_tables(g)
            augs.append(aug[None])
            bsws.append(bsw)
            bcs.append(bc)
        m = {"qT": np.ascontiguousarray(np.stack(qs)), "kT": np.ascontiguousarray(np.stack(ks)), "vcT": np.ascontiguousarray(np.stack(vcs)),
             "vtok": np.ascontiguousarray(np.stack(vts)), "gates": np.ascontiguousarray(np.stack(gs)),
             "w1k": w1k, "w2k": w2k, "w1v": w1v, "w2v": w2v,
             "pek": np.ascontiguousarray(pe_k.reshape(16, 128).T), "pev": np.ascontiguousarray(pe_v.reshape(16, 128).T),
             "augrow": np.ascontiguousarray(np.stack(augs)), "bias_sw": np.ascontiguousarray(np.stack(bsws)), "bias_c": np.ascontiguousarray(np.stack(bcs))}
        m.update(consts)
        maps.append(m)
    res = _run(nc, maps)
    mixs = []
    for c in range(NCORES):
        b, hh = c // 2, c % 2
        parts = []
        for g in range(4):
            src = res[2 * b + g // 2]["o"][g % 2]
            parts.append(src[hh * TPC:(hh + 1) * TPC].T)
        mixs.append(np.ascontiguousarray(np.concatenate(parts, axis=0)))
    return mixs


def kernel_unfused(x, norm_mix, norm_ffn, norm_final,
           rg_w_in, rg_conv_w, rg_conv_b, rg_w_a, rg_b_a, rg_w_x, rg_b_x, rg_lambda, rg_w_out,
           nsa_w_in, nsa_b_gate, nsa_pe_k, nsa_pe_v, nsa_w1_k, nsa_w2_k, nsa_w1_v, nsa_w2_v, nsa_w_out,
           mlp_w_up, mlp_w_down):
    f = lambda a: np.ascontiguousarray(np.asarray(a, dtype=np.float32))
    x = f(x)
    xls = [to_xl(x[c // 2, (c % 2) * TPC:(c % 2 + 1) * TPC]) for c in range(NCORES)]
    r = run_token(xls, False, False, 'rg', g_nxt=f(norm_mix[0]), w_in=f(rg_w_in[0]))
    projs = [r[c]['proj'] for c in range(NCORES)]
    for i in range(4):
        j = i // 2
        if i % 2 == 0:
            mixs = run_scan(projs, f(rg_conv_w[j]), f(rg_conv_b[j]), f(rg_w_a[j]), f(rg_b_a[j]), f(rg_w_x[j]), f(rg_b_x[j]), f(rg_lambda[j]))
            w_out = f(rg_w_out[j])
        else:
            mixs = run_att(projs, f(nsa_w1_k[j]), f(nsa_w2_k[j]), f(nsa_w1_v[j]), f(nsa_w2_v[j]), f(nsa_pe_k[j]), f(nsa_pe_v[j]))
            w_out = f(nsa_w_out[j])
        if i == 3:
            r = run_token(xls, True, True, 'final', mixs=mixs, w_out=w_out, g_ffn=f(norm_ffn[i]), w_up=f(mlp_w_up[i]), w_down=f(mlp_w_down[i]),
                          g_nxt=f(norm_final))
            break
        if i % 2 == 0:
            r = run_token(xls, True, True, 'nsa', mixs=mixs, w_out=w_out, g_ffn=f(norm_ffn[i]), w_up=f(mlp_w_up[i]), w_down=f(mlp_w_down[i]),
                          g_nxt=f(norm_mix[i + 1]), w_in=f(nsa_w_in[(i + 1) // 2]), b_gate=f(nsa_b_gate[(i + 1) // 2]))
        else:
            r = run_token(xls, True, True, 'rg', mixs=mixs, w_out=w_out, g_ffn=f(norm_ffn[i]), w_up=f(mlp_w_up[i]), w_down=f(mlp_w_down[i]),
                          g_nxt=f(norm_mix[i + 1]), w_in=f(rg_w_in[(i + 1) // 2]))
        xls = [r[c]['x_out'] for c in range(NCORES)]
        projs = [r[c]['proj'] for c in range(NCORES)]
    out = np.empty((B, S, D), np.float32)
    for c in range(NCORES):
        out[c // 2, (c % 2) * TPC:(c % 2 + 1) * TPC] = from_xl(r[c]['y_out'])
    return out


NBLK_F = S // TB


def build_fused_prog():
    P = Prog()
    x_in = P.dram("x_in", [NBLK_F, 128, 8, TB], F32, "ExternalInput")
    y_out = P.dram("y_out", [NBLK_F, 128, 8, TB], F32, "ExternalOutput")
    gmix = P.dram("gmix", [4, 128, 8], F32, "ExternalInput")
    gffn = P.dram("gffn", [4, 128, 8], F32, "ExternalInput")
    gfin = P.dram("gfin", [128, 8], F32, "ExternalInput")
    rg_w_in = P.dram("rg_w_in", [2, D, 2048], F32, "ExternalInput")
    rg_cpar = P.dram("rg_cpar", [2, 4, 128, 2, 8], F32, "ExternalInput")
    rg_w_a = P.dram("rg_w_a", [2, 4, 256, 256], F32, "ExternalInput")
    rg_w_x = P.dram("rg_w_x", [2, 4, 256, 256], F32, "ExternalInput")
    rg_w_out = P.dram("rg_w_out", [2, D, D], F32, "ExternalInput")
    nsa_w_in = P.dram("nsa_w_in", [2, D, NSA_COLS], F32, "ExternalInput")
    nsa_b_gate = P.dram("nsa_b_gate", [2, 48, 1], F32, "ExternalInput")
    nsa_pek = P.dram("nsa_pek", [2, 128, 16], F32, "ExternalInput")
    nsa_pev = P.dram("nsa_pev", [2, 128, 16], F32, "ExternalInput")
    nsa_w1_k = P.dram("nsa_w1_k", [2, 2048, 64], F32, "ExternalInput")
    nsa_w2_k = P.dram("nsa_w2_k", [2, 64, 64], F32, "ExternalInput")
    nsa_w1_v = P.dram("nsa_w1_v", [2, 2048, 64], F32, "ExternalInput")
    nsa_w2_v = P.dram("nsa_w2_v", [2, 64, 64], F32, "ExternalInput")
    nsa_w_out = P.dram("nsa_w_out", [2, D, D], F32, "ExternalInput")
    mlp_w_up = P.dram("mlp_w_up", [4, D, DFF], F32, "ExternalInput")
    mlp_w_down = P.dram("mlp_w_down", [4, DFF, D], F32, "ExternalInput")
    T = {}
    for nm, shp in (("augrow", [4, 1, 4, QC]), ("bias_sw", [4, 128, 4, 67]), ("bias_c", [4, 128, 4, 4, NQC]), ("cmpmap", [128, 4, 128]),
                    ("eexp", [128, 64, 128]), ("winmask", [128, 8, QC]), ("causal", [128, 4, QC]), ("validW", [128, 256]), ("addW", [128, 256]),
                    ("ident", [128, 128])):
        T[nm] = P.dram(nm, shp, F32, "ExternalInput")
    X = P.dram("X_scr", [NBLK_F, 128, 8, TB], F32, "Internal")
    PROJ = P.dram("PROJ_scr", [NSA_COLS, S], F32, "Internal")
    MIX = P.dram("MIX_scr", [D, S], F32, "Internal")

    def scan_phase(j):
        emit_scan(P, 4,
                  lambda u, jj, a, b: PROJ[u * 256 + jj * 128:u * 256 + (jj + 1) * 128, a:b],
                  lambda u, jj, a, b: PROJ[1024 + u * 256 + jj * 128:1024 + u * 256 + (jj + 1) * 128, a:b],
                  rg_cpar[j], rg_w_a[j], rg_w_x[j],
                  lambda u, jj, a, b: MIX[u * 256 + jj * 128:u * 256 + (jj + 1) * 128, a:b])

    def att_phase(j):
        A = dict(T)
        A.update(w1k=nsa_w1_k[j], w2k=nsa_w2_k[j], w1v=nsa_w1_v[j], w2v=nsa_w2_v[j], pek=nsa_pek[j], pev=nsa_pev[j])
        A['q_src'] = lambda u, q0: PROJ[u * 256:(u + 1) * 256, q0:q0 + QC].rearrange("(h d) t -> d h t", d=64)
        A['k_src'] = lambda u, i: PROJ[1024 + (2 * i) * 256 + u * 64:1024 + (2 * i) * 256 + (u + 1) * 64, :]
        A['vc_src'] = lambda u: PROJ[1024 + 256 + u * 64:1024 + 256 + (u + 1) * 64, :]
        A['vT_src'] = lambda u, i: PROJ[1024 + (3 + 2 * i) * 256 + u * 64:1024 + (3 + 2 * i) * 256 + (u + 1) * 64, :]
        A['gT_src'] = lambda u, q0: PROJ[2560 + u * 12:2560 + (u + 1) * 12, q0:q0 + QC]
        A['oT_dst'] = lambda u, fb, q0: MIX[u * 256 + fb * 128:u * 256 + (fb + 1) * 128, q0:q0 + QC]
        emit_att(P, A, dbg=False, nunits=4, nqc=NQC, fused=True, groups=[0, 1, 2, 3])

    emit_token(P, NBLK_F, x_in, x_in, False, False, 'rg', dict(g_nxt=gmix[0], w_in=rg_w_in[0], proj=PROJ))
    xsrc = x_in
    for i in range(4):
        j = i // 2
        if i % 2 == 0:
            scan_phase(j)
            w_out = rg_w_out[j]
        else:
            att_phase(j)
            w_out = nsa_w_out[j]
        A = dict(mix=MIX, w_out=w_out, g_ffn=gffn[i], w_up=mlp_w_up[i], w_down=mlp_w_down[i])
        if i == 3:
            A.update(g_nxt=gfin, y_out=y_out)
            emit_token(P, NBLK_F, xsrc, X, True, True, 'final', A)
        elif i % 2 == 0:
            A.update(g_nxt=gmix[i + 1], w_in=nsa_w_in[(i + 1) // 2], b_gate=nsa_b_gate[(i + 1) // 2], proj=PROJ)
            emit_token(P, NBLK_F, xsrc, X, True, True, 'nsa', A)
        else:
            A.update(g_nxt=gmix[i + 1], w_in=rg_w_in[(i + 1) // 2], proj=PROJ)
            emit_token(P, NBLK_F, xsrc, X, True, True, 'rg', A)
        xsrc = X
    return P.build()


def kernel(x, norm_mix, norm_ffn, norm_final,
                 rg_w_in, rg_conv_w, rg_conv_b, rg_w_a, rg_b_a, rg_w_x, rg_b_x, rg_lambda, rg_w_out,
                 nsa_w_in, nsa_b_gate, nsa_pe_k, nsa_pe_v, nsa_w1_k, nsa_w2_k, nsa_w1_v, nsa_w2_v, nsa_w_out,
                 mlp_w_up, mlp_w_down):
    f = lambda a: np.ascontiguousarray(np.asarray(a, dtype=np.float32))
    x = f(x)
    nc = _prog(('fused',), build_fused_prog)
    common = {
        "gmix": np.stack([gvec(f(norm_mix[i])) for i in range(4)]),
        "gffn": np.stack([gvec(f(norm_ffn[i])) for i in range(4)]),
        "gfin": gvec(f(norm_final)),
        "rg_w_in": f(rg_w_in), "rg_w_a": f(rg_w_a), "rg_w_x": f(rg_w_x), "rg_w_out": f(rg_w_out),
        "nsa_w_in": f(nsa_w_in), "nsa_b_gate": f(nsa_b_gate).reshape(2, 48, 1),
        "nsa_pek": np.ascontiguousarray(f(nsa_pe_k).reshape(2, 16, 128).transpose(0, 2, 1)),
        "nsa_pev": np.ascontiguousarray(f(nsa_pe_v).reshape(2, 16, 128).transpose(0, 2, 1)),
        "nsa_w1_k": f(nsa_w1_k), "nsa_w2_k": f(nsa_w2_k), "nsa_w1_v": f(nsa_w1_v), "nsa_w2_v": f(nsa_w2_v), "nsa_w_out": f(nsa_w_out),
        "mlp_w_up": f(mlp_w_up), "mlp_w_down": f(mlp_w_down),
    }
    cps = []
    for j in range(2):
        par = np.stack([f(rg_conv_w[j])[0], f(rg_conv_w[j])[1], f(rg_conv_w[j])[2], f(rg_conv_w[j])[3], f(rg_conv_b[j]), f(rg_b_a[j]), f(rg_b_x[j]),
                        f(rg_lambda[j])], axis=-1)
        cps.append(par.reshape(4, 2, 128, 8).transpose(0, 2, 1, 3))
    common["rg_cpar"] = np.ascontiguousarray(np.stack(cps))
    tabs = [att_tables(g) for g in range(4)]
    common["augrow"] = np.ascontiguousarray(np.stack([t[0][None] for t in tabs]))
    common["bias_sw"] = np.ascontiguousarray(np.stack([t[1] for t in tabs]))
    common["bias_c"] = np.ascontiguousarray(np.stack([t[2] for t in tabs]))
    common.update(att_consts())
    maps = []
    for c in range(NCORES):
        m = dict(common)
        xb = x[c % B]
        m["x_in"] = np.ascontiguousarray(xb.reshape(NBLK_F, TB, 8, 128).transpose(0, 3, 2, 1))
        maps.append(m)
    res = _run(nc, maps)
    out = np.empty((B, S, D), np.float32)
    for b in range(B):
        out[b] = res[b]["y_out"].transpose(0, 3, 2, 1).reshape(S, D)
    return out
```

```python
import numpy as np
from contextlib import ExitStack
import concourse.bass as bass
import concourse.mybir as mybir
from concourse.bass_utils import run_bass_kernel_spmd

F32 = mybir.dt.float32
BF16 = mybir.dt.bfloat16
AF = mybir.ActivationFunctionType
ALU = mybir.AluOpType

ENGS = ('sp', 'pe', 'act', 'dve', 'pool')
SAME_ENGINE_SYNC = True

D = 1024
S = 8192
B = 4
NCORES = 8
TPC = 4096
TB = 256
NBLK = TPC // TB
DFF = 4096
EPS = 1e-6
NSA_COLS = 2608


class Prog:
    def __init__(self):
        self.nc = bass.Bass("TRN2", target_bir_lowering=False)
        self.es = ExitStack()
        self.ops = {e: [] for e in ENGS}
        self.cnt = {e: 0 for e in ENGS}
        self.lastw = {}
        self.readers = {}
        self.waited = {e: {} for e in ENGS}
        self.dmasem = {}
        self.semnames = ['c_' + e for e in ENGS if e != 'sp']
        self.nuniq = 0
        self.floor = {}
        self.pes = None

    def phase_begin(self):
        self.pes = ExitStack()

    def phase_end(self):
        for e in ENGS:
            if e != 'sp' and self.cnt[e] > 0:
                self.floor['c_' + e] = self.cnt[e]
        for name, c in self.dmasem.values():
            if c > 0:
                self.floor[name] = c
        self.pes.close()
        self.pes = None

    def sb(self, shape, dt, name=None):
        self.nuniq += 1
        return (self.pes or self.es).enter_context(self.nc.sbuf_tensor(name or f"sb{self.nuniq}", list(shape), dt))

    def ps(self, shape, dt, name=None):
        self.nuniq += 1
        return (self.pes or self.es).enter_context(self.nc.psum_tensor(name or f"ps{self.nuniq}", list(shape), dt))

    def dram(self, name, shape, dt, kind):
        return self.nc.dram_tensor(name, list(shape), dt, kind=kind).ap()

    def _deps(self, reads, writes):
        deps = {}

        def add(s, v):
            if deps.get(s, 0) < v:
                deps[s] = v
        for k in reads:
            t = self.lastw.get(k)
            if t is not None:
                add(*t)
        for k in writes:
            t = self.lastw.get(k)
            if t is not None:
                add(*t)
            for s, v in self.readers.get(k, {}).items():
                add(s, v)
        return deps

    def _commit(self, tok, reads, writes):
        s, v = tok
        for k in reads:
            r = self.readers.setdefault(k, {})
            if r.get(s, 0) < v:
                r[s] = v
        for k in writes:
            self.lastw[k] = tok
            self.readers[k] = {}

    def _waits(self, eng, deps):
        ws = []
        own = 'c_' + eng
        for s, v in self.floor.items():
            if deps.get(s, 0) < v:
                deps[s] = v
        for s, v in deps.items():
            if s == own and (eng == 'pe' or not SAME_ENGINE_SYNC):
                continue
            if self.waited[eng].get(s, 0) >= v:
                continue
            self.waited[eng][s] = v
            ws.append((s, v))
        return ws

    def op(self, eng, fn, reads=(), writes=()):
        deps = self._deps(reads, writes)
        ws = self._waits(eng, deps)
        self.cnt[eng] += 1
        tok = ('c_' + eng, self.cnt[eng])
        self.ops[eng].append((ws, fn, (tok[0], 1)))
        self._commit(tok, reads, writes)
        return tok

    def dma(self, q, out, in_, reads=(), writes=(), semkey=None):
        semkey = semkey if semkey is not None else writes[0]
        if semkey not in self.dmasem:
            name = f"d{len(self.dmasem)}"
            self.dmasem[semkey] = [name, 0]
            self.semnames.append(name)
        ent = self.dmasem[semkey]
        deps = self._deps(reads, writes)
        ws = self._waits(q, deps)
        ent[1] += 16
        tok = (ent[0], ent[1])
        self.ops[q].append((ws, lambda e: e.dma_start(out=out, in_=in_), (ent[0], 16)))
        self._commit(tok, reads, writes)
        return tok

    def build(self):
        nc = self.nc
        final = {}
        for e in ENGS:
            if e != 'sp' and self.cnt[e] > 0:
                final['c_' + e] = self.cnt[e]
        for name, c in self.dmasem.values():
            final[name] = c
        fws = list(final.items())
        with ExitStack() as es:
            sems = {n: es.enter_context(nc.semaphore(n)) for n in self.semnames}
            ops = self.ops

            def replay(name, e):
                for ws, fn, inc in ops[name]:
                    for s, v in ws:
                        e.wait_ge(sems[s], v)
                    fn(e).then_inc(sems[inc[0]], inc[1])
                if name == 'sp':
                    for s, v in fws:
                        e.wait_ge(sems[s], v)
            with nc.Block() as block:
                @block.sync
                def _(e):
                    replay('sp', e)

                @block.tensor
                def _(e):
                    replay('pe', e)

                @block.scalar
                def _(e):
                    replay('act', e)

                @block.vector
                def _(e):
                    replay('dve', e)

                @block.gpsimd
                def _(e):
                    replay('pool', e)
        self.es.close()
        return nc


class PsumRot:
    def __init__(self, P, n, prefix):
        self.tiles = [P.ps([128, 512], F32) for _ in range(n)]
        self.keys = [(prefix, i) for i in range(n)]
        self.i = 0

    def next(self):
        t, k = self.tiles[self.i], self.keys[self.i]
        self.i = (self.i + 1) % len(self.tiles)
        return t, k


def build_token_prog(has_post, has_mlp, nxt):
    P = Prog()
    A = {}
    x_in = P.dram("x_in", [NBLK, 128, 8, TB], F32, "ExternalInput")
    write_x = has_post or has_mlp
    if write_x and nxt != 'final':
        x_out = P.dram("x_out", [NBLK, 128, 8, TB], F32, "ExternalOutput")
    elif write_x:
        x_out = P.dram("x_scr", [NBLK, 128, 8, TB], F32, "Internal")
    else:
        x_out = x_in
    if has_post:
        A['mix'] = P.dram("mix", [D, TPC], F32, "ExternalInput")
        A['w_out'] = P.dram("w_out", [D, D], F32, "ExternalInput")
    if has_mlp:
        A['g_ffn'] = P.dram("g_ffn", [128, 8], F32, "ExternalInput")
        A['w_up'] = P.dram("w_up", [D, DFF], F32, "ExternalInput")
        A['w_down'] = P.dram("w_down", [DFF, D], F32, "ExternalInput")
    A['g_nxt'] = P.dram("g_nxt", [128, 8], F32, "ExternalInput")
    if nxt == 'nsa':
        A['b_gate'] = P.dram("b_gate", [48, 1], F32, "ExternalInput")
    if nxt != 'final':
        ncols = 2048 if nxt == 'rg' else NSA_COLS
        A['w_in'] = P.dram("w_in", [D, ncols], F32, "ExternalInput")
        A['proj'] = P.dram("proj", [ncols, TPC], F32, "ExternalOutput")
    else:
        A['y_out'] = P.dram("y_out", [NBLK, 128, 8, TB], F32, "ExternalOutput")
    emit_token(P, NBLK, x_in, x_out, has_post, has_mlp, nxt, A)
    return P.build()


def emit_token(P, NBLK, x_in, x_out, has_post, has_mlp, nxt, A):
    write_x = has_post or has_mlp
    mix, w_out = A.get('mix'), A.get('w_out')
    g_ffn, w_up, w_down = A.get('g_ffn'), A.get('w_up'), A.get('w_down')
    g_nxt, b_gate, w_in, proj, y_out = A.get('g_nxt'), A.get('b_gate'), A.get('w_in'), A.get('proj'), A.get('y_out')
    ncols = 2048 if nxt == 'rg' else NSA_COLS
    P.phase_begin()
    WA = P.sb([128, 8 * DFF], BF16)
    WB = P.sb([128, 32 * D], BF16)
    WO = P.sb([128, 8 * D], BF16)
    xbuf = [P.sb([128, 8, TB], F32) for _ in range(2)]
    mbuf = [P.sb([128, 8, TB], BF16) for _ in range(2)]
    hb = P.sb([128, 8, TB], BF16)
    sq = P.sb([128, 8, TB], BF16)
    rs = P.sb([128, TB], F32)
    hb2 = P.sb([128, 8, TB], BF16)
    rs2 = P.sb([128, TB], F32)
    u2 = P.sb([128, 32, TB], BF16)
    rt = [P.sb([128, TB], F32) for _ in range(2)]
    st = [P.sb([128, TB], F32) for _ in range(4)]
    ones_bf = P.sb([128, 128], BF16)
    gf = P.sb([128, 8], F32)
    gn = P.sb([128, 8], F32)
    P.eps_tile = P.sb([128, 1], F32)
    bg = P.sb([48, 1], F32)
    psr = PsumRot(P, 6, 'ps')
    psn = P.ps([128, 512], F32)

    P.op('pool', lambda e: e.memset(ones_bf[:], 1.0), writes=['ones'])
    P.op('pool', lambda e: e.memset(P.eps_tile[:], EPS), writes=['epsc'])
    if has_mlp:
        P.dma('sp', gf[:], g_ffn, writes=['gvec_f'])
    P.dma('sp', gn[:], g_nxt, writes=['gvec_n'])
    if nxt == 'nsa':
        P.dma('sp', bg[:], b_gate, writes=['bg'])

    WAv = WA[:].rearrange("p (c f) -> p c f", c=8)
    WBv = WB[:].rearrange("p (c f) -> p c f", c=32)
    WOv = WO[:].rearrange("p (c f) -> p c f", c=8)
    if has_post:
        wsrc = w_out.rearrange("(c p) f -> p c f", p=128)
        for c in range(0, 8, 4):
            P.dma('pool', WOv[:, c:c + 4, :], wsrc[:, c:c + 4, :], writes=[('WO', c)])
        wo_keys = [('WO', 0), ('WO', 4)]
    if has_mlp:
        wsrc = w_up.rearrange("(c p) f -> p c f", p=128)
        for c in range(8):
            P.dma('pool', WAv[:, c, :], wsrc[:, c, :], writes=[('WA', c)])
        wsrc = w_down.rearrange("(c p) f -> p c f", p=128)
        for c in range(0, 32, 4):
            P.dma('pool', WBv[:, c:c + 4, :], wsrc[:, c:c + 4, :], writes=[('WB', c)])

    mixv = mix.rearrange("(c p) t -> p c t", p=128) if has_post else None

    def xkeys(slot):
        return [('xb', slot, c) for c in range(8)]

    if write_x:
        hbs = [hb, hb2]
        sqs = [sq, sq]
        rss = [rs, rs2]

        def load(blk):
            sl = blk % 2
            P.dma('sp', xbuf[sl][:], x_in[blk], writes=xkeys(sl), semkey=('xbsem', sl))
            if has_post:
                P.dma('pool', mbuf[sl][:], mixv[:, :, blk * TB:(blk + 1) * TB], writes=[('mb', sl)])

        def stageA(blk):
            slot = blk % 2
            xb = xbuf[slot]
            xk = xkeys(slot)
            if has_post:
                mb = mbuf[slot]
                for cc in range(8):
                    pt, pk = psr.next()
                    for kc in range(8):
                        P.op('pe', lambda e, pt=pt, kc=kc, cc=cc, mb=mb: e.matmul(
                            pt[:, :TB], lhsT=WOv[:, kc, cc * 128:(cc + 1) * 128], rhs=mb[:, kc, :], start=(kc == 0), stop=(kc == 7)),
                            reads=[('mb', slot)] + wo_keys, writes=[pk])
                    P.op('dve', lambda e, pt=pt, cc=cc, xb=xb: e.tensor_tensor(out=xb[:, cc, :], in0=pt[:, :TB], in1=xb[:, cc, :], op=ALU.add),
                         reads=[pk, xk[cc]], writes=[xk[cc]])
            if has_mlp:
                hk = [('hb', slot, c) for c in range(8)]
                emit_norm_k(P, xb, xk, gf, 'gvec_f', hbs[slot], hk, ones_bf, sqs[slot], psn, rss[slot], sfx=slot, sqk=0)

        def stageB(blk):
            slot = blk % 2
            hk = [('hb', slot, c) for c in range(8)]
            hcur = hbs[slot]
            for fc in range(32):
                pt, pk = psr.next()
                for kc in range(8):
                    P.op('pe', lambda e, pt=pt, kc=kc, fc=fc, hcur=hcur: e.matmul(
                        pt[:, :TB], lhsT=WAv[:, kc, fc * 128:(fc + 1) * 128], rhs=hcur[:, kc, :], start=(kc == 0), stop=(kc == 7)),
                        reads=[hk[kc], ('WA', kc)], writes=[pk])
                r = rt[fc % 2]
                P.op('act', lambda e, pt=pt, r=r: e.activation(out=r[:], in_=pt[:, :TB], func=AF.Relu),
                     reads=[pk], writes=[('rt', fc % 2)])
                eng = 'dve' if fc % 2 == 0 else 'pool'
                P.op(eng, lambda e, r=r, fc=fc: e.tensor_tensor(out=u2[:, fc, :], in0=r[:], in1=r[:], op=ALU.mult),
                     reads=[('rt', fc % 2)], writes=[('u2', fc)])

        def stageC(blk):
            slot = blk % 2
            xb = xbuf[slot]
            xk = xkeys(slot)
            if has_mlp:
                for cc in range(8):
                    pt, pk = psr.next()
                    for fc in range(32):
                        P.op('pe', lambda e, pt=pt, fc=fc, cc=cc: e.matmul(
                            pt[:, :TB], lhsT=WBv[:, fc, cc * 128:(cc + 1) * 128], rhs=u2[:, fc, :], start=(fc == 0), stop=(fc == 31)),
                            reads=[('u2', fc), ('WB', (fc // 4) * 4)], writes=[pk])
                    P.op('dve', lambda e, pt=pt, cc=cc, xb=xb: e.tensor_tensor(out=xb[:, cc, :], in0=pt[:, :TB], in1=xb[:, cc, :], op=ALU.add),
                         reads=[pk, xk[cc]], writes=[xk[cc]])
            P.dma('sp', x_out[blk], xb[:], reads=xk, writes=[('xout', blk)], semkey=('xosem', slot))

        load(0)
        if NBLK > 1:
            load(1)
        stageA(0)
        for blk in range(NBLK):
            if has_mlp:
                stageB(blk)
            if blk + 1 < NBLK:
                stageA(blk + 1)
            stageC(blk)
            if blk + 2 < NBLK:
                load(blk + 2)

    if nxt != 'final':
        nch = (ncols + 127) // 128
        WIv = WA[:, 0:8 * ncols].rearrange("p (c f) -> p c f", c=8)
        wsrc = w_in.rearrange("(c p) f -> p c f", p=128)
        for c in range(8):
            P.dma('pool', WIv[:, c, :], wsrc[:, c, :], writes=[('WA', c)])
    hbs3 = [hb, hb2]

    def p3_load(blk):
        sl = blk % 2
        P.dma('sp', xbuf[sl][:], x_out[blk], reads=[('xout', blk)] if write_x else [], writes=xkeys(sl), semkey=('xbsem', sl))

    def p3_norm(blk):
        sl = blk % 2
        emit_norm_k(P, xbuf[sl], xkeys(sl), gn, 'gvec_n', hbs3[sl], [('hb', sl, c) for c in range(8)], ones_bf, sq, psn, rs)

    if nxt == 'final':
        for blk in range(NBLK):
            slot = blk % 2
            xb = xbuf[slot]
            xk = xkeys(slot)
            p3_load(blk)
            emit_norm_k(P, xb, xk, gn, 'gvec_n', None, [('st', c % 4) for c in range(8)], ones_bf, sq, psn, rs,
                        outs=lambda c: st[c % 4][:],
                        after=lambda c: P.dma('sp', y_out[blk][:, c, :], st[c % 4][:], reads=[('st', c % 4)], writes=[('yout', blk, c)],
                                              semkey=('stsem', c % 4)))
    else:
        p3_load(0)
        if NBLK > 1:
            p3_load(1)
        p3_norm(0)
        for blk in range(NBLK):
            slot = blk % 2
            hcur = hbs3[slot]
            hk = [('hb', slot, c) for c in range(8)]
            for cc in range(nch):
                if cc == nch // 2 and blk + 1 < NBLK:
                    p3_norm(blk + 1)
                m = min(128, ncols - cc * 128)
                pt, pk = psr.next()
                for kc in range(8):
                    P.op('pe', lambda e, pt=pt, kc=kc, cc=cc, m=m, hcur=hcur: e.matmul(
                        pt[0:m, :TB], lhsT=WIv[:, kc, cc * 128:cc * 128 + m], rhs=hcur[:, kc, :], start=(kc == 0), stop=(kc == 7)),
                        reads=[hk[kc], ('WA', kc)], writes=[pk])
                s_ = st[cc % 4]
                sk = ('st', cc % 4)
                if nxt == 'rg' and cc < 8:
                    P.op('act', lambda e, pt=pt, s_=s_: e.activation(out=s_[:], in_=pt[:, :TB], func=AF.Gelu_apprx_tanh), reads=[pk], writes=[sk])
                elif nxt == 'nsa' and m < 128:
                    P.op('act', lambda e, pt=pt, s_=s_, m=m: e.activation(out=s_[0:m, :], in_=pt[0:m, :TB], func=AF.Sigmoid, bias=bg[:]),
                         reads=[pk, 'bg'], writes=[sk])
                else:
                    P.op('dve', lambda e, pt=pt, s_=s_: e.tensor_copy(s_[:], pt[:, :TB]), reads=[pk], writes=[sk])
                P.dma('sp', proj[cc * 128:cc * 128 + m, blk * TB:(blk + 1) * TB], s_[0:m, :], reads=[sk], writes=[('proj', blk, cc)],
                      semkey=('stsem', cc % 4))
            if blk + 2 < NBLK:
                p3_load(blk + 2)
    P.phase_end()


def emit_norm_k(P, xb, xk, g_sb, gkey, hb, hk, ones_bf, sq, psn, rs, sfx=0, sqk=0, outs=None, after=None):
    P.op('act', lambda e: e.activation(out=sq[:], in_=xb[:], func=AF.Square), reads=xk, writes=[('sq', sqk)])
    for c in range(8):
        P.op('pe', lambda e, c=c: e.matmul(psn[:, :TB], lhsT=ones_bf[:], rhs=sq[:, c, :], start=(c == 0), stop=(c == 7)),
             reads=[('sq', sqk), 'ones'], writes=['psn'])
    P.op('act', lambda e: e.activation(out=rs[:], in_=psn[:, :TB], func=AF.Sqrt, bias=P.eps_tile[:], scale=1.0 / D),
         reads=['psn', 'epsc'], writes=[('rs', sfx)])
    P.op('dve', lambda e: e.reciprocal(rs[:], rs[:]), reads=[('rs', sfx)], writes=[('rs', sfx)])
    for c in range(8):
        dst = outs(c) if outs is not None else hb[:, c, :]
        P.op('dve', lambda e, c=c, dst=dst: e.scalar_tensor_tensor(out=dst, in0=xb[:, c, :], scalar=g_sb[:, c:c + 1], in1=rs[:],
                                                                 op0=ALU.mult, op1=ALU.mult),
             reads=[xk[c], ('rs', sfx), gkey], writes=[hk[c]])
        if after is not None:
            after(c)


SEG = 2048
NSEG = S // SEG


def build_scan_prog():
    P = Prog()
    ysrc = P.dram("ysrc", [2, 256, S], F32, "ExternalInput")
    xpsrc = P.dram("xpsrc", [2, 256, S], F32, "ExternalInput")
    cpar = P.dram("cpar", [2, 128, 2, 8], F32, "ExternalInput")
    wa = P.dram("wa", [2, 256, 256], F32, "ExternalInput")
    wx = P.dram("wx", [2, 256, 256], F32, "ExternalInput")
    mixo = P.dram("mixo", [2, 256, S], F32, "ExternalOutput")
    emit_scan(P, 2, lambda u, j, a, b: ysrc[u, j * 128:(j + 1) * 128, a:b], lambda u, j, a, b: xpsrc[u, j * 128:(j + 1) * 128, a:b],
              cpar, wa, wx, lambda u, j, a, b: mixo[u, j * 128:(j + 1) * 128, a:b])
    return P.build()


def emit_scan(P, NU, ysrc, xpsrc, cpar, wa, wx, mixo):
    P.phase_begin()
    cp = P.sb([128, NU, 2, 8], F32)
    cst = P.sb([128, NU, 2, 4], F32)
    wab = P.sb([128, NU, 2, 256], BF16)
    wxb = P.sb([128, NU, 2, 256], BF16)
    xp = [P.sb([128, 3 + SEG], F32) for _ in range(2)]
    xb = [P.sb([128, SEG], F32) for _ in range(2)]
    xbb = [P.sb([128, SEG], BF16) for _ in range(2)]
    rt = P.sb([128, SEG], F32)
    it = P.sb([128, SEG], F32)
    at = P.sb([128, SEG], F32)
    mt = P.sb([128, SEG], F32)
    ut = P.sb([128, SEG], F32)
    ht = P.sb([128, SEG], F32)
    yt = P.sb([128, SEG], F32)
    ot = P.sb([128, SEG], F32)
    hcar = P.sb([128, 2], F32)
    psr = PsumRot(P, 6, 'ps')

    for u in range(NU):
        P.dma('sp', cp[:, u], cpar[u], writes=[('cp', u)])
        P.dma('pool', wab[:, u], wa[u].rearrange("(j p) o -> p j o", p=128), writes=[('wab', u)])
        P.dma('pool', wxb[:, u], wx[u].rearrange("(j p) o -> p j o", p=128), writes=[('wxb', u)])
        for j in range(2):
            lam = cp[:, u, j, 7:8]
            P.op('act', lambda e, u=u, j=j, lam=lam: e.activation(out=cst[:, u, j, 0:1], in_=lam, func=AF.Exp, scale=-1.0),
                 reads=[('cp', u)], writes=[('cst', u, j)])
            P.op('act', lambda e, u=u, j=j: e.activation(out=cst[:, u, j, 1:2], in_=cst[:, u, j, 0:1], func=AF.Ln, bias=1.0),
                 reads=[('cst', u, j)], writes=[('cst', u, j)])
            P.op('dve', lambda e, u=u, j=j: e.tensor_scalar(out=cst[:, u, j, 2:3], in0=cst[:, u, j, 1:2], scalar1=-8.0, scalar2=None, op0=ALU.mult),
                 reads=[('cst', u, j)], writes=[('cst', u, j)])
            P.op('dve', lambda e, u=u, j=j: e.tensor_scalar(out=cst[:, u, j, 3:4], in0=cst[:, u, j, 1:2], scalar1=-16.0, scalar2=None, op0=ALU.mult),
                 reads=[('cst', u, j)], writes=[('cst', u, j)])

    for u in range(NU):
        for s in range(NSEG):
            t0 = s * SEG
            for j in range(2):
                rows = slice(j * 128, (j + 1) * 128)
                if s == 0:
                    P.op('pool', lambda e, j=j: e.memset(xp[j][:, 0:3], 0.0), writes=[('xp', j)])
                    P.dma('sp', xp[j][:, 3:3 + SEG], xpsrc(u, j, 0, SEG), writes=[('xp', j)], semkey=('xpsem', j))
                else:
                    P.dma('sp', xp[j][:, :], xpsrc(u, j, t0 - 3, t0 + SEG), writes=[('xp', j)], semkey=('xpsem', j))
                cw = lambda k, u=u, j=j: cp[:, u, j, k:k + 1]
                P.op('dve', lambda e, j=j, cw=cw: e.tensor_scalar(out=xb[j][:], in0=xp[j][:, 3:3 + SEG], scalar1=cw(3), scalar2=cw(4),
                                                                 op0=ALU.mult, op1=ALU.add),
                     reads=[('xp', j), ('cp', u)], writes=[('xb', j)])
                for k in (2, 1, 0):
                    P.op('dve', lambda e, j=j, k=k, cw=cw: e.scalar_tensor_tensor(out=xb[j][:], in0=xp[j][:, k:k + SEG], scalar=cw(k), in1=xb[j][:],
                                                                                 op0=ALU.mult, op1=ALU.add),
                         reads=[('xp', j), ('cp', u), ('xb', j)], writes=[('xb', j)])
                P.op('act', lambda e, j=j: e.copy(out=xbb[j][:], in_=xb[j][:]), reads=[('xb', j)], writes=[('xbb', j)])
            for j in range(2):
                rows = slice(j * 128, (j + 1) * 128)
                P.dma('sp', yt[:], ysrc(u, j, t0, t0 + SEG), writes=['yt'])
                for (wt, wkey, dst, dkey, bk) in ((wab, 'wab', rt, 'rt', 5), (wxb, 'wxb', it, 'it', 6)):
                    for tb in range(SEG // 512):
                        pt, pk = psr.next()
                        for jin in range(2):
                            P.op('pe', lambda e, pt=pt, wt=wt, jin=jin, j=j, tb=tb, u=u: e.matmul(
                                pt[:], lhsT=wt[:, u, jin, j * 128:(j + 1) * 128], rhs=xbb[jin][:, tb * 512:(tb + 1) * 512],
                                start=(jin == 0), stop=(jin == 1)),
                                reads=[('xbb', jin), (wkey, u)], writes=[pk])
                        P.op('act', lambda e, pt=pt, dst=dst, tb=tb, u=u, j=j, bk=bk: e.activation(
                            out=dst[:, tb * 512:(tb + 1) * 512], in_=pt[:], func=AF.Sigmoid, bias=cp[:, u, j, bk:bk + 1]),
                            reads=[pk, ('cp', u)], writes=[dkey])
                P.op('act', lambda e, u=u, j=j: e.activation(out=at[:], in_=rt[:], func=AF.Exp, scale=cst[:, u, j, 2:3]),
                     reads=['rt', ('cst', u, j)], writes=['at'])
                P.op('act', lambda e, u=u, j=j: e.activation(out=mt[:], in_=rt[:], func=AF.Exp, scale=cst[:, u, j, 3:4]),
                     reads=['rt', ('cst', u, j)], writes=['mt'])
                P.op('dve', lambda e: e.tensor_scalar(out=mt[:], in0=mt[:], scalar1=-1.0, scalar2=1.0, op0=ALU.mult, op1=ALU.add),
                     reads=['mt'], writes=['mt'])
                P.op('dve', lambda e: e.tensor_scalar_max(out=mt[:], in0=mt[:], scalar1=1e-20), reads=['mt'], writes=['mt'])
                P.op('act', lambda e: e.activation(out=mt[:], in_=mt[:], func=AF.Sqrt), reads=['mt'], writes=['mt'])
                P.op('pool', lambda e, j=j: e.tensor_tensor(out=ut[:], in0=it[:], in1=xb[j][:], op=ALU.mult),
                     reads=['it', ('xb', j)], writes=['ut'])
                P.op('dve', lambda e: e.tensor_tensor(out=ut[:], in0=ut[:], in1=mt[:], op=ALU.mult), reads=['ut', 'mt'], writes=['ut'])
                init = 0.0 if s == 0 else hcar[:, j:j + 1]
                P.op('dve', lambda e, init=init: e.tensor_tensor_scan(ht[:], at[:], ut[:], init, ALU.mult, ALU.add),
                     reads=['at', 'ut', ('hcar', j)], writes=['ht'])
                P.op('dve', lambda e, j=j: e.tensor_copy(hcar[:, j:j + 1], ht[:, SEG - 1:SEG]), reads=['ht'], writes=[('hcar', j)])
                P.op('pool', lambda e: e.tensor_tensor(out=ot[:], in0=ht[:], in1=yt[:], op=ALU.mult), reads=['ht', 'yt'], writes=['ot'])
                P.dma('sp', mixo(u, j, t0, t0 + SEG), ot[:], reads=['ot'], writes=[('mixo', u, j, s)], semkey=('osem',))
    P.phase_end()


_PROGS = {}


def _prog(key, fn):
    if key not in _PROGS:
        _PROGS[key] = fn()
    return _PROGS[key]


def _run(nc, in_maps):
    res = run_bass_kernel_spmd(nc, in_maps, core_ids=list(range(NCORES)))
    return res.results


def to_xl(xc):
    return np.ascontiguousarray(xc.reshape(NBLK, TB, 8, 128).transpose(0, 3, 2, 1))


def from_xl(xl):
    return np.ascontiguousarray(xl.transpose(0, 3, 2, 1).reshape(TPC, D))


def gvec(g):
    return np.ascontiguousarray(g.reshape(8, 128).T)


def run_token(xls, has_post, has_mlp, nxt, mixs=None, w_out=None, g_ffn=None, w_up=None, w_down=None, g_nxt=None, w_in=None, b_gate=None):
    nc = _prog(('tok', has_post, has_mlp, nxt), lambda: build_token_prog(has_post, has_mlp, nxt))
    maps = []
    for c in range(NCORES):
        m = {"x_in": xls[c], "g_nxt": gvec(g_nxt)}
        if has_post:
            m["mix"] = mixs[c]
            m["w_out"] = w_out
        if has_mlp:
            m["g_ffn"] = gvec(g_ffn)
            m["w_up"] = w_up
            m["w_down"] = w_down
        if nxt != 'final':
            m["w_in"] = w_in
        if nxt == 'nsa':
            m["b_gate"] = np.ascontiguousarray(b_gate.reshape(48, 1))
        maps.append(m)
    return _run(nc, maps)


def run_scan(projs, conv_w, conv_b, w_a, b_a, w_x, b_x, lam):
    nc = _prog(('scan',), build_scan_prog)
    maps = []
    for c in range(NCORES):
        b, hh = c // 2, c % 2
        ys, xs, cps, was, wxs = [], [], [], [], []
        for u in range(2):
            n = 2 * hh + u
            ch = slice(n * 256, (n + 1) * 256)
            ys.append(np.concatenate([projs[2 * b][ch], projs[2 * b + 1][ch]], axis=1))
            ch2 = slice(1024 + n * 256, 1024 + (n + 1) * 256)
            xs.append(np.concatenate([projs[2 * b][ch2], projs[2 * b + 1][ch2]], axis=1))
            par = np.stack([conv_w[0, ch], conv_w[1, ch], conv_w[2, ch], conv_w[3, ch], conv_b[ch], b_a[ch], b_x[ch], lam[ch]], axis=-1)
            cps.append(par.reshape(2, 128, 8).transpose(1, 0, 2))
            was.append(w_a[n])
            wxs.append(w_x[n])
        maps.append({"ysrc": np.ascontiguousarray(np.stack(ys)), "xpsrc": np.ascontiguousarray(np.stack(xs)),
                     "cpar": np.ascontiguousarray(np.stack(cps)), "wa": np.ascontiguousarray(np.stack(was)),
                     "wx": np.ascontiguousarray(np.stack(wxs))})
    res = _run(nc, maps)
    mixs = []
    for c in range(NCORES):
        b, hh = c // 2, c % 2
        parts = []
        for n in range(4):
            src = res[2 * b + n // 2]["mixo"][n % 2]
            parts.append(src[:, hh * TPC:(hh + 1) * TPC])
        mixs.append(np.ascontiguousarray(np.concatenate(parts, axis=0)))
    return mixs


QC = 512
NQC = S // QC
NCMP = 511
SCALE = 0.125


def att_tables(g):
    slopes = np.exp2(-8.0 * (np.arange(1, 17, dtype=np.float64)) / 16.0).reshape(4, 4)[g]
    tq = np.arange(QC, dtype=np.float64)
    aug = (-slopes[:, None] * tq[None, :] / SCALE).astype(np.float32)
    kk = np.arange(128, dtype=np.float64)
    dl = (np.arange(67, dtype=np.float64) - 63.0) * 128.0
    bias_sw = (slopes[None, :, None] * (dl[None, None, :] + kk[:, None, None])).astype(np.float32)
    kt = np.arange(4, dtype=np.float64)
    qc = np.arange(NQC, dtype=np.float64)
    pos = 16.0 * (128.0 * kt[None, :, None] + kk[:, None, None]) + 31.0 - QC * qc[None, None, :]
    bias_c = (slopes[None, :, None, None] * pos[:, None, :, :]).astype(np.float32)
    return aug, bias_sw, bias_c


def att_consts():
    n = np.arange(512)
    j = np.arange(128)
    ov = np.minimum(n[:, None] * 16 + 32, j[None, :] * 64 + 64) - np.maximum(n[:, None] * 16, j[None, :] * 64)
    cm = (np.maximum(ov, 0) / 16.0).astype(np.float32)
    cm[511] = 0.0
    cmpmap = np.ascontiguousarray(cm.reshape(4, 128, 128).transpose(1, 0, 2))
    jj = np.arange(128)[:, None, None]
    ktt = np.arange(64)[None, :, None]
    kk = np.arange(128)[None, None, :]
    eexp = (jj == 2 * ktt + kk // 64).astype(np.float32)
    kq = np.arange(128)[:, None, None]
    tq = np.arange(QC)[None, None, :]
    dw = (np.arange(8)[None, :, None] - 4) * 128
    d = tq - kq - dw
    winmask = ((d >= 0) & (d < 512)).astype(np.float32)
    dc = np.arange(4)[None, :, None] * 128
    causal = ((tq - kq - dc) >= 0).astype(np.float32)
    tt = np.arange(128)[:, None]
    w = np.arange(256)[None, :]
    jr = w - 126
    c = (tt >= 64).astype(np.int64)
    valid = jr <= c
    forced = (jr == c) | (jr == c - 1)
    tb = (300.0 - w) * 1e-35
    validW = valid.astype(np.float32)
    addW = np.where(valid, 1e4 * forced + tb, -1.0).astype(np.float32)
    ident = np.eye(128, dtype=np.float32)
    return dict(cmpmap=cmpmap, eexp=eexp, winmask=winmask, causal=causal, validW=validW, addW=addW, ident=ident)


def build_att_prog(dbg=False, nunits=2, nqc=NQC):
    P = Prog()
    A = {}
    qT = P.dram("qT", [2, 4, 64, S], F32, "ExternalInput")
    kT = P.dram("kT", [2, 3, 64, S], F32, "ExternalInput")
    vcT = P.dram("vcT", [2, 64, S], F32, "ExternalInput")
    vtok = P.dram("vtok", [2, 2, S, 64], F32, "ExternalInput")
    gates = P.dram("gates", [2, S, 12], F32, "ExternalInput")
    for nm, shp in (("w1k", [2048, 64]), ("w2k", [64, 64]), ("w1v", [2048, 64]), ("w2v", [64, 64]), ("pek", [128, 16]), ("pev", [128, 16]),
                    ("augrow", [2, 1, 4, QC]), ("bias_sw", [2, 128, 4, 67]), ("bias_c", [2, 128, 4, 4, NQC]), ("cmpmap", [128, 4, 128]),
                    ("eexp", [128, 64, 128]), ("winmask", [128, 8, QC]), ("causal", [128, 4, QC]), ("validW", [128, 256]), ("addW", [128, 256]),
                    ("ident", [128, 128])):
        A[nm] = P.dram(nm, shp, F32, "ExternalInput")
    o_d = P.dram("o", [2, 3, S, 256] if dbg else [2, S, 256], F32, "ExternalOutput")
    A['q_src'] = lambda u, q0: qT[u, :, :, q0:q0 + QC].rearrange("h d t -> d h t")
    A['k_src'] = lambda u, i: kT[u, i]
    A['vc_src'] = lambda u: vcT[u]
    A['vtok_src'] = lambda u, i: vtok[u, i]
    A['gates_src'] = lambda u, q0: gates[u, q0:q0 + QC, :]
    A['o_dst'] = (lambda u, br, q0: o_d[u, br, q0:q0 + QC, :]) if dbg else (lambda u, br, q0: o_d[u, q0:q0 + QC, :])
    emit_att(P, A, dbg=dbg, nunits=nunits, nqc=nqc, fused=False)
    return P.build()


def emit_att(P, A, dbg=False, nunits=2, nqc=NQC, fused=False, groups=None):
    P.phase_begin()
    ALIBI_CUT = 100.0

    def dcut(u, h):
        if groups is None:
            return 1e30
        return ALIBI_CUT / (2.0 ** (-(4 * groups[u] + h + 1) / 2.0))

    NU = nunits
    w1k, w2k, w1v, w2v, pek, pev = A['w1k'], A['w2k'], A['w1v'], A['w2v'], A['pek'], A['pev']
    augrow, bias_sw_d, bias_c_d = A['augrow'], A['bias_sw'], A['bias_c']
    cmpmap_d, eexp_d, winmask_d, causal_d = A['cmpmap'], A['eexp'], A['winmask'], A['causal']
    validW_d, addW_d, ident_d = A['validW'], A['addW'], A['ident']

    zk = P.sb([64, S], BF16)
    zv = P.sb([64, S], BF16)
    ksA = P.sb([65, S], BF16)
    kwA = P.sb([65, S], BF16)
    vsA = P.sb([128, 64, 65], BF16)
    vwA = P.sb([128, 64, 65], BF16)
    kcA = P.sb([65, 512], BF16)
    vcA = P.sb([128, 4, 65], BF16)
    eexp = P.sb([128, 64, 128], BF16)
    cmpb = P.sb([128, 4, 128], BF16)
    winm = P.sb([128, 8, QC], BF16)
    caus = P.sb([128, 4, QC], BF16)
    validW = P.sb([128, 256], F32)
    addW = P.sb([128, 256], F32)
    identf = P.sb([128, 128], F32)
    bsw = P.sb([128, NU, 4, 67], F32)
    bc = P.sb([128, NU, 4, 4, NQC], F32)
    w1d = [P.sb([64, 32, 64], BF16) for _ in range(2)]
    w1p = [P.sb([128, 16, 64], BF16) for _ in range(2)]
    w2b = [P.sb([64, 64], BF16) for _ in range(2)]
    pef = [P.sb([128, 16], BF16) for _ in range(2)]
    cb = P.sb([64, 2], F32)
    h1 = P.sb([64, 512], BF16)
    qbuf = [P.sb([65, 4, QC], BF16) for _ in range(2)]
    gt = [P.sb([128, 4, 12], F32) for _ in range(2)]
    acc = [[P.sb([128, 4, 256], F32) for _ in range(3 if dbg else 1)] for _ in range(2)]
    PTc = [P.sb([128, QC], BF16) for _ in range(4)]
    NPT = 10
    PT = [P.sb([128, QC], BF16) for _ in range(NPT)]
    PTm = [P.sb([128, QC], BF16) for _ in range(NPT)]
    msb = [P.sb([128, QC], BF16) for _ in range(3)]
    osb = [P.sb([65, QC], F32) for _ in range(2)]
    rl = [P.sb([128, 4], F32) for _ in range(2)]
    ff = [P.sb([128, 4], F32) for _ in range(2)]
    impacc = P.sb([128, 4, 128], F32)
    sc = P.sb([128, 128], F32)
    sc2s = [P.sb([128, 128], F32) for _ in range(4)]
    t8 = P.sb([128, 16], F32)
    selT = P.sb([128, QC], BF16)
    pso = [P.ps([128, 512], F32) for _ in range(4)]
    pss = PsumRot(P, 2, 'pss')
    psM = P.ps([128, 512], F32)
    pss3 = PsumRot(P, 0, 'pss3')
    pss3.tiles = pss.tiles + [psM]
    pss3.keys = pss.keys + ['psM']
    SK = 7
    psx = P.ps([128, 512], F32)
    if fused:
        vstage = P.sb([64, 2048], F32)
        gtf = [P.sb([12, QC], F32) for _ in range(2)]
        oT = [P.sb([128, QC], F32) for _ in range(2)]

    P.dma('pool', eexp[:], eexp_d, writes=['eexp'])
    P.dma('pool', cmpb[:], cmpmap_d, writes=['cmpb'])
    P.dma('pool', winm[:], winmask_d, writes=['winm'])
    P.dma('pool', caus[:], causal_d, writes=['caus'])
    P.dma('sp', validW[:], validW_d, writes=['validW'])
    P.dma('sp', addW[:], addW_d, writes=['addW'])
    P.dma('sp', identf[:], ident_d, writes=['identf'])
    for u in range(NU):
        P.dma('sp', bsw[:, u], bias_sw_d[u], writes=[('bsw', u)])
        P.dma('sp', bc[:, u], bias_c_d[u], writes=[('bc', u)])
    for i, (w1, w2, pe) in enumerate(((w1k, w2k, pek), (w1v, w2v, pev))):
        P.dma('pool', w1d[i][:], w1.rearrange("(l d) o -> d l o", d=64), writes=[('w1d', i)])
        P.dma('pool', w1p[i][:], w1.rearrange("(j p) o -> p j o", p=128), writes=[('w1p', i)])
        P.dma('pool', w2b[i][:], w2, writes=[('w2b', i)])
        P.dma('pool', pef[i][:], pe, writes=[('pef', i)])
    P.op('pool', lambda e: e.memset(ksA[64:65, :], 1.0), writes=['ksA1'])
    P.op('pool', lambda e: e.memset(kwA[64:65, :], 1.0), writes=['kwA1'])
    P.op('pool', lambda e: e.memset(kcA[64:65, :], 1.0), writes=['kcA1'])
    P.op('pool', lambda e: e.memset(vsA[:, :, 64:65], 1.0), writes=['vsA1'])
    P.op('pool', lambda e: e.memset(vwA[:, :, 64:65], 1.0), writes=['vwA1'])
    P.op('pool', lambda e: e.memset(vcA[:, :, 64:65], 1.0), writes=['vcA1'])
    P.op('pool', lambda e: e.memset(h1[:], 0.0), writes=['h1'])
    for i in range(2):
        for j in range(16):
            P.op('pe', lambda e, i=i, j=j: e.matmul(psx[0:64, i:i + 1], lhsT=w1p[i][:, j, :], rhs=pef[i][:, j:j + 1], start=(j == 0), stop=(j == 15)),
                 reads=[('w1p', i), ('pef', i)], writes=['psx'])
        P.op('act', lambda e, i=i: e.copy(out=cb[:, i:i + 1], in_=psx[0:64, i:i + 1]), reads=['psx'], writes=[('cb', i)])

    cnt = {'pt': 0, 'ptm': 0, 'msb': 0, 'osb': 0}

    def do_unit(u):
        P.dma('pool', zk[:], A['k_src'](u, 0), writes=['zk'])
        P.dma('pool', zv[:], A['vc_src'](u), writes=['zv'])
        P.dma('pool', ksA[0:64, :], A['k_src'](u, 1), writes=['ksA'])
        P.dma('pool', kwA[0:64, :], A['k_src'](u, 2), writes=['kwA'])
        if not fused:
            P.dma('pool', vsA[:, :, 0:64], A['vtok_src'](u, 0).rearrange("(kt p) d -> p kt d", p=128), writes=['vsA'])
            P.dma('pool', vwA[:, :, 0:64], A['vtok_src'](u, 1).rearrange("(kt p) d -> p kt d", p=128), writes=['vwA'])
        else:
            for i, (vA, vkey) in enumerate(((vsA, 'vsA'), (vwA, 'vwA'))):
                for pc in range(4):
                    P.dma('sp', vstage[:], A['vT_src'](u, i)[:, pc * 2048:(pc + 1) * 2048], writes=['vstage'])
                    for half in range(2):
                        for t8i in range(8):
                            c0 = half * 1024 + t8i * 128
                            P.op('pe', lambda e, t8i=t8i, c0=c0: e.transpose(psx[:, t8i * 64:(t8i + 1) * 64], vstage[0:64, c0:c0 + 128], identf[0:64, 0:64]),
                                 reads=['vstage', 'identf'], writes=['psx'])
                        kt0 = pc * 16 + half * 8
                        P.op('act', lambda e, vA=vA, kt0=kt0: e.copy(out=vA[:, kt0:kt0 + 8, 0:64], in_=psx[:].rearrange("p (a d) -> p a d", d=64)),
                             reads=['psx'], writes=[vkey])
        for slot in range(2):
            P.dma('pool', qbuf[slot][64:65, :, :], augrow[u], writes=[('qaug', slot)])
        for i, z in enumerate((zk, zv)):
            zkey = 'zk' if i == 0 else 'zv'
            pt, pk = pss.next()
            for l in range(32):
                P.op('pe', lambda e, pt=pt, i=i, l=l, z=z: e.matmul(pt[0:64, 0:NCMP], lhsT=w1d[i][:, l, :], rhs=z[:, l:l + 16 * (NCMP - 1) + 1:16],
                                                                  start=(l == 0), stop=(l == 31)),
                     reads=[zkey, ('w1d', i)], writes=[pk])
            P.op('act', lambda e, pt=pt, i=i: e.activation(out=h1[:, 0:NCMP], in_=pt[0:64, 0:NCMP], func=AF.Gelu_apprx_tanh, bias=cb[:, i:i + 1]),
                 reads=[pk, ('cb', i)], writes=['h1'])
            if i == 0:
                pt2, pk2 = pss.next()
                P.op('pe', lambda e, pt2=pt2: e.matmul(pt2[0:64, 0:NCMP], lhsT=w2b[0][:], rhs=h1[:, 0:NCMP], start=True, stop=True),
                     reads=['h1', ('w2b', 0)], writes=[pk2])
                P.op('act', lambda e, pt2=pt2: e.copy(out=kcA[0:64, 0:NCMP], in_=pt2[0:64, 0:NCMP]), reads=[pk2], writes=['kcA'])
            else:
                for kt in range(4):
                    nk = 127 if kt == 3 else 128
                    P.op('pe', lambda e, kt=kt, nk=nk: e.matmul(psx[0:nk, 0:64], lhsT=h1[:, kt * 128:kt * 128 + nk], rhs=w2b[1][:], start=True, stop=True),
                         reads=['h1', ('w2b', 1)], writes=['psx'])
                    P.op('act', lambda e, kt=kt, nk=nk: e.copy(out=vcA[0:nk, kt, 0:64], in_=psx[0:nk, 0:64]), reads=['psx'], writes=['vcA'])

    def do_chunk(u, qc):
        if True:
            q0 = qc * QC
            slot = qc % 2
            qa = qbuf[slot]
            def chunk_loads(qq):
                sl = qq % 2
                P.dma('pool', qbuf[sl][0:64, :, :], A['q_src'](u, qq * QC), writes=[('qa', sl)])
                if not fused:
                    P.dma('sp', gt[sl][:], A['gates_src'](u, qq * QC).rearrange("(ts p) k -> p ts k", p=128), writes=[('gt', sl)])
                else:
                    P.dma('sp', gtf[sl][:], A['gT_src'](u, qq * QC), writes=[('gtf', sl)])
            if qc == 0:
                chunk_loads(0)
            if qc + 1 < nqc:
                chunk_loads(qc + 1)
            if fused:
                for ts in range(4):
                    P.op('pe', lambda e, ts=ts: e.transpose(psx[:, ts * 12:(ts + 1) * 12], gtf[slot][0:12, ts * 128:(ts + 1) * 128], identf[0:12, 0:12]),
                         reads=[('gtf', slot), 'identf'], writes=['psx'])
                P.op('act', lambda e: e.copy(out=gt[slot][:], in_=psx[:, 0:48].rearrange("p (a k) -> p a k", k=12)), reads=['psx'], writes=[('gt', slot)])
            qkeys = [('qa', slot), ('qaug', slot)]

            def epilogue(h, br, with_imp=False):
                r = cnt['osb'] % 2
                cnt['osb'] += 1
                ob = osb[r]
                P.op('act', lambda e, ob=ob, h=h: e.copy(out=ob[:], in_=pso[h][0:65, :]), reads=[('pso', h)], writes=[('osb', r)])
                for ts in range(4):
                    P.op('pe', lambda e, ob=ob, ts=ts: e.transpose(psx[:, ts * 65:(ts + 1) * 65], ob[0:65, ts * 128:(ts + 1) * 128], identf[0:65, 0:65]),
                         reads=[('osb', r), 'identf'], writes=['psx'])
                rr, fr = rl[r], ff[r]
                P.op('dve', lambda e, rr=rr: e.tensor_scalar_max(out=rr[:], in0=psx[:, 64:64 + 65 * 4:65], scalar1=1e-30), reads=['psx'], writes=[('rl', r)])
                P.op('dve', lambda e, rr=rr: e.reciprocal(rr[:], rr[:]), reads=[('rl', r)], writes=[('rl', r)])
                gi = h * 3 + br
                P.op('dve', lambda e, rr=rr, fr=fr, gi=gi: e.tensor_tensor(out=fr[:], in0=rr[:], in1=gt[slot][:, :, gi], op=ALU.mult),
                     reads=[('rl', r), ('gt', slot)], writes=[('ff', r)])
                for ts in range(4):
                    dst = acc[slot][br if dbg else 0][:, ts, h * 64:(h + 1) * 64]
                    src = psx[:, ts * 65:ts * 65 + 64]
                    if br == 0 or dbg:
                        P.op('dve', lambda e, dst=dst, src=src, fr=fr, ts=ts: e.tensor_scalar(out=dst, in0=src, scalar1=fr[:, ts:ts + 1], scalar2=None, op0=ALU.mult),
                             reads=['psx', ('ff', r)], writes=[('acc', slot, br if dbg else 0, h)])
                    else:
                        P.op('dve', lambda e, dst=dst, src=src, fr=fr, ts=ts: e.scalar_tensor_tensor(out=dst, in0=src, scalar=fr[:, ts:ts + 1], in1=dst,
                                                                                                  op0=ALU.mult, op1=ALU.add),
                             reads=['psx', ('ff', r), ('acc', slot, 0, h)], writes=[('acc', slot, 0, h)])
                if with_imp:
                    for ts in range(4):
                        dst = impacc[:, ts, :]
                        src = psM[:, ts * 128:(ts + 1) * 128]
                        if h == 0:
                            P.op('dve', lambda e, dst=dst, src=src, rr=rr, ts=ts: e.tensor_scalar(out=dst, in0=src, scalar1=rr[:, ts:ts + 1], scalar2=None, op0=ALU.mult),
                                 reads=['psM', ('rl', r)], writes=['impacc'])
                        else:
                            P.op('dve', lambda e, dst=dst, src=src, rr=rr, ts=ts: e.scalar_tensor_tensor(out=dst, in0=src, scalar=rr[:, ts:ts + 1], in1=dst,
                                                                                                      op0=ALU.mult, op1=ALU.add),
                                 reads=['psM', ('rl', r), 'impacc'], writes=['impacc'])

            ktc = min(3, (32 * qc + 30) // 128)
            for h in range(4):
                k0h = 0
                while k0h < ktc and q0 - (16 * (128 * k0h + 127) + 31) > dcut(u, h):
                    k0h += 1
                for kt in range(k0h, ktc + 1):
                    nk = 127 if kt == 3 else 128
                    pt, pk = pss.next()
                    P.op('pe', lambda e, pt=pt, kt=kt, nk=nk, h=h: e.matmul(pt[0:nk, :], lhsT=kcA[0:65, kt * 128:kt * 128 + nk], rhs=qa[0:65, h, :], start=True, stop=True),
                         reads=['kcA', 'kcA1'] + qkeys, writes=[pk])
                    P.op('act', lambda e, pt=pt, kt=kt, nk=nk, h=h: e.activation(out=PTc[kt][0:nk, :], in_=pt[0:nk, :], func=AF.Exp,
                                                                              bias=bc[0:nk, u, h, kt, qc:qc + 1], scale=SCALE),
                         reads=[pk, ('bc', u)], writes=[('PTc', kt)])
                    if q0 - 16 * (128 * kt + nk - 1) - 31 < 0:
                        P.op('pool', lambda e, kt=kt, nk=nk: e.affine_select(out=PTc[kt][0:nk, :], in_=PTc[kt][0:nk, :], pattern=[[1, QC]],
                                                                           compare_op=ALU.is_ge, fill=0.0, base=q0 - 2048 * kt - 31, channel_multiplier=-16),
                             reads=[('PTc', kt)], writes=[('PTc', kt)])
                for kt in range(k0h, ktc + 1):
                    nk = 127 if kt == 3 else 128
                    P.op('pe', lambda e, kt=kt, nk=nk, h=h, k0h=k0h: e.matmul(pso[h][0:65, :], lhsT=vcA[0:nk, kt, 0:65], rhs=PTc[kt][0:nk, :],
                                                                  start=(kt == k0h), stop=(kt == ktc)),
                         reads=[('PTc', kt), 'vcA', 'vcA1'], writes=[('pso', h)])
                for ts in range(4):
                    for kt in range(k0h, ktc + 1):
                        nk = 127 if kt == 3 else 128
                        P.op('pe', lambda e, kt=kt, nk=nk, ts=ts, k0h=k0h: e.matmul(psM[:, ts * 128:(ts + 1) * 128], lhsT=PTc[kt][0:nk, ts * 128:(ts + 1) * 128],
                                                                        rhs=cmpb[0:nk, kt, :], start=(kt == k0h), stop=(kt == ktc)),
                             reads=[('PTc', kt), 'cmpb'], writes=['psM'])
                epilogue(h, 0, with_imp=True)

            for ts in range(4):
                w0 = 126 - 2 * (4 * qc + ts)
                P.op('dve', lambda e, ts=ts, w0=w0: e.tensor_tensor(out=sc[:], in0=impacc[:, ts, :], in1=validW[:, w0:w0 + 128], op=ALU.mult),
                     reads=['impacc', 'validW'], writes=['sc'])
                P.op('dve', lambda e, w0=w0, ts=ts: e.tensor_tensor(out=sc[:], in0=sc[:], in1=addW[:, w0:w0 + 128], op=ALU.add),
                     reads=['sc', 'addW'], writes=['sc'])
                P.op('dve', lambda e, ts=ts: e.tensor_scalar_add(out=sc[:, 0:1], in0=sc[:, 0:1], scalar1=1e4), reads=['sc'], writes=['sc'])
                P.op('dve', lambda e, ts=ts: e.max(t8[:, 0:8], sc[:]), reads=['sc'], writes=['t8'])
                P.op('dve', lambda e, ts=ts: e.match_replace(sc2s[ts][:], t8[:, 0:8], sc[:], -1e30), reads=['sc', 't8'], writes=[('sc2', ts)])
                P.op('dve', lambda e, ts=ts: e.max(t8[:, 8:16], sc2s[ts][:]), reads=[('sc2', ts)], writes=['t8'])
                P.op('dve', lambda e, ts=ts: e.tensor_scalar(out=sc2s[ts][:], in0=sc[:], scalar1=t8[:, 15:16], scalar2=None, op0=ALU.is_ge), reads=['sc', 't8'], writes=[('sc2', ts)])
                P.op('dve', lambda e, w0=w0, ts=ts: e.tensor_tensor(out=sc2s[ts][:], in0=sc2s[ts][:], in1=validW[:, w0:w0 + 128], op=ALU.mult), reads=[('sc2', ts), 'validW'], writes=[('sc2', ts)])

            kts = 4 * qc + 3

            def expand(kt):
                mi = cnt['msb'] % 3
                cnt['msb'] += 1
                mt_ = msb[mi]
                P.op('pe', lambda e, kt=kt: e.matmul(psx[:], lhsT=eexp[:, kt, :], rhs=selT[:], start=True, stop=True), reads=['eexp', 'selT'], writes=['psx'])
                P.op('act', lambda e, mt_=mt_: e.copy(out=mt_[:], in_=psx[:]), reads=['psx'], writes=[('msb', mi)])
                return mi
            pend = []

            def flush(n):
                while len(pend) > n:
                    (kt_, h_, pi_, vA_, vkeys_, first_, last_) = pend.pop(0)
                    P.op('pe', lambda e, pi_=pi_, kt_=kt_, h_=h_, vA_=vA_, first_=first_, last_=last_: e.matmul(
                        pso[h_][0:65, :], lhsT=vA_[:, kt_, 0:65], rhs=PTm[pi_][:], start=first_, stop=last_),
                        reads=[('PTm', pi_)] + vkeys_, writes=[('pso', h_)])
            kmin = [0] * 4
            for h in range(4):
                while kmin[h] < kts and q0 - (128 * kmin[h] + 127) > dcut(u, h):
                    kmin[h] += 1
            ktlo = min(kmin)

            kt0w = max(0, 4 * qc - 4)
            kminw = [max(kt0w, kmin[h]) for h in range(4)]
            for kt in range(kt0w, kts + 1):
                dl = 128 * kt - q0
                for h in range(4):
                    if kt < kminw[h]:
                        continue
                    pt, pk = pss3.next()
                    P.op('pe', lambda e, pt=pt, kt=kt, h=h: e.matmul(pt[:], lhsT=kwA[0:65, kt * 128:(kt + 1) * 128], rhs=qa[0:65, h, :], start=True, stop=True),
                         reads=['kwA', 'kwA1'] + qkeys, writes=[pk])
                    pi = cnt['pt'] % NPT
                    cnt['pt'] += 1
                    P.op('act', lambda e, pt=pt, pi=pi, h=h, dl=dl: e.activation(out=PT[pi][:], in_=pt[:], func=AF.Exp, bias=bsw[:, u, h, dl // 128 + 63:dl // 128 + 64], scale=SCALE),
                         reads=[pk, ('bsw', u)], writes=[('PT', pi)])
                    eng = 'pool' if h % 2 == 0 else 'pool'
                    if dl >= 0:
                        P.op(eng, lambda e, pi=pi, dl=dl: e.affine_select(out=PTm[pi][:], in_=PT[pi][:], pattern=[[1, QC]], compare_op=ALU.is_ge,
                                                                        fill=0.0, base=-dl, channel_multiplier=-1),
                             reads=[('PT', pi)], writes=[('PTm', pi)])
                    else:
                        P.op(eng, lambda e, pi=pi, dl=dl: e.affine_select(out=PTm[pi][:], in_=PT[pi][:], pattern=[[-1, QC]], compare_op=ALU.is_ge,
                                                                        fill=0.0, base=dl + 511, channel_multiplier=1),
                             reads=[('PT', pi)], writes=[('PTm', pi)])
                    pend.append((kt, h, pi, vwA, ['vwA', 'vwA1'], kt == kminw[h], kt == kts))
                    flush(SK)
            flush(0)
            for h in range(4):
                epilogue(h, 2)
            for ts in range(4):
                P.op('pe', lambda e, ts=ts: e.transpose(psx[:, ts * 128:(ts + 1) * 128], sc2s[ts][:], identf[:]), reads=[('sc2', ts), 'identf'], writes=['psx'])
            P.op('act', lambda e: e.copy(out=selT[:], in_=psx[:]), reads=['psx'], writes=['selT'])

            mi_next = expand(ktlo)
            for kt in range(ktlo, kts + 1):
                dl = 128 * kt - q0
                mi = mi_next
                mt_ = msb[mi]
                hs = [h for h in range(4) if kt >= kmin[h]]
                for h in hs:
                    pt, pk = pss3.next()
                    P.op('pe', lambda e, pt=pt, kt=kt, h=h: e.matmul(pt[:], lhsT=ksA[0:65, kt * 128:(kt + 1) * 128], rhs=qa[0:65, h, :], start=True, stop=True),
                         reads=['ksA', 'ksA1'] + qkeys, writes=[pk])
                    if h == hs[0] and kt < kts:
                        mi_next = expand(kt + 1)
                    pi = cnt['pt'] % NPT
                    cnt['pt'] += 1
                    P.op('act', lambda e, pt=pt, pi=pi, h=h, dl=dl: e.activation(out=PT[pi][:], in_=pt[:], func=AF.Exp, bias=bsw[:, u, h, dl // 128 + 63:dl // 128 + 64], scale=SCALE),
                         reads=[pk, ('bsw', u)], writes=[('PT', pi)])
                    if dl >= 0:
                        P.op('pool', lambda e, pi=pi, dl=dl: e.affine_select(out=PT[pi][:], in_=PT[pi][:], pattern=[[1, QC]], compare_op=ALU.is_ge,
                                                                           fill=0.0, base=-dl, channel_multiplier=-1),
                             reads=[('PT', pi)], writes=[('PT', pi)])
                        eng = 'dve'
                    else:
                        eng = 'dve' if h % 2 == 0 else 'pool'
                    P.op(eng, lambda e, pi=pi, mt_=mt_: e.tensor_tensor(out=PTm[pi][:], in0=PT[pi][:], in1=mt_[:], op=ALU.mult),
                         reads=[('PT', pi), ('msb', mi)], writes=[('PTm', pi)])
                    pend.append((kt, h, pi, vsA, ['vsA', 'vsA1'], kt == kmin[h], kt == kts))
                    flush(SK)
            flush(0)
            for h in range(4):
                epilogue(h, 1)

            for br in range(3 if dbg else 1):
                if not fused:
                    dst = A['o_dst'](u, br, q0).rearrange("(ts p) f -> p ts f", p=128)
                    P.dma('sp', dst, acc[slot][br][:], reads=[('acc', slot, br, h) for h in range(4)],
                          writes=[('o', u, qc, br)], semkey=('osem', slot, br))
                else:
                    for fb in range(2):
                        for ts in range(4):
                            P.op('pe', lambda e, ts=ts, fb=fb, br=br: e.transpose(psx[:, ts * 128:(ts + 1) * 128], acc[slot][br][:, ts, fb * 128:(fb + 1) * 128], identf[:]),
                                 reads=[('acc', slot, br, 2 * fb), ('acc', slot, br, 2 * fb + 1), 'identf'], writes=['psx'])
                        P.op('act', lambda e, fb=fb: e.copy(out=oT[fb][:], in_=psx[:]), reads=['psx'], writes=[('oT', fb)])
                        P.dma('sp', A['oT_dst'](u, fb, q0), oT[fb][:], reads=[('oT', fb)], writes=[('o', u, qc, fb)], semkey=('osem', fb))

    for u in range(nunits):
        do_unit(u)
        for qc in range(nqc):
            do_chunk(u, qc)
    P.phase_end()


def run_att(projs, w1k, w2k, w1v, w2v, pe_k, pe_v):
    nc = _prog(('att',), build_att_prog)
    consts = att_consts()
    maps = []
    for c in range(NCORES):
        b, hh = c // 2, c % 2
        full = np.concatenate([projs[2 * b], projs[2 * b + 1]], axis=1)
        qs, ks, vcs, vts, gs, augs, bsws, bcs = [], [], [], [], [], [], [], []
        for u in range(2):
            g = 2 * hh + u
            qs.append(full[g * 256:(g + 1) * 256].reshape(4, 64, S))
            kv = lambda i: full[1024 + i * 256 + g * 64:1024 + i * 256 + (g + 1) * 64]
            ks.append(np.stack([kv(0), kv(2), kv(4)]))
            vcs.append(kv(1))
            vts.append(np.stack([kv(3).T, kv(5).T]))
            gs.append(full[2560 + g * 12:2560 + (g + 1) * 12].T)
            aug, bsw, bc = att_tables(g)
            augs.append(aug[None])
            bsws.append(bsw)
            bcs.append(bc)
        m = {"qT": np.ascontiguousarray(np.stack(qs)), "kT": np.ascontiguousarray(np.stack(ks)), "vcT": np.ascontiguousarray(np.stack(vcs)),
             "vtok": np.ascontiguousarray(np.stack(vts)), "gates": np.ascontiguousarray(np.stack(gs)),
             "w1k": w1k, "w2k": w2k, "w1v": w1v, "w2v": w2v,
             "pek": np.ascontiguousarray(pe_k.reshape(16, 128).T), "pev": np.ascontiguousarray(pe_v.reshape(16, 128).T),
             "augrow": np.ascontiguousarray(np.stack(augs)), "bias_sw": np.ascontiguousarray(np.stack(bsws)), "bias_c": np.ascontiguousarray(np.stack(bcs))}
        m.update(consts)
        maps.append(m)
    res = _run(nc, maps)
    mixs = []
    for c in range(NCORES):
        b, hh = c // 2, c % 2
        parts = []
        for g in range(4):
            src = res[2 * b + g // 2]["o"][g % 2]
            parts.append(src[hh * TPC:(hh + 1) * TPC].T)
        mixs.append(np.ascontiguousarray(np.concatenate(parts, axis=0)))
    return mixs


def kernel_unfused(x, norm_mix, norm_ffn, norm_final,
           rg_w_in, rg_conv_w, rg_conv_b, rg_w_a, rg_b_a, rg_w_x, rg_b_x, rg_lambda, rg_w_out,
           nsa_w_in, nsa_b_gate, nsa_pe_k, nsa_pe_v, nsa_w1_k, nsa_w2_k, nsa_w1_v, nsa_w2_v, nsa_w_out,
           mlp_w_up, mlp_w_down):
    f = lambda a: np.ascontiguousarray(np.asarray(a, dtype=np.float32))
    x = f(x)
    xls = [to_xl(x[c // 2, (c % 2) * TPC:(c % 2 + 1) * TPC]) for c in range(NCORES)]
    r = run_token(xls, False, False, 'rg', g_nxt=f(norm_mix[0]), w_in=f(rg_w_in[0]))
    projs = [r[c]['proj'] for c in range(NCORES)]
    for i in range(4):
        j = i // 2
        if i % 2 == 0:
            mixs = run_scan(projs, f(rg_conv_w[j]), f(rg_conv_b[j]), f(rg_w_a[j]), f(rg_b_a[j]), f(rg_w_x[j]), f(rg_b_x[j]), f(rg_lambda[j]))
            w_out = f(rg_w_out[j])
        else:
            mixs = run_att(projs, f(nsa_w1_k[j]), f(nsa_w2_k[j]), f(nsa_w1_v[j]), f(nsa_w2_v[j]), f(nsa_pe_k[j]), f(nsa_pe_v[j]))
            w_out = f(nsa_w_out[j])
        if i == 3:
            r = run_token(xls, True, True, 'final', mixs=mixs, w_out=w_out, g_ffn=f(norm_ffn[i]), w_up=f(mlp_w_up[i]), w_down=f(mlp_w_down[i]),
                          g_nxt=f(norm_final))
            break
        if i % 2 == 0:
            r = run_token(xls, True, True, 'nsa', mixs=mixs, w_out=w_out, g_ffn=f(norm_ffn[i]), w_up=f(mlp_w_up[i]), w_down=f(mlp_w_down[i]),
                          g_nxt=f(norm_mix[i + 1]), w_in=f(nsa_w_in[(i + 1) // 2]), b_gate=f(nsa_b_gate[(i + 1) // 2]))
        else:
            r = run_token(xls, True, True, 'rg', mixs=mixs, w_out=w_out, g_ffn=f(norm_ffn[i]), w_up=f(mlp_w_up[i]), w_down=f(mlp_w_down[i]),
                          g_nxt=f(norm_mix[i + 1]), w_in=f(rg_w_in[(i + 1) // 2]))
        xls = [r[c]['x_out'] for c in range(NCORES)]
        projs = [r[c]['proj'] for c in range(NCORES)]
    out = np.empty((B, S, D), np.float32)
    for c in range(NCORES):
        out[c // 2, (c % 2) * TPC:(c % 2 + 1) * TPC] = from_xl(r[c]['y_out'])
    return out


NBLK_F = S // TB


def build_fused_prog():
    P = Prog()
    x_in = P.dram("x_in", [NBLK_F, 128, 8, TB], F32, "ExternalInput")
    y_out = P.dram("y_out", [NBLK_F, 128, 8, TB], F32, "ExternalOutput")
    gmix = P.dram("gmix", [4, 128, 8], F32, "ExternalInput")
    gffn = P.dram("gffn", [4, 128, 8], F32, "ExternalInput")
    gfin = P.dram("gfin", [128, 8], F32, "ExternalInput")
    rg_w_in = P.dram("rg_w_in", [2, D, 2048], F32, "ExternalInput")
    rg_cpar = P.dram("rg_cpar", [2, 4, 128, 2, 8], F32, "ExternalInput")
    rg_w_a = P.dram("rg_w_a", [2, 4, 256, 256], F32, "ExternalInput")
    rg_w_x = P.dram("rg_w_x", [2, 4, 256, 256], F32, "ExternalInput")
    rg_w_out = P.dram("rg_w_out", [2, D, D], F32, "ExternalInput")
    nsa_w_in = P.dram("nsa_w_in", [2, D, NSA_COLS], F32, "ExternalInput")
    nsa_b_gate = P.dram("nsa_b_gate", [2, 48, 1], F32, "ExternalInput")
    nsa_pek = P.dram("nsa_pek", [2, 128, 16], F32, "ExternalInput")
    nsa_pev = P.dram("nsa_pev", [2, 128, 16], F32, "ExternalInput")
    nsa_w1_k = P.dram("nsa_w1_k", [2, 2048, 64], F32, "ExternalInput")
    nsa_w2_k = P.dram("nsa_w2_k", [2, 64, 64], F32, "ExternalInput")
    nsa_w1_v = P.dram("nsa_w1_v", [2, 2048, 64], F32, "ExternalInput")
    nsa_w2_v = P.dram("nsa_w2_v", [2, 64, 64], F32, "ExternalInput")
    nsa_w_out = P.dram("nsa_w_out", [2, D, D], F32, "ExternalInput")
    mlp_w_up = P.dram("mlp_w_up", [4, D, DFF], F32, "ExternalInput")
    mlp_w_down = P.dram("mlp_w_down", [4, DFF, D], F32, "ExternalInput")
    T = {}
    for nm, shp in (("augrow", [4, 1, 4, QC]), ("bias_sw", [4, 128, 4, 67]), ("bias_c", [4, 128, 4, 4, NQC]), ("cmpmap", [128, 4, 128]),
                    ("eexp", [128, 64, 128]), ("winmask", [128, 8, QC]), ("causal", [128, 4, QC]), ("validW", [128, 256]), ("addW", [128, 256]),
                    ("ident", [128, 128])):
        T[nm] = P.dram(nm, shp, F32, "ExternalInput")
    X = P.dram("X_scr", [NBLK_F, 128, 8, TB], F32, "Internal")
    PROJ = P.dram("PROJ_scr", [NSA_COLS, S], F32, "Internal")
    MIX = P.dram("MIX_scr", [D, S], F32, "Internal")

    def scan_phase(j):
        emit_scan(P, 4,
                  lambda u, jj, a, b: PROJ[u * 256 + jj * 128:u * 256 + (jj + 1) * 128, a:b],
                  lambda u, jj, a, b: PROJ[1024 + u * 256 + jj * 128:1024 + u * 256 + (jj + 1) * 128, a:b],
                  rg_cpar[j], rg_w_a[j], rg_w_x[j],
                  lambda u, jj, a, b: MIX[u * 256 + jj * 128:u * 256 + (jj + 1) * 128, a:b])

    def att_phase(j):
        A = dict(T)
        A.update(w1k=nsa_w1_k[j], w2k=nsa_w2_k[j], w1v=nsa_w1_v[j], w2v=nsa_w2_v[j], pek=nsa_pek[j], pev=nsa_pev[j])
        A['q_src'] = lambda u, q0: PROJ[u * 256:(u + 1) * 256, q0:q0 + QC].rearrange("(h d) t -> d h t", d=64)
        A['k_src'] = lambda u, i: PROJ[1024 + (2 * i) * 256 + u * 64:1024 + (2 * i) * 256 + (u + 1) * 64, :]
        A['vc_src'] = lambda u: PROJ[1024 + 256 + u * 64:1024 + 256 + (u + 1) * 64, :]
        A['vT_src'] = lambda u, i: PROJ[1024 + (3 + 2 * i) * 256 + u * 64:1024 + (3 + 2 * i) * 256 + (u + 1) * 64, :]
        A['gT_src'] = lambda u, q0: PROJ[2560 + u * 12:2560 + (u + 1) * 12, q0:q0 + QC]
        A['oT_dst'] = lambda u, fb, q0: MIX[u * 256 + fb * 128:u * 256 + (fb + 1) * 128, q0:q0 + QC]
        emit_att(P, A, dbg=False, nunits=4, nqc=NQC, fused=True, groups=[0, 1, 2, 3])

    emit_token(P, NBLK_F, x_in, x_in, False, False, 'rg', dict(g_nxt=gmix[0], w_in=rg_w_in[0], proj=PROJ))
    xsrc = x_in
    for i in range(4):
        j = i // 2
        if i % 2 == 0:
            scan_phase(j)
            w_out = rg_w_out[j]
        else:
            att_phase(j)
            w_out = nsa_w_out[j]
        A = dict(mix=MIX, w_out=w_out, g_ffn=gffn[i], w_up=mlp_w_up[i], w_down=mlp_w_down[i])
        if i == 3:
            A.update(g_nxt=gfin, y_out=y_out)
            emit_token(P, NBLK_F, xsrc, X, True, True, 'final', A)
        elif i % 2 == 0:
            A.update(g_nxt=gmix[i + 1], w_in=nsa_w_in[(i + 1) // 2], b_gate=nsa_b_gate[(i + 1) // 2], proj=PROJ)
            emit_token(P, NBLK_F, xsrc, X, True, True, 'nsa', A)
        else:
            A.update(g_nxt=gmix[i + 1], w_in=rg_w_in[(i + 1) // 2], proj=PROJ)
            emit_token(P, NBLK_F, xsrc, X, True, True, 'rg', A)
        xsrc = X
    return P.build()


def kernel(x, norm_mix, norm_ffn, norm_final,
                 rg_w_in, rg_conv_w, rg_conv_b, rg_w_a, rg_b_a, rg_w_x, rg_b_x, rg_lambda, rg_w_out,
                 nsa_w_in, nsa_b_gate, nsa_pe_k, nsa_pe_v, nsa_w1_k, nsa_w2_k, nsa_w1_v, nsa_w2_v, nsa_w_out,
                 mlp_w_up, mlp_w_down):
    f = lambda a: np.ascontiguousarray(np.asarray(a, dtype=np.float32))
    x = f(x)
    nc = _prog(('fused',), build_fused_prog)
    common = {
        "gmix": np.stack([gvec(f(norm_mix[i])) for i in range(4)]),
        "gffn": np.stack([gvec(f(norm_ffn[i])) for i in range(4)]),
        "gfin": gvec(f(norm_final)),
        "rg_w_in": f(rg_w_in), "rg_w_a": f(rg_w_a), "rg_w_x": f(rg_w_x), "rg_w_out": f(rg_w_out),
        "nsa_w_in": f(nsa_w_in), "nsa_b_gate": f(nsa_b_gate).reshape(2, 48, 1),
        "nsa_pek": np.ascontiguousarray(f(nsa_pe_k).reshape(2, 16, 128).transpose(0, 2, 1)),
        "nsa_pev": np.ascontiguousarray(f(nsa_pe_v).reshape(2, 16, 128).transpose(0, 2, 1)),
        "nsa_w1_k": f(nsa_w1_k), "nsa_w2_k": f(nsa_w2_k), "nsa_w1_v": f(nsa_w1_v), "nsa_w2_v": f(nsa_w2_v), "nsa_w_out": f(nsa_w_out),
        "mlp_w_up": f(mlp_w_up), "mlp_w_down": f(mlp_w_down),
    }
    cps = []
    for j in range(2):
        par = np.stack([f(rg_conv_w[j])[0], f(rg_conv_w[j])[1], f(rg_conv_w[j])[2], f(rg_conv_w[j])[3], f(rg_conv_b[j]), f(rg_b_a[j]), f(rg_b_x[j]),
                        f(rg_lambda[j])], axis=-1)
        cps.append(par.reshape(4, 2, 128, 8).transpose(0, 2, 1, 3))
    common["rg_cpar"] = np.ascontiguousarray(np.stack(cps))
    tabs = [att_tables(g) for g in range(4)]
    common["augrow"] = np.ascontiguousarray(np.stack([t[0][None] for t in tabs]))
    common["bias_sw"] = np.ascontiguousarray(np.stack([t[1] for t in tabs]))
    common["bias_c"] = np.ascontiguousarray(np.stack([t[2] for t in tabs]))
    common.update(att_consts())
    maps = []
    for c in range(NCORES):
        m = dict(common)
        xb = x[c % B]
        m["x_in"] = np.ascontiguousarray(xb.reshape(NBLK_F, TB, 8, 128).transpose(0, 3, 2, 1))
        maps.append(m)
    res = _run(nc, maps)
    out = np.empty((B, S, D), np.float32)
    for b in range(B):
        out[b] = res[b]["y_out"].transpose(0, 3, 2, 1).reshape(S, D)
    return out
```

```python
import numpy as np
from contextlib import ExitStack
import concourse.bass as bass
import concourse.mybir as mybir
from concourse.bass_utils import run_bass_kernel_spmd

F32 = mybir.dt.float32
BF16 = mybir.dt.bfloat16
AF = mybir.ActivationFunctionType
ALU = mybir.AluOpType

ENGS = ('sp', 'pe', 'act', 'dve', 'pool')
SAME_ENGINE_SYNC = True

D = 1024
S = 8192
B = 4
NCORES = 8
TPC = 4096
TB = 256
NBLK = TPC // TB
DFF = 4096
EPS = 1e-6
NSA_COLS = 2608


class Prog:
    def __init__(self):
        self.nc = bass.Bass("TRN2", target_bir_lowering=False)
        self.es = ExitStack()
        self.ops = {e: [] for e in ENGS}
        self.cnt = {e: 0 for e in ENGS}
        self.lastw = {}
        self.readers = {}
        self.waited = {e: {} for e in ENGS}
        self.dmasem = {}
        self.semnames = ['c_' + e for e in ENGS if e != 'sp']
        self.nuniq = 0
        self.floor = {}
        self.pes = None

    def phase_begin(self):
        self.pes = ExitStack()

    def phase_end(self):
        for e in ENGS:
            if e != 'sp' and self.cnt[e] > 0:
                self.floor['c_' + e] = self.cnt[e]
        for name, c in self.dmasem.values():
            if c > 0:
                self.floor[name] = c
        self.pes.close()
        self.pes = None

    def sb(self, shape, dt, name=None):
        self.nuniq += 1
        return (self.pes or self.es).enter_context(self.nc.sbuf_tensor(name or f"sb{self.nuniq}", list(shape), dt))

    def ps(self, shape, dt, name=None):
        self.nuniq += 1
        return (self.pes or self.es).enter_context(self.nc.psum_tensor(name or f"ps{self.nuniq}", list(shape), dt))

    def dram(self, name, shape, dt, kind):
        return self.nc.dram_tensor(name, list(shape), dt, kind=kind).ap()

    def _deps(self, reads, writes):
        deps = {}

        def add(s, v):
            if deps.get(s, 0) < v:
                deps[s] = v
        for k in reads:
            t = self.lastw.get(k)
            if t is not None:
                add(*t)
        for k in writes:
            t = self.lastw.get(k)
            if t is not None:
                add(*t)
            for s, v in self.readers.get(k, {}).items():
                add(s, v)
        return deps

    def _commit(self, tok, reads, writes):
        s, v = tok
        for k in reads:
            r = self.readers.setdefault(k, {})
            if r.get(s, 0) < v:
                r[s] = v
        for k in writes:
            self.lastw[k] = tok
            self.readers[k] = {}

    def _waits(self, eng, deps):
        ws = []
        own = 'c_' + eng
        for s, v in self.floor.items():
            if deps.get(s, 0) < v:
                deps[s] = v
        for s, v in deps.items():
            if s == own and (eng == 'pe' or not SAME_ENGINE_SYNC):
                continue
            if self.waited[eng].get(s, 0) >= v:
                continue
            self.waited[eng][s] = v
            ws.append((s, v))
        return ws

    def op(self, eng, fn, reads=(), writes=()):
        deps = self._deps(reads, writes)
        ws = self._waits(eng, deps)
        self.cnt[eng] += 1
        tok = ('c_' + eng, self.cnt[eng])
        self.ops[eng].append((ws, fn, (tok[0], 1)))
        self._commit(tok, reads, writes)
        return tok

    def dma(self, q, out, in_, reads=(), writes=(), semkey=None):
        semkey = semkey if semkey is not None else writes[0]
        if semkey not in self.dmasem:
            name = f"d{len(self.dmasem)}"
            self.dmasem[semkey] = [name, 0]
            self.semnames.append(name)
        ent = self.dmasem[semkey]
        deps = self._deps(reads, writes)
        ws = self._waits(q, deps)
        ent[1] += 16
        tok = (ent[0], ent[1])
        self.ops[q].append((ws, lambda e: e.dma_start(out=out, in_=in_), (ent[0], 16)))
        self._commit(tok, reads, writes)
        return tok

    def build(self):
        nc = self.nc
        final = {}
        for e in ENGS:
            if e != 'sp' and self.cnt[e] > 0:
                final['c_' + e] = self.cnt[e]
        for name, c in self.dmasem.values():
            final[name] = c
        fws = list(final.items())
        with ExitStack() as es:
            sems = {n: es.enter_context(nc.semaphore(n)) for n in self.semnames}
            ops = self.ops

            def replay(name, e):
                for ws, fn, inc in ops[name]:
                    for s, v in ws:
                        e.wait_ge(sems[s], v)
                    fn(e).then_inc(sems[inc[0]], inc[1])
                if name == 'sp':
                    for s, v in fws:
                        e.wait_ge(sems[s], v)
            with nc.Block() as block:
                @block.sync
                def _(e):
                    replay('sp', e)

                @block.tensor
                def _(e):
                    replay('pe', e)

                @block.scalar
                def _(e):
                    replay('act', e)

                @block.vector
                def _(e):
                    replay('dve', e)

                @block.gpsimd
                def _(e):
                    replay('pool', e)
        self.es.close()
        return nc


class PsumRot:
    def __init__(self, P, n, prefix):
        self.tiles = [P.ps([128, 512], F32) for _ in range(n)]
        self.keys = [(prefix, i) for i in range(n)]
        self.i = 0

    def next(self):
        t, k = self.tiles[self.i], self.keys[self.i]
        self.i = (self.i + 1) % len(self.tiles)
        return t, k


def build_token_prog(has_post, has_mlp, nxt):
    P = Prog()
    A = {}
    x_in = P.dram("x_in", [NBLK, 128, 8, TB], F32, "ExternalInput")
    write_x = has_post or has_mlp
    if write_x and nxt != 'final':
        x_out = P.dram("x_out", [NBLK, 128, 8, TB], F32, "ExternalOutput")
    elif write_x:
        x_out = P.dram("x_scr", [NBLK, 128, 8, TB], F32, "Internal")
    else:
        x_out = x_in
    if has_post:
        A['mix'] = P.dram("mix", [D, TPC], F32, "ExternalInput")
        A['w_out'] = P.dram("w_out", [D, D], F32, "ExternalInput")
    if has_mlp:
        A['g_ffn'] = P.dram("g_ffn", [128, 8], F32, "ExternalInput")
        A['w_up'] = P.dram("w_up", [D, DFF], F32, "ExternalInput")
        A['w_down'] = P.dram("w_down", [DFF, D], F32, "ExternalInput")
    A['g_nxt'] = P.dram("g_nxt", [128, 8], F32, "ExternalInput")
    if nxt == 'nsa':
        A['b_gate'] = P.dram("b_gate", [48, 1], F32, "ExternalInput")
    if nxt != 'final':
        ncols = 2048 if nxt == 'rg' else NSA_COLS
        A['w_in'] = P.dram("w_in", [D, ncols], F32, "ExternalInput")
        A['proj'] = P.dram("proj", [ncols, TPC], F32, "ExternalOutput")
    else:
        A['y_out'] = P.dram("y_out", [NBLK, 128, 8, TB], F32, "ExternalOutput")
    emit_token(P, NBLK, x_in, x_out, has_post, has_mlp, nxt, A)
    return P.build()


def emit_token(P, NBLK, x_in, x_out, has_post, has_mlp, nxt, A):
    write_x = has_post or has_mlp
    mix, w_out = A.get('mix'), A.get('w_out')
    g_ffn, w_up, w_down = A.get('g_ffn'), A.get('w_up'), A.get('w_down')
    g_nxt, b_gate, w_in, proj, y_out = A.get('g_nxt'), A.get('b_gate'), A.get('w_in'), A.get('proj'), A.get('y_out')
    ncols = 2048 if nxt == 'rg' else NSA_COLS
    P.phase_begin()
    WA = P.sb([128, 8 * DFF], BF16)
    WB = P.sb([128, 32 * D], BF16)
    WO = P.sb([128, 8 * D], BF16)
    xbuf = [P.sb([128, 8, TB], F32) for _ in range(2)]
    mbuf = [P.sb([128, 8, TB], BF16) for _ in range(2)]
    hb = P.sb([128, 8, TB], BF16)
    sq = P.sb([128, 8, TB], BF16)
    rs = P.sb([128, TB], F32)
    hb2 = P.sb([128, 8, TB], BF16)
    rs2 = P.sb([128, TB], F32)
    u2 = P.sb([128, 32, TB], BF16)
    rt = [P.sb([128, TB], F32) for _ in range(2)]
    st = [P.sb([128, TB], F32) for _ in range(4)]
    ones_bf = P.sb([128, 128], BF16)
    gf = P.sb([128, 8], F32)
    gn = P.sb([128, 8], F32)
    P.eps_tile = P.sb([128, 1], F32)
    bg = P.sb([48, 1], F32)
    psr = PsumRot(P, 6, 'ps')
    psn = P.ps([128, 512], F32)

    P.op('pool', lambda e: e.memset(ones_bf[:], 1.0), writes=['ones'])
    P.op('pool', lambda e: e.memset(P.eps_tile[:], EPS), writes=['epsc'])
    if has_mlp:
        P.dma('sp', gf[:], g_ffn, writes=['gvec_f'])
    P.dma('sp', gn[:], g_nxt, writes=['gvec_n'])
    if nxt == 'nsa':
        P.dma('sp', bg[:], b_gate, writes=['bg'])

    WAv = WA[:].rearrange("p (c f) -> p c f", c=8)
    WBv = WB[:].rearrange("p (c f) -> p c f", c=32)
    WOv = WO[:].rearrange("p (c f) -> p c f", c=8)
    if has_post:
        wsrc = w_out.rearrange("(c p) f -> p c f", p=128)
        for c in range(0, 8, 4):
            P.dma('pool', WOv[:, c:c + 4, :], wsrc[:, c:c + 4, :], writes=[('WO', c)])
        wo_keys = [('WO', 0), ('WO', 4)]
    if has_mlp:
        wsrc = w_up.rearrange("(c p) f -> p c f", p=128)
        for c in range(8):
            P.dma('pool', WAv[:, c, :], wsrc[:, c, :], writes=[('WA', c)])
        wsrc = w_down.rearrange("(c p) f -> p c f", p=128)
        for c in range(0, 32, 4):
            P.dma('pool', WBv[:, c:c + 4, :], wsrc[:, c:c + 4, :], writes=[('WB', c)])

    mixv = mix.rearrange("(c p) t -> p c t", p=128) if has_post else None

    def xkeys(slot):
        return [('xb', slot, c) for c in range(8)]

    if write_x:
        hbs = [hb, hb2]
        sqs = [sq, sq]
        rss = [rs, rs2]

        def load(blk):
            sl = blk % 2
            P.dma('sp', xbuf[sl][:], x_in[blk], writes=xkeys(sl), semkey=('xbsem', sl))
            if has_post:
                P.dma('pool', mbuf[sl][:], mixv[:, :, blk * TB:(blk + 1) * TB], writes=[('mb', sl)])

        def stageA(blk):
            slot = blk % 2
            xb = xbuf[slot]
            xk = xkeys(slot)
            if has_post:
                mb = mbuf[slot]
                for cc in range(8):
                    pt, pk = psr.next()
                    for kc in range(8):
                        P.op('pe', lambda e, pt=pt, kc=kc, cc=cc, mb=mb: e.matmul(
                            pt[:, :TB], lhsT=WOv[:, kc, cc * 128:(cc + 1) * 128], rhs=mb[:, kc, :], start=(kc == 0), stop=(kc == 7)),
                            reads=[('mb', slot)] + wo_keys, writes=[pk])
                    P.op('dve', lambda e, pt=pt, cc=cc, xb=xb: e.tensor_tensor(out=xb[:, cc, :], in0=pt[:, :TB], in1=xb[:, cc, :], op=ALU.add),
                         reads=[pk, xk[cc]], writes=[xk[cc]])
            if has_mlp:
                hk = [('hb', slot, c) for c in range(8)]
                emit_norm_k(P, xb, xk, gf, 'gvec_f', hbs[slot], hk, ones_bf, sqs[slot], psn, rss[slot], sfx=slot, sqk=0)

        def stageB(blk):
            slot = blk % 2
            hk = [('hb', slot, c) for c in range(8)]
            hcur = hbs[slot]
            for fc in range(32):
                pt, pk = psr.next()
                for kc in range(8):
                    P.op('pe', lambda e, pt=pt, kc=kc, fc=fc, hcur=hcur: e.matmul(
                        pt[:, :TB], lhsT=WAv[:, kc, fc * 128:(fc + 1) * 128], rhs=hcur[:, kc, :], start=(kc == 0), stop=(kc == 7)),
                        reads=[hk[kc], ('WA', kc)], writes=[pk])
                r = rt[fc % 2]
                P.op('act', lambda e, pt=pt, r=r: e.activation(out=r[:], in_=pt[:, :TB], func=AF.Relu),
                     reads=[pk], writes=[('rt', fc % 2)])
                eng = 'dve' if fc % 2 == 0 else 'pool'
                P.op(eng, lambda e, r=r, fc=fc: e.tensor_tensor(out=u2[:, fc, :], in0=r[:], in1=r[:], op=ALU.mult),
                     reads=[('rt', fc % 2)], writes=[('u2', fc)])

        def stageC(blk):
            slot = blk % 2
            xb = xbuf[slot]
            xk = xkeys(slot)
            if has_mlp:
                for cc in range(8):
                    pt, pk = psr.next()
                    for fc in range(32):
                        P.op('pe', lambda e, pt=pt, fc=fc, cc=cc: e.matmul(
                            pt[:, :TB], lhsT=WBv[:, fc, cc * 128:(cc + 1) * 128], rhs=u2[:, fc, :], start=(fc == 0), stop=(fc == 31)),
                            reads=[('u2', fc), ('WB', (fc // 4) * 4)], writes=[pk])
                    P.op('dve', lambda e, pt=pt, cc=cc, xb=xb: e.tensor_tensor(out=xb[:, cc, :], in0=pt[:, :TB], in1=xb[:, cc, :], op=ALU.add),
                         reads=[pk, xk[cc]], writes=[xk[cc]])
            P.dma('sp', x_out[blk], xb[:], reads=xk, writes=[('xout', blk)], semkey=('xosem', slot))

        load(0)
        if NBLK > 1:
            load(1)
        stageA(0)
        for blk in range(NBLK):
            if has_mlp:
                stageB(blk)
            if blk + 1 < NBLK:
                stageA(blk + 1)
            stageC(blk)
            if blk + 2 < NBLK:
                load(blk + 2)

    if nxt != 'final':
        nch = (ncols + 127) // 128
        WIv = WA[:, 0:8 * ncols].rearrange("p (c f) -> p c f", c=8)
        wsrc = w_in.rearrange("(c p) f -> p c f", p=128)
        for c in range(8):
            P.dma('pool', WIv[:, c, :], wsrc[:, c, :], writes=[('WA', c)])
    hbs3 = [hb, hb2]

    def p3_load(blk):
        sl = blk % 2
        P.dma('sp', xbuf[sl][:], x_out[blk], reads=[('xout', blk)] if write_x else [], writes=xkeys(sl), semkey=('xbsem', sl))

    def p3_norm(blk):
        sl = blk % 2
        emit_norm_k(P, xbuf[sl], xkeys(sl), gn, 'gvec_n', hbs3[sl], [('hb', sl, c) for c in range(8)], ones_bf, sq, psn, rs)

    if nxt == 'final':
        for blk in range(NBLK):
            slot = blk % 2
            xb = xbuf[slot]
            xk = xkeys(slot)
            p3_load(blk)
            emit_norm_k(P, xb, xk, gn, 'gvec_n', None, [('st', c % 4) for c in range(8)], ones_bf, sq, psn, rs,
                        outs=lambda c: st[c % 4][:],
                        after=lambda c: P.dma('sp', y_out[blk][:, c, :], st[c % 4][:], reads=[('st', c % 4)], writes=[('yout', blk, c)],
                                              semkey=('stsem', c % 4)))
    else:
        p3_load(0)
        if NBLK > 1:
            p3_load(1)
        p3_norm(0)
        for blk in range(NBLK):
            slot = blk % 2
            hcur = hbs3[slot]
            hk = [('hb', slot, c) for c in range(8)]
            for cc in range(nch):
                if cc == nch // 2 and blk + 1 < NBLK:
                    p3_norm(blk + 1)
                m = min(128, ncols - cc * 128)
                pt, pk = psr.next()
                for kc in range(8):
                    P.op('pe', lambda e, pt=pt, kc=kc, cc=cc, m=m, hcur=hcur: e.matmul(
                        pt[0:m, :TB], lhsT=WIv[:, kc, cc * 128:cc * 128 + m], rhs=hcur[:, kc, :], start=(kc == 0), stop=(kc == 7)),
                        reads=[hk[kc], ('WA', kc)], writes=[pk])
                s_ = st[cc % 4]
                sk = ('st', cc % 4)
                if nxt == 'rg' and cc < 8:
                    P.op('act', lambda e, pt=pt, s_=s_: e.activation(out=s_[:], in_=pt[:, :TB], func=AF.Gelu_apprx_tanh), reads=[pk], writes=[sk])
                elif nxt == 'nsa' and m < 128:
                    P.op('act', lambda e, pt=pt, s_=s_, m=m: e.activation(out=s_[0:m, :], in_=pt[0:m, :TB], func=AF.Sigmoid, bias=bg[:]),
                         reads=[pk, 'bg'], writes=[sk])
                else:
                    P.op('dve', lambda e, pt=pt, s_=s_: e.tensor_copy(s_[:], pt[:, :TB]), reads=[pk], writes=[sk])
                P.dma('sp', proj[cc * 128:cc * 128 + m, blk * TB:(blk + 1) * TB], s_[0:m, :], reads=[sk], writes=[('proj', blk, cc)],
                      semkey=('stsem', cc % 4))
            if blk + 2 < NBLK:
                p3_load(blk + 2)
    P.phase_end()


def emit_norm_k(P, xb, xk, g_sb, gkey, hb, hk, ones_bf, sq, psn, rs, sfx=0, sqk=0, outs=None, after=None):
    P.op('act', lambda e: e.activation(out=sq[:], in_=xb[:], func=AF.Square), reads=xk, writes=[('sq', sqk)])
    for c in range(8):
        P.op('pe', lambda e, c=c: e.matmul(psn[:, :TB], lhsT=ones_bf[:], rhs=sq[:, c, :], start=(c == 0), stop=(c == 7)),
             reads=[('sq', sqk), 'ones'], writes=['psn'])
    P.op('act', lambda e: e.activation(out=rs[:], in_=psn[:, :TB], func=AF.Sqrt, bias=P.eps_tile[:], scale=1.0 / D),
         reads=['psn', 'epsc'], writes=[('rs', sfx)])
    P.op('dve', lambda e: e.reciprocal(rs[:], rs[:]), reads=[('rs', sfx)], writes=[('rs', sfx)])
    for c in range(8):
        dst = outs(c) if outs is not None else hb[:, c, :]
        P.op('dve', lambda e, c=c, dst=dst: e.scalar_tensor_tensor(out=dst, in0=xb[:, c, :], scalar=g_sb[:, c:c + 1], in1=rs[:],
                                                                 op0=ALU.mult, op1=ALU.mult),
             reads=[xk[c], ('rs', sfx), gkey], writes=[hk[c]])
        if after is not None:
            after(c)


SEG = 2048
NSEG = S // SEG


def build_scan_prog():
    P = Prog()
    ysrc = P.dram("ysrc", [2, 256, S], F32, "ExternalInput")
    xpsrc = P.dram("xpsrc", [2, 256, S], F32, "ExternalInput")
    cpar = P.dram("cpar", [2, 128, 2, 8], F32, "ExternalInput")
    wa = P.dram("wa", [2, 256, 256], F32, "ExternalInput")
    wx = P.dram("wx", [2, 256, 256], F32, "ExternalInput")
    mixo = P.dram("mixo", [2, 256, S], F32, "ExternalOutput")
    emit_scan(P, 2, lambda u, j, a, b: ysrc[u, j * 128:(j + 1) * 128, a:b], lambda u, j, a, b: xpsrc[u, j * 128:(j + 1) * 128, a:b],
              cpar, wa, wx, lambda u, j, a, b: mixo[u, j * 128:(j + 1) * 128, a:b])
    return P.build()


def emit_scan(P, NU, ysrc, xpsrc, cpar, wa, wx, mixo):
    P.phase_begin()
    cp = P.sb([128, NU, 2, 8], F32)
    cst = P.sb([128, NU, 2, 4], F32)
    wab = P.sb([128, NU, 2, 256], BF16)
    wxb = P.sb([128, NU, 2, 256], BF16)
    xp = [P.sb([128, 3 + SEG], F32) for _ in range(2)]
    xb = [P.sb([128, SEG], F32) for _ in range(2)]
    xbb = [P.sb([128, SEG], BF16) for _ in range(2)]
    rt = P.sb([128, SEG], F32)
    it = P.sb([128, SEG], F32)
    at = P.sb([128, SEG], F32)
    mt = P.sb([128, SEG], F32)
    ut = P.sb([128, SEG], F32)
    ht = P.sb([128, SEG], F32)
    yt = P.sb([128, SEG], F32)
    ot = P.sb([128, SEG], F32)
    hcar = P.sb([128, 2], F32)
    psr = PsumRot(P, 6, 'ps')

    for u in range(NU):
        P.dma('sp', cp[:, u], cpar[u], writes=[('cp', u)])
        P.dma('pool', wab[:, u], wa[u].rearrange("(j p) o -> p j o", p=128), writes=[('wab', u)])
        P.dma('pool', wxb[:, u], wx[u].rearrange("(j p) o -> p j o", p=128), writes=[('wxb', u)])
        for j in range(2):
            lam = cp[:, u, j, 7:8]
            P.op('act', lambda e, u=u, j=j, lam=lam: e.activation(out=cst[:, u, j, 0:1], in_=lam, func=AF.Exp, scale=-1.0),
                 reads=[('cp', u)], writes=[('cst', u, j)])
            P.op('act', lambda e, u=u, j=j: e.activation(out=cst[:, u, j, 1:2], in_=cst[:, u, j, 0:1], func=AF.Ln, bias=1.0),
                 reads=[('cst', u, j)], writes=[('cst', u, j)])
            P.op('dve', lambda e, u=u, j=j: e.tensor_scalar(out=cst[:, u, j, 2:3], in0=cst[:, u, j, 1:2], scalar1=-8.0, scalar2=None, op0=ALU.mult),
                 reads=[('cst', u, j)], writes=[('cst', u, j)])
            P.op('dve', lambda e, u=u, j=j: e.tensor_scalar(out=cst[:, u, j, 3:4], in0=cst[:, u, j, 1:2], scalar1=-16.0, scalar2=None, op0=ALU.mult),
                 reads=[('cst', u, j)], writes=[('cst', u, j)])

    for u in range(NU):
        for s in range(NSEG):
            t0 = s * SEG
            for j in range(2):
                rows = slice(j * 128, (j + 1) * 128)
                if s == 0:
                    P.op('pool', lambda e, j=j: e.memset(xp[j][:, 0:3], 0.0), writes=[('xp', j)])
                    P.dma('sp', xp[j][:, 3:3 + SEG], xpsrc(u, j, 0, SEG), writes=[('xp', j)], semkey=('xpsem', j))
                else:
                    P.dma('sp', xp[j][:, :], xpsrc(u, j, t0 - 3, t0 + SEG), writes=[('xp', j)], semkey=('xpsem', j))
                cw = lambda k, u=u, j=j: cp[:, u, j, k:k + 1]
                P.op('dve', lambda e, j=j, cw=cw: e.tensor_scalar(out=xb[j][:], in0=xp[j][:, 3:3 + SEG], scalar1=cw(3), scalar2=cw(4),
                                                                 op0=ALU.mult, op1=ALU.add),
                     reads=[('xp', j), ('cp', u)], writes=[('xb', j)])
                for k in (2, 1, 0):
                    P.op('dve', lambda e, j=j, k=k, cw=cw: e.scalar_tensor_tensor(out=xb[j][:], in0=xp[j][:, k:k + SEG], scalar=cw(k), in1=xb[j][:],
                                                                                 op0=ALU.mult, op1=ALU.add),
                         reads=[('xp', j), ('cp', u), ('xb', j)], writes=[('xb', j)])
                P.op('act', lambda e, j=j: e.copy(out=xbb[j][:], in_=xb[j][:]), reads=[('xb', j)], writes=[('xbb', j)])
            for j in range(2):
                rows = slice(j * 128, (j + 1) * 128)
                P.dma('sp', yt[:], ysrc(u, j, t0, t0 + SEG), writes=['yt'])
                for (wt, wkey, dst, dkey, bk) in ((wab, 'wab', rt, 'rt', 5), (wxb, 'wxb', it, 'it', 6)):
                    for tb in range(SEG // 512):
                        pt, pk = psr.next()
                        for jin in range(2):
                            P.op('pe', lambda e, pt=pt, wt=wt, jin=jin, j=j, tb=tb, u=u: e.matmul(
                                pt[:], lhsT=wt[:, u, jin, j * 128:(j + 1) * 128], rhs=xbb[jin][:, tb * 512:(tb + 1) * 512],
                                start=(jin == 0), stop=(jin == 1)),
                                reads=[('xbb', jin), (wkey, u)], writes=[pk])
                        P.op('act', lambda e, pt=pt, dst=dst, tb=tb, u=u, j=j, bk=bk: e.activation(
                            out=dst[:, tb * 512:(tb + 1) * 512], in_=pt[:], func=AF.Sigmoid, bias=cp[:, u, j, bk:bk + 1]),
                            reads=[pk, ('cp', u)], writes=[dkey])
                P.op('act', lambda e, u=u, j=j: e.activation(out=at[:], in_=rt[:], func=AF.Exp, scale=cst[:, u, j, 2:3]),
                     reads=['rt', ('cst', u, j)], writes=['at'])
                P.op('act', lambda e, u=u, j=j: e.activation(out=mt[:], in_=rt[:], func=AF.Exp, scale=cst[:, u, j, 3:4]),
                     reads=['rt', ('cst', u, j)], writes=['mt'])
                P.op('dve', lambda e: e.tensor_scalar(out=mt[:], in0=mt[:], scalar1=-1.0, scalar2=1.0, op0=ALU.mult, op1=ALU.add),
                     reads=['mt'], writes=['mt'])
                P.op('dve', lambda e: e.tensor_scalar_max(out=mt[:], in0=mt[:], scalar1=1e-20), reads=['mt'], writes=['mt'])
                P.op('act', lambda e: e.activation(out=mt[:], in_=mt[:], func=AF.Sqrt), reads=['mt'], writes=['mt'])
                P.op('pool', lambda e, j=j: e.tensor_tensor(out=ut[:], in0=it[:], in1=xb[j][:], op=ALU.mult),
                     reads=['it', ('xb', j)], writes=['ut'])
                P.op('dve', lambda e: e.tensor_tensor(out=ut[:], in0=ut[:], in1=mt[:], op=ALU.mult), reads=['ut', 'mt'], writes=['ut'])
                init = 0.0 if s == 0 else hcar[:, j:j + 1]
                P.op('dve', lambda e, init=init: e.tensor_tensor_scan(ht[:], at[:], ut[:], init, ALU.mult, ALU.add),
                     reads=['at', 'ut', ('hcar', j)], writes=['ht'])
                P.op('dve', lambda e, j=j: e.tensor_copy(hcar[:, j:j + 1], ht[:, SEG - 1:SEG]), reads=['ht'], writes=[('hcar', j)])
                P.op('pool', lambda e: e.tensor_tensor(out=ot[:], in0=ht[:], in1=yt[:], op=ALU.mult), reads=['ht', 'yt'], writes=['ot'])
                P.dma('sp', mixo(u, j, t0, t0 + SEG), ot[:], reads=['ot'], writes=[('mixo', u, j, s)], semkey=('osem',))
    P.phase_end()


_PROGS = {}


def _prog(key, fn):
    if key not in _PROGS:
        _PROGS[key] = fn()
    return _PROGS[key]


def _run(nc, in_maps):
    res = run_bass_kernel_spmd(nc, in_maps, core_ids=list(range(NCORES)))
    return res.results


def to_xl(xc):
    return np.ascontiguousarray(xc.reshape(NBLK, TB, 8, 128).transpose(0, 3, 2, 1))


def from_xl(xl):
    return np.ascontiguousarray(xl.transpose(0, 3, 2, 1).reshape(TPC, D))


def gvec(g):
    return np.ascontiguousarray(g.reshape(8, 128).T)


def run_token(xls, has_post, has_mlp, nxt, mixs=None, w_out=None, g_ffn=None, w_up=None, w_down=None, g_nxt=None, w_in=None, b_gate=None):
    nc = _prog(('tok', has_post, has_mlp, nxt), lambda: build_token_prog(has_post, has_mlp, nxt))
    maps = []
    for c in range(NCORES):
        m = {"x_in": xls[c], "g_nxt": gvec(g_nxt)}
        if has_post:
            m["mix"] = mixs[c]
            m["w_out"] = w_out
        if has_mlp:
            m["g_ffn"] = gvec(g_ffn)
            m["w_up"] = w_up
            m["w_down"] = w_down
        if nxt != 'final':
            m["w_in"] = w_in
        if nxt == 'nsa':
            m["b_gate"] = np.ascontiguousarray(b_gate.reshape(48, 1))
        maps.append(m)
    return _run(nc, maps)


def run_scan(projs, conv_w, conv_b, w_a, b_a, w_x, b_x, lam):
    nc = _prog(('scan',), build_scan_prog)
    maps = []
    for c in range(NCORES):
        b, hh = c // 2, c % 2
        ys, xs, cps, was, wxs = [], [], [], [], []
        for u in range(2):
            n = 2 * hh + u
            ch = slice(n * 256, (n + 1) * 256)
            ys.append(np.concatenate([projs[2 * b][ch], projs[2 * b + 1][ch]], axis=1))
            ch2 = slice(1024 + n * 256, 1024 + (n + 1) * 256)
            xs.append(np.concatenate([projs[2 * b][ch2], projs[2 * b + 1][ch2]], axis=1))
            par = np.stack([conv_w[0, ch], conv_w[1, ch], conv_w[2, ch], conv_w[3, ch], conv_b[ch], b_a[ch], b_x[ch], lam[ch]], axis=-1)
            cps.append(par.reshape(2, 128, 8).transpose(1, 0, 2))
            was.append(w_a[n])
            wxs.append(w_x[n])
        maps.append({"ysrc": np.ascontiguousarray(np.stack(ys)), "xpsrc": np.ascontiguousarray(np.stack(xs)),
                     "cpar": np.ascontiguousarray(np.stack(cps)), "wa": np.ascontiguousarray(np.stack(was)),
                     "wx": np.ascontiguousarray(np.stack(wxs))})
    res = _run(nc, maps)
    mixs = []
    for c in range(NCORES):
        b, hh = c // 2, c % 2
        parts = []
        for n in range(4):
            src = res[2 * b + n // 2]["mixo"][n % 2]
            parts.append(src[:, hh * TPC:(hh + 1) * TPC])
        mixs.append(np.ascontiguousarray(np.concatenate(parts, axis=0)))
    return mixs


QC = 512
NQC = S // QC
NCMP = 511
SCALE = 0.125


def att_tables(g):
    slopes = np.exp2(-8.0 * (np.arange(1, 17, dtype=np.float64)) / 16.0).reshape(4, 4)[g]
    tq = np.arange(QC, dtype=np.float64)
    aug = (-slopes[:, None] * tq[None, :] / SCALE).astype(np.float32)
    kk = np.arange(128, dtype=np.float64)
    dl = (np.arange(67, dtype=np.float64) - 63.0) * 128.0
    bias_sw = (slopes[None, :, None] * (dl[None, None, :] + kk[:, None, None])).astype(np.float32)
    kt = np.arange(4, dtype=np.float64)
    qc = np.arange(NQC, dtype=np.float64)
    pos = 16.0 * (128.0 * kt[None, :, None] + kk[:, None, None]) + 31.0 - QC * qc[None, None, :]
    bias_c = (slopes[None, :, None, None] * pos[:, None, :, :]).astype(np.float32)
    return aug, bias_sw, bias_c


def att_consts():
    n = np.arange(512)
    j = np.arange(128)
    ov = np.minimum(n[:, None] * 16 + 32, j[None, :] * 64 + 64) - np.maximum(n[:, None] * 16, j[None, :] * 64)
    cm = (np.maximum(ov, 0) / 16.0).astype(np.float32)
    cm[511] = 0.0
    cmpmap = np.ascontiguousarray(cm.reshape(4, 128, 128).transpose(1, 0, 2))
    jj = np.arange(128)[:, None, None]
    ktt = np.arange(64)[None, :, None]
    kk = np.arange(128)[None, None, :]
    eexp = (jj == 2 * ktt + kk // 64).astype(np.float32)
    kq = np.arange(128)[:, None, None]
    tq = np.arange(QC)[None, None, :]
    dw = (np.arange(8)[None, :, None] - 4) * 128
    d = tq - kq - dw
    winmask = ((d >= 0) & (d < 512)).astype(np.float32)
    dc = np.arange(4)[None, :, None] * 128
    causal = ((tq - kq - dc) >= 0).astype(np.float32)
    tt = np.arange(128)[:, None]
    w = np.arange(256)[None, :]
    jr = w - 126
    c = (tt >= 64).astype(np.int64)
    valid = jr <= c
    forced = (jr == c) | (jr == c - 1)
    tb = (300.0 - w) * 1e-35
    validW = valid.astype(np.float32)
    addW = np.where(valid, 1e4 * forced + tb, -1.0).astype(np.float32)
    ident = np.eye(128, dtype=np.float32)
    return dict(cmpmap=cmpmap, eexp=eexp, winmask=winmask, causal=causal, validW=validW, addW=addW, ident=ident)


def build_att_prog(dbg=False, nunits=2, nqc=NQC):
    P = Prog()
    A = {}
    qT = P.dram("qT", [2, 4, 64, S], F32, "ExternalInput")
    kT = P.dram("kT", [2, 3, 64, S], F32, "ExternalInput")
    vcT = P.dram("vcT", [2, 64, S], F32, "ExternalInput")
    vtok = P.dram("vtok", [2, 2, S, 64], F32, "ExternalInput")
    gates = P.dram("gates", [2, S, 12], F32, "ExternalInput")
    for nm, shp in (("w1k", [2048, 64]), ("w2k", [64, 64]), ("w1v", [2048, 64]), ("w2v", [64, 64]), ("pek", [128, 16]), ("pev", [128, 16]),
                    ("augrow", [2, 1, 4, QC]), ("bias_sw", [2, 128, 4, 67]), ("bias_c", [2, 128, 4, 4, NQC]), ("cmpmap", [128, 4, 128]),
                    ("eexp", [128, 64, 128]), ("winmask", [128, 8, QC]), ("causal", [128, 4, QC]), ("validW", [128, 256]), ("addW", [128, 256]),
                    ("ident", [128, 128])):
        A[nm] = P.dram(nm, shp, F32, "ExternalInput")
    o_d = P.dram("o", [2, 3, S, 256] if dbg else [2, S, 256], F32, "ExternalOutput")
    A['q_src'] = lambda u, q0: qT[u, :, :, q0:q0 + QC].rearrange("h d t -> d h t")
    A['k_src'] = lambda u, i: kT[u, i]
    A['vc_src'] = lambda u: vcT[u]
    A['vtok_src'] = lambda u, i: vtok[u, i]
    A['gates_src'] = lambda u, q0: gates[u, q0:q0 + QC, :]
    A['o_dst'] = (lambda u, br, q0: o_d[u, br, q0:q0 + QC, :]) if dbg else (lambda u, br, q0: o_d[u, q0:q0 + QC, :])
    emit_att(P, A, dbg=dbg, nunits=nunits, nqc=nqc, fused=False)
    return P.build()


def emit_att(P, A, dbg=False, nunits=2, nqc=NQC, fused=False, groups=None):
    P.phase_begin()
    ALIBI_CUT = 100.0

    def dcut(u, h):
        if groups is None:
            return 1e30
        return ALIBI_CUT / (2.0 ** (-(4 * groups[u] + h + 1) / 2.0))

    NU = nunits
    w1k, w2k, w1v, w2v, pek, pev = A['w1k'], A['w2k'], A['w1v'], A['w2v'], A['pek'], A['pev']
    augrow, bias_sw_d, bias_c_d = A['augrow'], A['bias_sw'], A['bias_c']
    cmpmap_d, eexp_d, winmask_d, causal_d = A['cmpmap'], A['eexp'], A['winmask'], A['causal']
    validW_d, addW_d, ident_d = A['validW'], A['addW'], A['ident']

    zk = P.sb([64, S], BF16)
    zv = P.sb([64, S], BF16)
    ksA = P.sb([65, S], BF16)
    kwA = P.sb([65, S], BF16)
    vsA = P.sb([128, 64, 65], BF16)
    vwA = P.sb([128, 64, 65], BF16)
    kcA = P.sb([65, 512], BF16)
    vcA = P.sb([128, 4, 65], BF16)
    eexp = P.sb([128, 64, 128], BF16)
    cmpb = P.sb([128, 4, 128], BF16)
    winm = P.sb([128, 8, QC], BF16)
    caus = P.sb([128, 4, QC], BF16)
    validW = P.sb([128, 256], F32)
    addW = P.sb([128, 256], F32)
    identf = P.sb([128, 128], F32)
    bsw = P.sb([128, NU, 4, 67], F32)
    bc = P.sb([128, NU, 4, 4, NQC], F32)
    w1d = [P.sb([64, 32, 64], BF16) for _ in range(2)]
    w1p = [P.sb([128, 16, 64], BF16) for _ in range(2)]
    w2b = [P.sb([64, 64], BF16) for _ in range(2)]
    pef = [P.sb([128, 16], BF16) for _ in range(2)]
    cb = P.sb([64, 2], F32)
    h1 = P.sb([64, 512], BF16)
    qbuf = [P.sb([65, 4, QC], BF16) for _ in range(2)]
    gt = [P.sb([128, 4, 12], F32) for _ in range(2)]
    acc = [[P.sb([128, 4, 256], F32) for _ in range(3 if dbg else 1)] for _ in range(2)]
    PTc = [P.sb([128, QC], BF16) for _ in range(4)]
    NPT = 9
    PT = [P.sb([128, QC], BF16) for _ in range(NPT)]
    PTm = [P.sb([128, QC], BF16) for _ in range(NPT)]
    msb = [P.sb([128, QC], BF16) for _ in range(3)]
    osb = [P.sb([65, QC], F32) for _ in range(4)]
    rl = [P.sb([128, 4], F32) for _ in range(4)]
    ff = [P.sb([128, 4], F32) for _ in range(4)]
    impacc = P.sb([128, 4, 128], F32)
    sc = P.sb([128, 128], F32)
    sc2s = [P.sb([128, 128], F32) for _ in range(4)]
    t8 = P.sb([128, 16], F32)
    selT = P.sb([128, QC], BF16)
    pso = [P.ps([128, 512], F32) for _ in range(4)]
    pss = PsumRot(P, 2, 'pss')
    psM = P.ps([128, 512], F32)
    pss3 = PsumRot(P, 0, 'pss3')
    pss3.tiles = pss.tiles + [psM]
    pss3.keys = pss.keys + ['psM']
    SK = 7
    psx = P.ps([128, 512], F32)
    if fused:
        vstage = P.sb([64, 2048], F32)
        gtf = [P.sb([12, QC], F32) for _ in range(2)]
        oT = [P.sb([128, QC], F32) for _ in range(2)]

    P.dma('pool', eexp[:], eexp_d, writes=['eexp'])
    P.dma('pool', cmpb[:], cmpmap_d, writes=['cmpb'])
    P.dma('pool', winm[:], winmask_d, writes=['winm'])
    P.dma('pool', caus[:], causal_d, writes=['caus'])
    P.dma('sp', validW[:], validW_d, writes=['validW'])
    P.dma('sp', addW[:], addW_d, writes=['addW'])
    P.dma('sp', identf[:], ident_d, writes=['identf'])
    for u in range(NU):
        P.dma('sp', bsw[:, u], bias_sw_d[u], writes=[('bsw', u)])
        P.dma('sp', bc[:, u], bias_c_d[u], writes=[('bc', u)])
    for i, (w1, w2, pe) in enumerate(((w1k, w2k, pek), (w1v, w2v, pev))):
        P.dma('pool', w1d[i][:], w1.rearrange("(l d) o -> d l o", d=64), writes=[('w1d', i)])
        P.dma('pool', w1p[i][:], w1.rearrange("(j p) o -> p j o", p=128), writes=[('w1p', i)])
        P.dma('pool', w2b[i][:], w2, writes=[('w2b', i)])
        P.dma('pool', pef[i][:], pe, writes=[('pef', i)])
    P.op('pool', lambda e: e.memset(ksA[64:65, :], 1.0), writes=['ksA1'])
    P.op('pool', lambda e: e.memset(kwA[64:65, :], 1.0), writes=['kwA1'])
    P.op('pool', lambda e: e.memset(kcA[64:65, :], 1.0), writes=['kcA1'])
    P.op('pool', lambda e: e.memset(vsA[:, :, 64:65], 1.0), writes=['vsA1'])
    P.op('pool', lambda e: e.memset(vwA[:, :, 64:65], 1.0), writes=['vwA1'])
    P.op('pool', lambda e: e.memset(vcA[:, :, 64:65], 1.0), writes=['vcA1'])
    P.op('pool', lambda e: e.memset(h1[:], 0.0), writes=['h1'])
    for i in range(2):
        for j in range(16):
            P.op('pe', lambda e, i=i, j=j: e.matmul(psx[0:64, i:i + 1], lhsT=w1p[i][:, j, :], rhs=pef[i][:, j:j + 1], start=(j == 0), stop=(j == 15)),
                 reads=[('w1p', i), ('pef', i)], writes=['psx'])
        P.op('act', lambda e, i=i: e.copy(out=cb[:, i:i + 1], in_=psx[0:64, i:i + 1]), reads=['psx'], writes=[('cb', i)])

    cnt = {'pt': 0, 'ptm': 0, 'msb': 0, 'osb': 0}

    def do_unit(u):
        P.dma('pool', zk[:], A['k_src'](u, 0), writes=['zk'])
        P.dma('pool', zv[:], A['vc_src'](u), writes=['zv'])
        P.dma('pool', ksA[0:64, :], A['k_src'](u, 1), writes=['ksA'])
        P.dma('pool', kwA[0:64, :], A['k_src'](u, 2), writes=['kwA'])
        if not fused:
            P.dma('pool', vsA[:, :, 0:64], A['vtok_src'](u, 0).rearrange("(kt p) d -> p kt d", p=128), writes=['vsA'])
            P.dma('pool', vwA[:, :, 0:64], A['vtok_src'](u, 1).rearrange("(kt p) d -> p kt d", p=128), writes=['vwA'])
        else:
            for i, (vA, vkey) in enumerate(((vsA, 'vsA'), (vwA, 'vwA'))):
                for pc in range(4):
                    P.dma('sp', vstage[:], A['vT_src'](u, i)[:, pc * 2048:(pc + 1) * 2048], writes=['vstage'])
                    for half in range(2):
                        for t8i in range(8):
                            c0 = half * 1024 + t8i * 128
                            P.op('pe', lambda e, t8i=t8i, c0=c0: e.transpose(psx[:, t8i * 64:(t8i + 1) * 64], vstage[0:64, c0:c0 + 128], identf[0:64, 0:64]),
                                 reads=['vstage', 'identf'], writes=['psx'])
                        kt0 = pc * 16 + half * 8
                        P.op('act', lambda e, vA=vA, kt0=kt0: e.copy(out=vA[:, kt0:kt0 + 8, 0:64], in_=psx[:].rearrange("p (a d) -> p a d", d=64)),
                             reads=['psx'], writes=[vkey])
        for slot in range(2):
            P.dma('pool', qbuf[slot][64:65, :, :], augrow[u], writes=[('qaug', slot)])
        for i, z in enumerate((zk, zv)):
            zkey = 'zk' if i == 0 else 'zv'
            pt, pk = pss.next()
            for l in range(32):
                P.op('pe', lambda e, pt=pt, i=i, l=l, z=z: e.matmul(pt[0:64, 0:NCMP], lhsT=w1d[i][:, l, :], rhs=z[:, l:l + 16 * (NCMP - 1) + 1:16],
                                                                  start=(l == 0), stop=(l == 31)),
                     reads=[zkey, ('w1d', i)], writes=[pk])
            P.op('act', lambda e, pt=pt, i=i: e.activation(out=h1[:, 0:NCMP], in_=pt[0:64, 0:NCMP], func=AF.Gelu_apprx_tanh, bias=cb[:, i:i + 1]),
                 reads=[pk, ('cb', i)], writes=['h1'])
            if i == 0:
                pt2, pk2 = pss.next()
                P.op('pe', lambda e, pt2=pt2: e.matmul(pt2[0:64, 0:NCMP], lhsT=w2b[0][:], rhs=h1[:, 0:NCMP], start=True, stop=True),
                     reads=['h1', ('w2b', 0)], writes=[pk2])
                P.op('act', lambda e, pt2=pt2: e.copy(out=kcA[0:64, 0:NCMP], in_=pt2[0:64, 0:NCMP]), reads=[pk2], writes=['kcA'])
            else:
                for kt in range(4):
                    nk = 127 if kt == 3 else 128
                    P.op('pe', lambda e, kt=kt, nk=nk: e.matmul(psx[0:nk, 0:64], lhsT=h1[:, kt * 128:kt * 128 + nk], rhs=w2b[1][:], start=True, stop=True),
                         reads=['h1', ('w2b', 1)], writes=['psx'])
                    P.op('act', lambda e, kt=kt, nk=nk: e.copy(out=vcA[0:nk, kt, 0:64], in_=psx[0:nk, 0:64]), reads=['psx'], writes=['vcA'])

    def do_chunk(u, qc):
        if True:
            q0 = qc * QC
            slot = qc % 2
            qa = qbuf[slot]
            def chunk_loads(qq):
                sl = qq % 2
                P.dma('pool', qbuf[sl][0:64, :, :], A['q_src'](u, qq * QC), writes=[('qa', sl)])
                if not fused:
                    P.dma('sp', gt[sl][:], A['gates_src'](u, qq * QC).rearrange("(ts p) k -> p ts k", p=128), writes=[('gt', sl)])
                else:
                    P.dma('sp', gtf[sl][:], A['gT_src'](u, qq * QC), writes=[('gtf', sl)])
            if qc == 0:
                chunk_loads(0)
            if qc + 1 < nqc:
                chunk_loads(qc + 1)
            if fused:
                for ts in range(4):
                    P.op('pe', lambda e, ts=ts: e.transpose(psx[:, ts * 12:(ts + 1) * 12], gtf[slot][0:12, ts * 128:(ts + 1) * 128], identf[0:12, 0:12]),
                         reads=[('gtf', slot), 'identf'], writes=['psx'])
                P.op('act', lambda e: e.copy(out=gt[slot][:], in_=psx[:, 0:48].rearrange("p (a k) -> p a k", k=12)), reads=['psx'], writes=[('gt', slot)])
            qkeys = [('qa', slot), ('qaug', slot)]

            def epilogue(h, br, with_imp=False):
                r = cnt['osb'] % 4
                cnt['osb'] += 1
                ob = osb[r]
                if br == 0:
                    pe_, pek_ = psx, 'psx'
                else:
                    pe_, pek_ = [(psx, 'psx'), (pss.tiles[0], pss.keys[0]), (pss.tiles[1], pss.keys[1]), (psM, 'psM')][h]
                P.op('act', lambda e, ob=ob, h=h: e.copy(out=ob[:], in_=pso[h][0:65, :]), reads=[('pso', h)], writes=[('osb', r)])
                for ts in range(4):
                    P.op('pe', lambda e, ob=ob, ts=ts, pe_=pe_: e.transpose(pe_[:, ts * 65:(ts + 1) * 65], ob[0:65, ts * 128:(ts + 1) * 128], identf[0:65, 0:65]),
                         reads=[('osb', r), 'identf'], writes=[pek_])
                rr, fr = rl[r], ff[r]
                P.op('dve', lambda e, rr=rr, pe_=pe_: e.tensor_scalar_max(out=rr[:], in0=pe_[:, 64:64 + 65 * 4:65], scalar1=1e-30), reads=[pek_], writes=[('rl', r)])
                P.op('dve', lambda e, rr=rr: e.reciprocal(rr[:], rr[:]), reads=[('rl', r)], writes=[('rl', r)])
                gi = h * 3 + br
                P.op('dve', lambda e, rr=rr, fr=fr, gi=gi: e.tensor_tensor(out=fr[:], in0=rr[:], in1=gt[slot][:, :, gi], op=ALU.mult),
                     reads=[('rl', r), ('gt', slot)], writes=[('ff', r)])
                for ts in range(4):
                    dst = acc[slot][br if dbg else 0][:, ts, h * 64:(h + 1) * 64]
                    src = pe_[:, ts * 65:ts * 65 + 64]
                    if br == 0 or dbg:
                        P.op('dve', lambda e, dst=dst, src=src, fr=fr, ts=ts: e.tensor_scalar(out=dst, in0=src, scalar1=fr[:, ts:ts + 1], scalar2=None, op0=ALU.mult),
                             reads=[pek_, ('ff', r)], writes=[('acc', slot, br if dbg else 0, h)])
                    else:
                        P.op('dve', lambda e, dst=dst, src=src, fr=fr, ts=ts: e.scalar_tensor_tensor(out=dst, in0=src, scalar=fr[:, ts:ts + 1], in1=dst,
                                                                                                  op0=ALU.mult, op1=ALU.add),
                             reads=[pek_, ('ff', r), ('acc', slot, 0, h)], writes=[('acc', slot, 0, h)])
                if with_imp:
                    for ts in range(4):
                        dst = impacc[:, ts, :]
                        src = psM[:, ts * 128:(ts + 1) * 128]
                        if h == 0:
                            P.op('dve', lambda e, dst=dst, src=src, rr=rr, ts=ts: e.tensor_scalar(out=dst, in0=src, scalar1=rr[:, ts:ts + 1], scalar2=None, op0=ALU.mult),
                                 reads=['psM', ('rl', r)], writes=['impacc'])
                        else:
                            P.op('dve', lambda e, dst=dst, src=src, rr=rr, ts=ts: e.scalar_tensor_tensor(out=dst, in0=src, scalar=rr[:, ts:ts + 1], in1=dst,
                                                                                                      op0=ALU.mult, op1=ALU.add),
                                 reads=['psM', ('rl', r), 'impacc'], writes=['impacc'])

            ktc = min(3, (32 * qc + 30) // 128)
            for h in range(4):
                k0h = 0
                while k0h < ktc and q0 - (16 * (128 * k0h + 127) + 31) > dcut(u, h):
                    k0h += 1
                for kt in range(k0h, ktc + 1):
                    nk = 127 if kt == 3 else 128
                    pt, pk = pss.next()
                    P.op('pe', lambda e, pt=pt, kt=kt, nk=nk, h=h: e.matmul(pt[0:nk, :], lhsT=kcA[0:65, kt * 128:kt * 128 + nk], rhs=qa[0:65, h, :], start=True, stop=True),
                         reads=['kcA', 'kcA1'] + qkeys, writes=[pk])
                    P.op('act', lambda e, pt=pt, kt=kt, nk=nk, h=h: e.activation(out=PTc[kt][0:nk, :], in_=pt[0:nk, :], func=AF.Exp,
                                                                              bias=bc[0:nk, u, h, kt, qc:qc + 1], scale=SCALE),
                         reads=[pk, ('bc', u)], writes=[('PTc', kt)])
                    if q0 - 16 * (128 * kt + nk - 1) - 31 < 0:
                        P.op('pool', lambda e, kt=kt, nk=nk: e.affine_select(out=PTc[kt][0:nk, :], in_=PTc[kt][0:nk, :], pattern=[[1, QC]],
                                                                           compare_op=ALU.is_ge, fill=0.0, base=q0 - 2048 * kt - 31, channel_multiplier=-16),
                             reads=[('PTc', kt)], writes=[('PTc', kt)])
                for kt in range(k0h, ktc + 1):
                    nk = 127 if kt == 3 else 128
                    P.op('pe', lambda e, kt=kt, nk=nk, h=h, k0h=k0h: e.matmul(pso[h][0:65, :], lhsT=vcA[0:nk, kt, 0:65], rhs=PTc[kt][0:nk, :],
                                                                  start=(kt == k0h), stop=(kt == ktc)),
                         reads=[('PTc', kt), 'vcA', 'vcA1'], writes=[('pso', h)])
                for ts in range(4):
                    for kt in range(k0h, ktc + 1):
                        nk = 127 if kt == 3 else 128
                        P.op('pe', lambda e, kt=kt, nk=nk, ts=ts, k0h=k0h: e.matmul(psM[:, ts * 128:(ts + 1) * 128], lhsT=PTc[kt][0:nk, ts * 128:(ts + 1) * 128],
                                                                        rhs=cmpb[0:nk, kt, :], start=(kt == k0h), stop=(kt == ktc)),
                             reads=[('PTc', kt), 'cmpb'], writes=['psM'])
                epilogue(h, 0, with_imp=True)

            for ts in range(4):
                w0 = 126 - 2 * (4 * qc + ts)
                P.op('dve', lambda e, ts=ts, w0=w0: e.tensor_tensor(out=sc[:], in0=impacc[:, ts, :], in1=validW[:, w0:w0 + 128], op=ALU.mult),
                     reads=['impacc', 'validW'], writes=['sc'])
                P.op('dve', lambda e, w0=w0, ts=ts: e.tensor_tensor(out=sc[:], in0=sc[:], in1=addW[:, w0:w0 + 128], op=ALU.add),
                     reads=['sc', 'addW'], writes=['sc'])
                P.op('dve', lambda e, ts=ts: e.tensor_scalar_add(out=sc[:, 0:1], in0=sc[:, 0:1], scalar1=1e4), reads=['sc'], writes=['sc'])
                P.op('dve', lambda e, ts=ts: e.max(t8[:, 0:8], sc[:]), reads=['sc'], writes=['t8'])
                P.op('dve', lambda e, ts=ts: e.match_replace(sc2s[ts][:], t8[:, 0:8], sc[:], -1e30), reads=['sc', 't8'], writes=[('sc2', ts)])
                P.op('dve', lambda e, ts=ts: e.max(t8[:, 8:16], sc2s[ts][:]), reads=[('sc2', ts)], writes=['t8'])
                P.op('dve', lambda e, ts=ts: e.tensor_scalar(out=sc2s[ts][:], in0=sc[:], scalar1=t8[:, 15:16], scalar2=None, op0=ALU.is_ge), reads=['sc', 't8'], writes=[('sc2', ts)])
                P.op('dve', lambda e, w0=w0, ts=ts: e.tensor_tensor(out=sc2s[ts][:], in0=sc2s[ts][:], in1=validW[:, w0:w0 + 128], op=ALU.mult), reads=[('sc2', ts), 'validW'], writes=[('sc2', ts)])

            kts = 4 * qc + 3

            def expand(kt):
                mi = cnt['msb'] % 3
                cnt['msb'] += 1
                mt_ = msb[mi]
                P.op('pe', lambda e, kt=kt: e.matmul(psx[:], lhsT=eexp[:, kt, :], rhs=selT[:], start=True, stop=True), reads=['eexp', 'selT'], writes=['psx'])
                P.op('act', lambda e, mt_=mt_: e.copy(out=mt_[:], in_=psx[:]), reads=['psx'], writes=[('msb', mi)])
                return mi
            pend = []

            def flush(n):
                while len(pend) > n:
                    (kt_, h_, pi_, vA_, vkeys_, first_, last_) = pend.pop(0)
                    P.op('pe', lambda e, pi_=pi_, kt_=kt_, h_=h_, vA_=vA_, first_=first_, last_=last_: e.matmul(
                        pso[h_][0:65, :], lhsT=vA_[:, kt_, 0:65], rhs=PTm[pi_][:], start=first_, stop=last_),
                        reads=[('PTm', pi_)] + vkeys_, writes=[('pso', h_)])
            kmin = [0] * 4
            for h in range(4):
                while kmin[h] < kts and q0 - (128 * kmin[h] + 127) > dcut(u, h):
                    kmin[h] += 1
            ktlo = min(kmin)

            kt0w = max(0, 4 * qc - 4)
            kminw = [max(kt0w, kmin[h]) for h in range(4)]
            for kt in range(kt0w, kts + 1):
                dl = 128 * kt - q0
                for h in range(4):
                    if kt < kminw[h]:
                        continue
                    pt, pk = pss3.next()
                    P.op('pe', lambda e, pt=pt, kt=kt, h=h: e.matmul(pt[:], lhsT=kwA[0:65, kt * 128:(kt + 1) * 128], rhs=qa[0:65, h, :], start=True, stop=True),
                         reads=['kwA', 'kwA1'] + qkeys, writes=[pk])
                    pi = cnt['pt'] % NPT
                    cnt['pt'] += 1
                    P.op('act', lambda e, pt=pt, pi=pi, h=h, dl=dl: e.activation(out=PT[pi][:], in_=pt[:], func=AF.Exp, bias=bsw[:, u, h, dl // 128 + 63:dl // 128 + 64], scale=SCALE),
                         reads=[pk, ('bsw', u)], writes=[('PT', pi)])
                    eng = 'pool' if h % 2 == 0 else 'pool'
                    if dl >= 0:
                        P.op(eng, lambda e, pi=pi, dl=dl: e.affine_select(out=PTm[pi][:], in_=PT[pi][:], pattern=[[1, QC]], compare_op=ALU.is_ge,
                                                                        fill=0.0, base=-dl, channel_multiplier=-1),
                             reads=[('PT', pi)], writes=[('PTm', pi)])
                    else:
                        P.op(eng, lambda e, pi=pi, dl=dl: e.affine_select(out=PTm[pi][:], in_=PT[pi][:], pattern=[[-1, QC]], compare_op=ALU.is_ge,
                                                                        fill=0.0, base=dl + 511, channel_multiplier=1),
                             reads=[('PT', pi)], writes=[('PTm', pi)])
                    pend.append((kt, h, pi, vwA, ['vwA', 'vwA1'], kt == kminw[h], kt == kts))
                    flush(SK)
            flush(0)
            for h in range(4):
                epilogue(h, 2)
            for ts in range(4):
                P.op('pe', lambda e, ts=ts: e.transpose(psx[:, ts * 128:(ts + 1) * 128], sc2s[ts][:], identf[:]), reads=[('sc2', ts), 'identf'], writes=['psx'])
            P.op('act', lambda e: e.copy(out=selT[:], in_=psx[:]), reads=['psx'], writes=['selT'])

            mi_next = expand(ktlo)
            for kt in range(ktlo, kts + 1):
                dl = 128 * kt - q0
                mi = mi_next
                mt_ = msb[mi]
                hs = [h for h in range(4) if kt >= kmin[h]]
                for h in hs:
                    pt, pk = pss3.next()
                    P.op('pe', lambda e, pt=pt, kt=kt, h=h: e.matmul(pt[:], lhsT=ksA[0:65, kt * 128:(kt + 1) * 128], rhs=qa[0:65, h, :], start=True, stop=True),
                         reads=['ksA', 'ksA1'] + qkeys, writes=[pk])
                    if h == hs[0] and kt < kts:
                        mi_next = expand(kt + 1)
                    pi = cnt['pt'] % NPT
                    cnt['pt'] += 1
                    P.op('act', lambda e, pt=pt, pi=pi, h=h, dl=dl: e.activation(out=PT[pi][:], in_=pt[:], func=AF.Exp, bias=bsw[:, u, h, dl // 128 + 63:dl // 128 + 64], scale=SCALE),
                         reads=[pk, ('bsw', u)], writes=[('PT', pi)])
                    if dl >= 0:
                        P.op('pool', lambda e, pi=pi, dl=dl: e.affine_select(out=PT[pi][:], in_=PT[pi][:], pattern=[[1, QC]], compare_op=ALU.is_ge,
                                                                           fill=0.0, base=-dl, channel_multiplier=-1),
                             reads=[('PT', pi)], writes=[('PT', pi)])
                        eng = 'dve'
                    else:
                        eng = 'dve' if h % 2 == 0 else 'pool'
                    P.op(eng, lambda e, pi=pi, mt_=mt_: e.tensor_tensor(out=PTm[pi][:], in0=PT[pi][:], in1=mt_[:], op=ALU.mult),
                         reads=[('PT', pi), ('msb', mi)], writes=[('PTm', pi)])
                    pend.append((kt, h, pi, vsA, ['vsA', 'vsA1'], kt == kmin[h], kt == kts))
                    flush(SK)
            flush(0)
            for h in range(4):
                epilogue(h, 1)

            for br in range(3 if dbg else 1):
                if not fused:
                    dst = A['o_dst'](u, br, q0).rearrange("(ts p) f -> p ts f", p=128)
                    P.dma('sp', dst, acc[slot][br][:], reads=[('acc', slot, br, h) for h in range(4)],
                          writes=[('o', u, qc, br)], semkey=('osem', slot, br))
                else:
                    for fb in range(2):
                        for ts in range(4):
                            P.op('pe', lambda e, ts=ts, fb=fb, br=br: e.transpose(psx[:, ts * 128:(ts + 1) * 128], acc[slot][br][:, ts, fb * 128:(fb + 1) * 128], identf[:]),
                                 reads=[('acc', slot, br, 2 * fb), ('acc', slot, br, 2 * fb + 1), 'identf'], writes=['psx'])
                        P.op('act', lambda e, fb=fb: e.copy(out=oT[fb][:], in_=psx[:]), reads=['psx'], writes=[('oT', fb)])
                        P.dma('sp', A['oT_dst'](u, fb, q0), oT[fb][:], reads=[('oT', fb)], writes=[('o', u, qc, fb)], semkey=('osem', fb))

    for u in range(nunits):
        do_unit(u)
        for qc in range(nqc):
            do_chunk(u, qc)
    P.phase_end()


def run_att(projs, w1k, w2k, w1v, w2v, pe_k, pe_v):
    nc = _prog(('att',), build_att_prog)
    consts = att_consts()
    maps = []
    for c in range(NCORES):
        b, hh = c // 2, c % 2
        full = np.concatenate([projs[2 * b], projs[2 * b + 1]], axis=1)
        qs, ks, vcs, vts, gs, augs, bsws, bcs = [], [], [], [], [], [], [], []
        for u in range(2):
            g = 2 * hh + u
            qs.append(full[g * 256:(g + 1) * 256].reshape(4, 64, S))
            kv = lambda i: full[1024 + i * 256 + g * 64:1024 + i * 256 + (g + 1) * 64]
            ks.append(np.stack([kv(0), kv(2), kv(4)]))
            vcs.append(kv(1))
            vts.append(np.stack([kv(3).T, kv(5).T]))
            gs.append(full[2560 + g * 12:2560 + (g + 1) * 12].T)
            aug, bsw, bc = att_tables(g)
            augs.append(aug[None])
            bsws.append(bsw)
            bcs.append(bc)
        m = {"qT": np.ascontiguousarray(np.stack(qs)), "kT": np.ascontiguousarray(np.stack(ks)), "vcT": np.ascontiguousarray(np.stack(vcs)),
             "vtok": np.ascontiguousarray(np.stack(vts)), "gates": np.ascontiguousarray(np.stack(gs)),
             "w1k": w1k, "w2k": w2k, "w1v": w1v, "w2v": w2v,
             "pek": np.ascontiguousarray(pe_k.reshape(16, 128).T), "pev": np.ascontiguousarray(pe_v.reshape(16, 128).T),
             "augrow": np.ascontiguousarray(np.stack(augs)), "bias_sw": np.ascontiguousarray(np.stack(bsws)), "bias_c": np.ascontiguousarray(np.stack(bcs))}
        m.update(consts)
        maps.append(m)
    res = _run(nc, maps)
    mixs = []
    for c in range(NCORES):
        b, hh = c // 2, c % 2
        parts = []
        for g in range(4):
            src = res[2 * b + g // 2]["o"][g % 2]
            parts.append(src[hh * TPC:(hh + 1) * TPC].T)
        mixs.append(np.ascontiguousarray(np.concatenate(parts, axis=0)))
    return mixs


def kernel_unfused(x, norm_mix, norm_ffn, norm_final,
           rg_w_in, rg_conv_w, rg_conv_b, rg_w_a, rg_b_a, rg_w_x, rg_b_x, rg_lambda, rg_w_out,
           nsa_w_in, nsa_b_gate, nsa_pe_k, nsa_pe_v, nsa_w1_k, nsa_w2_k, nsa_w1_v, nsa_w2_v, nsa_w_out,
           mlp_w_up, mlp_w_down):
    f = lambda a: np.ascontiguousarray(np.asarray(a, dtype=np.float32))
    x = f(x)
    xls = [to_xl(x[c // 2, (c % 2) * TPC:(c % 2 + 1) * TPC]) for c in range(NCORES)]
    r = run_token(xls, False, False, 'rg', g_nxt=f(norm_mix[0]), w_in=f(rg_w_in[0]))
    projs = [r[c]['proj'] for c in range(NCORES)]
    for i in range(4):
        j = i // 2
        if i % 2 == 0:
            mixs = run_scan(projs, f(rg_conv_w[j]), f(rg_conv_b[j]), f(rg_w_a[j]), f(rg_b_a[j]), f(rg_w_x[j]), f(rg_b_x[j]), f(rg_lambda[j]))
            w_out = f(rg_w_out[j])
        else:
            mixs = run_att(projs, f(nsa_w1_k[j]), f(nsa_w2_k[j]), f(nsa_w1_v[j]), f(nsa_w2_v[j]), f(nsa_pe_k[j]), f(nsa_pe_v[j]))
            w_out = f(nsa_w_out[j])
        if i == 3:
            r = run_token(xls, True, True, 'final', mixs=mixs, w_out=w_out, g_ffn=f(norm_ffn[i]), w_up=f(mlp_w_up[i]), w_down=f(mlp_w_down[i]),
                          g_nxt=f(norm_final))
            break
        if i % 2 == 0:
            r = run_token(xls, True, True, 'nsa', mixs=mixs, w_out=w_out, g_ffn=f(norm_ffn[i]), w_up=f(mlp_w_up[i]), w_down=f(mlp_w_down[i]),
                          g_nxt=f(norm_mix[i + 1]), w_in=f(nsa_w_in[(i + 1) // 2]), b_gate=f(nsa_b_gate[(i + 1) // 2]))
        else:
            r = run_token(xls, True, True, 'rg', mixs=mixs, w_out=w_out, g_ffn=f(norm_ffn[i]), w_up=f(mlp_w_up[i]), w_down=f(mlp_w_down[i]),
                          g_nxt=f(norm_mix[i + 1]), w_in=f(rg_w_in[(i + 1) // 2]))
        xls = [r[c]['x_out'] for c in range(NCORES)]
        projs = [r[c]['proj'] for c in range(NCORES)]
    out = np.empty((B, S, D), np.float32)
    for c in range(NCORES):
        out[c // 2, (c % 2) * TPC:(c % 2 + 1) * TPC] = from_xl(r[c]['y_out'])
    return out


NBLK_F = S // TB


def build_fused_prog():
    P = Prog()
    x_in = P.dram("x_in", [NBLK_F, 128, 8, TB], F32, "ExternalInput")
    y_out = P.dram("y_out", [NBLK_F, 128, 8, TB], F32, "ExternalOutput")
    gmix = P.dram("gmix", [4, 128, 8], F32, "ExternalInput")
    gffn = P.dram("gffn", [4, 128, 8], F32, "ExternalInput")
    gfin = P.dram("gfin", [128, 8], F32, "ExternalInput")
    rg_w_in = P.dram("rg_w_in", [2, D, 2048], F32, "ExternalInput")
    rg_cpar = P.dram("rg_cpar", [2, 4, 128, 2, 8], F32, "ExternalInput")
    rg_w_a = P.dram("rg_w_a", [2, 4, 256, 256], F32, "ExternalInput")
    rg_w_x = P.dram("rg_w_x", [2, 4, 256, 256], F32, "ExternalInput")
    rg_w_out = P.dram("rg_w_out", [2, D, D], F32, "ExternalInput")
    nsa_w_in = P.dram("nsa_w_in", [2, D, NSA_COLS], F32, "ExternalInput")
    nsa_b_gate = P.dram("nsa_b_gate", [2, 48, 1], F32, "ExternalInput")
    nsa_pek = P.dram("nsa_pek", [2, 128, 16], F32, "ExternalInput")
    nsa_pev = P.dram("nsa_pev", [2, 128, 16], F32, "ExternalInput")
    nsa_w1_k = P.dram("nsa_w1_k", [2, 2048, 64], F32, "ExternalInput")
    nsa_w2_k = P.dram("nsa_w2_k", [2, 64, 64], F32, "ExternalInput")
    nsa_w1_v = P.dram("nsa_w1_v", [2, 2048, 64], F32, "ExternalInput")
    nsa_w2_v = P.dram("nsa_w2_v", [2, 64, 64], F32, "ExternalInput")
    nsa_w_out = P.dram("nsa_w_out", [2, D, D], F32, "ExternalInput")
    mlp_w_up = P.dram("mlp_w_up", [4, D, DFF], F32, "ExternalInput")
    mlp_w_down = P.dram("mlp_w_down", [4, DFF, D], F32, "ExternalInput")
    T = {}
    for nm, shp in (("augrow", [4, 1, 4, QC]), ("bias_sw", [4, 128, 4, 67]), ("bias_c", [4, 128, 4, 4, NQC]), ("cmpmap", [128, 4, 128]),
                    ("eexp", [128, 64, 128]), ("winmask", [128, 8, QC]), ("causal", [128, 4, QC]), ("validW", [128, 256]), ("addW", [128, 256]),
                    ("ident", [128, 128])):
        T[nm] = P.dram(nm, shp, F32, "ExternalInput")
    X = P.dram("X_scr", [NBLK_F, 128, 8, TB], F32, "Internal")
    PROJ = P.dram("PROJ_scr", [NSA_COLS, S], F32, "Internal")
    MIX = P.dram("MIX_scr", [D, S], F32, "Internal")

    def scan_phase(j):
        emit_scan(P, 4,
                  lambda u, jj, a, b: PROJ[u * 256 + jj * 128:u * 256 + (jj + 1) * 128, a:b],
                  lambda u, jj, a, b: PROJ[1024 + u * 256 + jj * 128:1024 + u * 256 + (jj + 1) * 128, a:b],
                  rg_cpar[j], rg_w_a[j], rg_w_x[j],
                  lambda u, jj, a, b: MIX[u * 256 + jj * 128:u * 256 + (jj + 1) * 128, a:b])

    def att_phase(j):
        A = dict(T)
        A.update(w1k=nsa_w1_k[j], w2k=nsa_w2_k[j], w1v=nsa_w1_v[j], w2v=nsa_w2_v[j], pek=nsa_pek[j], pev=nsa_pev[j])
        A['q_src'] = lambda u, q0: PROJ[u * 256:(u + 1) * 256, q0:q0 + QC].rearrange("(h d) t -> d h t", d=64)
        A['k_src'] = lambda u, i: PROJ[1024 + (2 * i) * 256 + u * 64:1024 + (2 * i) * 256 + (u + 1) * 64, :]
        A['vc_src'] = lambda u: PROJ[1024 + 256 + u * 64:1024 + 256 + (u + 1) * 64, :]
        A['vT_src'] = lambda u, i: PROJ[1024 + (3 + 2 * i) * 256 + u * 64:1024 + (3 + 2 * i) * 256 + (u + 1) * 64, :]
        A['gT_src'] = lambda u, q0: PROJ[2560 + u * 12:2560 + (u + 1) * 12, q0:q0 + QC]
        A['oT_dst'] = lambda u, fb, q0: MIX[u * 256 + fb * 128:u * 256 + (fb + 1) * 128, q0:q0 + QC]
        emit_att(P, A, dbg=False, nunits=4, nqc=NQC, fused=True, groups=[0, 1, 2, 3])

    emit_token(P, NBLK_F, x_in, x_in, False, False, 'rg', dict(g_nxt=gmix[0], w_in=rg_w_in[0], proj=PROJ))
    xsrc = x_in
    for i in range(4):
        j = i // 2
        if i % 2 == 0:
            scan_phase(j)
            w_out = rg_w_out[j]
        else:
            att_phase(j)
            w_out = nsa_w_out[j]
        A = dict(mix=MIX, w_out=w_out, g_ffn=gffn[i], w_up=mlp_w_up[i], w_down=mlp_w_down[i])
        if i == 3:
            A.update(g_nxt=gfin, y_out=y_out)
            emit_token(P, NBLK_F, xsrc, X, True, True, 'final', A)
        elif i % 2 == 0:
            A.update(g_nxt=gmix[i + 1], w_in=nsa_w_in[(i + 1) // 2], b_gate=nsa_b_gate[(i + 1) // 2], proj=PROJ)
            emit_token(P, NBLK_F, xsrc, X, True, True, 'nsa', A)
        else:
            A.update(g_nxt=gmix[i + 1], w_in=rg_w_in[(i + 1) // 2], proj=PROJ)
            emit_token(P, NBLK_F, xsrc, X, True, True, 'rg', A)
        xsrc = X
    return P.build()


def kernel(x, norm_mix, norm_ffn, norm_final,
                 rg_w_in, rg_conv_w, rg_conv_b, rg_w_a, rg_b_a, rg_w_x, rg_b_x, rg_lambda, rg_w_out,
                 nsa_w_in, nsa_b_gate, nsa_pe_k, nsa_pe_v, nsa_w1_k, nsa_w2_k, nsa_w1_v, nsa_w2_v, nsa_w_out,
                 mlp_w_up, mlp_w_down):
    f = lambda a: np.ascontiguousarray(np.asarray(a, dtype=np.float32))
    x = f(x)
    nc = _prog(('fused',), build_fused_prog)
    common = {
        "gmix": np.stack([gvec(f(norm_mix[i])) for i in range(4)]),
        "gffn": np.stack([gvec(f(norm_ffn[i])) for i in range(4)]),
        "gfin": gvec(f(norm_final)),
        "rg_w_in": f(rg_w_in), "rg_w_a": f(rg_w_a), "rg_w_x": f(rg_w_x), "rg_w_out": f(rg_w_out),
        "nsa_w_in": f(nsa_w_in), "nsa_b_gate": f(nsa_b_gate).reshape(2, 48, 1),
        "nsa_pek": np.ascontiguousarray(f(nsa_pe_k).reshape(2, 16, 128).transpose(0, 2, 1)),
        "nsa_pev": np.ascontiguousarray(f(nsa_pe_v).reshape(2, 16, 128).transpose(0, 2, 1)),
        "nsa_w1_k": f(nsa_w1_k), "nsa_w2_k": f(nsa_w2_k), "nsa_w1_v": f(nsa_w1_v), "nsa_w2_v": f(nsa_w2_v), "nsa_w_out": f(nsa_w_out),
        "mlp_w_up": f(mlp_w_up), "mlp_w_down": f(mlp_w_down),
    }
    cps = []
    for j in range(2):
        par = np.stack([f(rg_conv_w[j])[0], f(rg_conv_w[j])[1], f(rg_conv_w[j])[2], f(rg_conv_w[j])[3], f(rg_conv_b[j]), f(rg_b_a[j]), f(rg_b_x[j]),
                        f(rg_lambda[j])], axis=-1)
        cps.append(par.reshape(4, 2, 128, 8).transpose(0, 2, 1, 3))
    common["rg_cpar"] = np.ascontiguousarray(np.stack(cps))
    tabs = [att_tables(g) for g in range(4)]
    common["augrow"] = np.ascontiguousarray(np.stack([t[0][None] for t in tabs]))
    common["bias_sw"] = np.ascontiguousarray(np.stack([t[1] for t in tabs]))
    common["bias_c"] = np.ascontiguousarray(np.stack([t[2] for t in tabs]))
    common.update(att_consts())
    maps = []
    for c in range(NCORES):
        m = dict(common)
        xb = x[c % B]
        m["x_in"] = np.ascontiguousarray(xb.reshape(NBLK_F, TB, 8, 128).transpose(0, 3, 2, 1))
        maps.append(m)
    res = _run(nc, maps)
    out = np.empty((B, S, D), np.float32)
    for b in range(B):
        out[b] = res[b]["y_out"].transpose(0, 3, 2, 1).reshape(S, D)
    return out
```

```python
import numpy as np
from contextlib import ExitStack
import concourse.bass as bass
import concourse.mybir as mybir
from concourse.bass_utils import run_bass_kernel_spmd

F32 = mybir.dt.float32
BF16 = mybir.dt.bfloat16
AF = mybir.ActivationFunctionType
ALU = mybir.AluOpType

ENGS = ('sp', 'pe', 'act', 'dve', 'pool')
SAME_ENGINE_SYNC = True

D = 1024
S = 8192
B = 4
NCORES = 8
TPC = 4096
TB = 256
NBLK = TPC // TB
DFF = 4096
EPS = 1e-6
NSA_COLS = 2608


class Prog:
    def __init__(self):
        self.nc = bass.Bass("TRN2", target_bir_lowering=False)
        self.es = ExitStack()
        self.ops = {e: [] for e in ENGS}
        self.cnt = {e: 0 for e in ENGS}
        self.lastw = {}
        self.readers = {}
        self.waited = {e: {} for e in ENGS}
        self.dmasem = {}
        self.semnames = ['c_' + e for e in ENGS if e != 'sp']
        self.nuniq = 0
        self.floor = {}
        self.pes = None

    def phase_begin(self):
        self.pes = ExitStack()

    def phase_end(self):
        for e in ENGS:
            if e != 'sp' and self.cnt[e] > 0:
                self.floor['c_' + e] = self.cnt[e]
        for name, c in self.dmasem.values():
            if c > 0:
                self.floor[name] = c
        self.pes.close()
        self.pes = None

    def sb(self, shape, dt, name=None):
        self.nuniq += 1
        return (self.pes or self.es).enter_context(self.nc.sbuf_tensor(name or f"sb{self.nuniq}", list(shape), dt))

    def ps(self, shape, dt, name=None):
        self.nuniq += 1
        return (self.pes or self.es).enter_context(self.nc.psum_tensor(name or f"ps{self.nuniq}", list(shape), dt))

    def dram(self, name, shape, dt, kind):
        return self.nc.dram_tensor(name, list(shape), dt, kind=kind).ap()

    def _deps(self, reads, writes):
        deps = {}

        def add(s, v):
            if deps.get(s, 0) < v:
                deps[s] = v
        for k in reads:
            t = self.lastw.get(k)
            if t is not None:
                add(*t)
        for k in writes:
            t = self.lastw.get(k)
            if t is not None:
                add(*t)
            for s, v in self.readers.get(k, {}).items():
                add(s, v)
        return deps

    def _commit(self, tok, reads, writes):
        s, v = tok
        for k in reads:
            r = self.readers.setdefault(k, {})
            if r.get(s, 0) < v:
                r[s] = v
        for k in writes:
            self.lastw[k] = tok
            self.readers[k] = {}

    def _waits(self, eng, deps):
        ws = []
        own = 'c_' + eng
        for s, v in self.floor.items():
            if deps.get(s, 0) < v:
                deps[s] = v
        for s, v in deps.items():
            if s == own and (eng == 'pe' or not SAME_ENGINE_SYNC):
                continue
            if self.waited[eng].get(s, 0) >= v:
                continue
            self.waited[eng][s] = v
            ws.append((s, v))
        return ws

    def op(self, eng, fn, reads=(), writes=()):
        deps = self._deps(reads, writes)
        ws = self._waits(eng, deps)
        self.cnt[eng] += 1
        tok = ('c_' + eng, self.cnt[eng])
        self.ops[eng].append((ws, fn, (tok[0], 1)))
        self._commit(tok, reads, writes)
        return tok

    def dma(self, q, out, in_, reads=(), writes=(), semkey=None):
        semkey = semkey if semkey is not None else writes[0]
        if semkey not in self.dmasem:
            name = f"d{len(self.dmasem)}"
            self.dmasem[semkey] = [name, 0]
            self.semnames.append(name)
        ent = self.dmasem[semkey]
        deps = self._deps(reads, writes)
        ws = self._waits(q, deps)
        ent[1] += 16
        tok = (ent[0], ent[1])
        self.ops[q].append((ws, lambda e: e.dma_start(out=out, in_=in_), (ent[0], 16)))
        self._commit(tok, reads, writes)
        return tok

    def build(self):
        nc = self.nc
        final = {}
        for e in ENGS:
            if e != 'sp' and self.cnt[e] > 0:
                final['c_' + e] = self.cnt[e]
        for name, c in self.dmasem.values():
            final[name] = c
        fws = list(final.items())
        with ExitStack() as es:
            sems = {n: es.enter_context(nc.semaphore(n)) for n in self.semnames}
            ops = self.ops

            def replay(name, e):
                for ws, fn, inc in ops[name]:
                    for s, v in ws:
                        e.wait_ge(sems[s], v)
                    fn(e).then_inc(sems[inc[0]], inc[1])
                if name == 'sp':
                    for s, v in fws:
                        e.wait_ge(sems[s], v)
            with nc.Block() as block:
                @block.sync
                def _(e):
                    replay('sp', e)

                @block.tensor
                def _(e):
                    replay('pe', e)

                @block.scalar
                def _(e):
                    replay('act', e)

                @block.vector
                def _(e):
                    replay('dve', e)

                @block.gpsimd
                def _(e):
                    replay('pool', e)
        self.es.close()
        return nc


class PsumRot:
    def __init__(self, P, n, prefix):
        self.tiles = [P.ps([128, 512], F32) for _ in range(n)]
        self.keys = [(prefix, i) for i in range(n)]
        self.i = 0

    def next(self):
        t, k = self.tiles[self.i], self.keys[self.i]
        self.i = (self.i + 1) % len(self.tiles)
        return t, k


def build_token_prog(has_post, has_mlp, nxt):
    P = Prog()
    A = {}
    x_in = P.dram("x_in", [NBLK, 128, 8, TB], F32, "ExternalInput")
    write_x = has_post or has_mlp
    if write_x and nxt != 'final':
        x_out = P.dram("x_out", [NBLK, 128, 8, TB], F32, "ExternalOutput")
    elif write_x:
        x_out = P.dram("x_scr", [NBLK, 128, 8, TB], F32, "Internal")
    else:
        x_out = x_in
    if has_post:
        A['mix'] = P.dram("mix", [D, TPC], F32, "ExternalInput")
        A['w_out'] = P.dram("w_out", [D, D], F32, "ExternalInput")
    if has_mlp:
        A['g_ffn'] = P.dram("g_ffn", [128, 8], F32, "ExternalInput")
        A['w_up'] = P.dram("w_up", [D, DFF], F32, "ExternalInput")
        A['w_down'] = P.dram("w_down", [DFF, D], F32, "ExternalInput")
    A['g_nxt'] = P.dram("g_nxt", [128, 8], F32, "ExternalInput")
    if nxt == 'nsa':
        A['b_gate'] = P.dram("b_gate", [48, 1], F32, "ExternalInput")
    if nxt != 'final':
        ncols = 2048 if nxt == 'rg' else NSA_COLS
        A['w_in'] = P.dram("w_in", [D, ncols], F32, "ExternalInput")
        A['proj'] = P.dram("proj", [ncols, TPC], F32, "ExternalOutput")
    else:
        A['y_out'] = P.dram("y_out", [NBLK, 128, 8, TB], F32, "ExternalOutput")
    emit_token(P, NBLK, x_in, x_out, has_post, has_mlp, nxt, A)
    return P.build()


def emit_token(P, NBLK, x_in, x_out, has_post, has_mlp, nxt, A):
    write_x = has_post or has_mlp
    mix, w_out = A.get('mix'), A.get('w_out')
    g_ffn, w_up, w_down = A.get('g_ffn'), A.get('w_up'), A.get('w_down')
    g_nxt, b_gate, w_in, proj, y_out = A.get('g_nxt'), A.get('b_gate'), A.get('w_in'), A.get('proj'), A.get('y_out')
    ncols = 2048 if nxt == 'rg' else NSA_COLS
    P.phase_begin()
    WA = P.sb([128, 8 * DFF], BF16)
    WB = P.sb([128, 32 * D], BF16)
    WO = P.sb([128, 8 * D], BF16)
    xbuf = [P.sb([128, 8, TB], F32) for _ in range(2)]
    mbuf = [P.sb([128, 8, TB], BF16) for _ in range(2)]
    hb = P.sb([128, 8, TB], BF16)
    sq = P.sb([128, 8, TB], BF16)
    rs = P.sb([128, TB], F32)
    hb2 = P.sb([128, 8, TB], BF16)
    rs2 = P.sb([128, TB], F32)
    u2 = P.sb([128, 32, TB], BF16)
    rt = [P.sb([128, TB], F32) for _ in range(2)]
    st = [P.sb([128, TB], F32) for _ in range(4)]
    ones_bf = P.sb([128, 128], BF16)
    gf = P.sb([128, 8], F32)
    gn = P.sb([128, 8], F32)
    P.eps_tile = P.sb([128, 1], F32)
    bg = P.sb([48, 1], F32)
    psr = PsumRot(P, 6, 'ps')
    psn = P.ps([128, 512], F32)

    P.op('pool', lambda e: e.memset(ones_bf[:], 1.0), writes=['ones'])
    P.op('pool', lambda e: e.memset(P.eps_tile[:], EPS), writes=['epsc'])
    if has_mlp:
        P.dma('sp', gf[:], g_ffn, writes=['gvec_f'])
    P.dma('sp', gn[:], g_nxt, writes=['gvec_n'])
    if nxt == 'nsa':
        P.dma('sp', bg[:], b_gate, writes=['bg'])

    WAv = WA[:].rearrange("p (c f) -> p c f", c=8)
    WBv = WB[:].rearrange("p (c f) -> p c f", c=32)
    WOv = WO[:].rearrange("p (c f) -> p c f", c=8)
    if has_post:
        wsrc = w_out.rearrange("(c p) f -> p c f", p=128)
        for c in range(0, 8, 4):
            P.dma('pool', WOv[:, c:c + 4, :], wsrc[:, c:c + 4, :], writes=[('WO', c)])
        wo_keys = [('WO', 0), ('WO', 4)]
    if has_mlp:
        wsrc = w_up.rearrange("(c p) f -> p c f", p=128)
        for c in range(8):
            P.dma('pool', WAv[:, c, :], wsrc[:, c, :], writes=[('WA', c)])
        wsrc = w_down.rearrange("(c p) f -> p c f", p=128)
        for c in range(0, 32, 4):
            P.dma('pool', WBv[:, c:c + 4, :], wsrc[:, c:c + 4, :], writes=[('WB', c)])

    mixv = mix.rearrange("(c p) t -> p c t", p=128) if has_post else None

    def xkeys(slot):
        return [('xb', slot, c) for c in range(8)]

    if write_x:
        hbs = [hb, hb2]
        sqs = [sq, sq]
        rss = [rs, rs2]

        def load(blk):
            sl = blk % 2
            P.dma('sp', xbuf[sl][:], x_in[blk], writes=xkeys(sl), semkey=('xbsem', sl))
            if has_post:
                P.dma('pool', mbuf[sl][:], mixv[:, :, blk * TB:(blk + 1) * TB], writes=[('mb', sl)])

        def stageA(blk):
            slot = blk % 2
            xb = xbuf[slot]
            xk = xkeys(slot)
            if has_post:
                mb = mbuf[slot]
                for cc in range(8):
                    pt, pk = psr.next()
                    for kc in range(8):
                        P.op('pe', lambda e, pt=pt, kc=kc, cc=cc, mb=mb: e.matmul(
                            pt[:, :TB], lhsT=WOv[:, kc, cc * 128:(cc + 1) * 128], rhs=mb[:, kc, :], start=(kc == 0), stop=(kc == 7)),
                            reads=[('mb', slot)] + wo_keys, writes=[pk])
                    P.op('dve', lambda e, pt=pt, cc=cc, xb=xb: e.tensor_tensor(out=xb[:, cc, :], in0=pt[:, :TB], in1=xb[:, cc, :], op=ALU.add),
                         reads=[pk, xk[cc]], writes=[xk[cc]])
            if has_mlp:
                hk = [('hb', slot, c) for c in range(8)]
                emit_norm_k(P, xb, xk, gf, 'gvec_f', hbs[slot], hk, ones_bf, sqs[slot], psn, rss[slot], sfx=slot, sqk=0)

        def stageB(blk):
            slot = blk % 2
            hk = [('hb', slot, c) for c in range(8)]
            hcur = hbs[slot]
            for fc in range(32):
                pt, pk = psr.next()
                for kc in range(8):
                    P.op('pe', lambda e, pt=pt, kc=kc, fc=fc, hcur=hcur: e.matmul(
                        pt[:, :TB], lhsT=WAv[:, kc, fc * 128:(fc + 1) * 128], rhs=hcur[:, kc, :], start=(kc == 0), stop=(kc == 7)),
                        reads=[hk[kc], ('WA', kc)], writes=[pk])
                r = rt[fc % 2]
                P.op('act', lambda e, pt=pt, r=r: e.activation(out=r[:], in_=pt[:, :TB], func=AF.Relu),
                     reads=[pk], writes=[('rt', fc % 2)])
                eng = 'dve' if fc % 2 == 0 else 'pool'
                P.op(eng, lambda e, r=r, fc=fc: e.tensor_tensor(out=u2[:, fc, :], in0=r[:], in1=r[:], op=ALU.mult),
                     reads=[('rt', fc % 2)], writes=[('u2', fc)])

        def stageC(blk):
            slot = blk % 2
            xb = xbuf[slot]
            xk = xkeys(slot)
            if has_mlp:
                for cc in range(8):
                    pt, pk = psr.next()
                    for fc in range(32):
                        P.op('pe', lambda e, pt=pt, fc=fc, cc=cc: e.matmul(
                            pt[:, :TB], lhsT=WBv[:, fc, cc * 128:(cc + 1) * 128], rhs=u2[:, fc, :], start=(fc == 0), stop=(fc == 31)),
                            reads=[('u2', fc), ('WB', (fc // 4) * 4)], writes=[pk])
                    P.op('dve', lambda e, pt=pt, cc=cc, xb=xb: e.tensor_tensor(out=xb[:, cc, :], in0=pt[:, :TB], in1=xb[:, cc, :], op=ALU.add),
                         reads=[pk, xk[cc]], writes=[xk[cc]])
            P.dma('sp', x_out[blk], xb[:], reads=xk, writes=[('xout', blk)], semkey=('xosem', slot))

        load(0)
        if NBLK > 1:
            load(1)
        stageA(0)
        for blk in range(NBLK):
            if has_mlp:
                stageB(blk)
            if blk + 1 < NBLK:
                stageA(blk + 1)
            stageC(blk)
            if blk + 2 < NBLK:
                load(blk + 2)

    if nxt != 'final':
        nch = (ncols + 127) // 128
        WIv = WA[:, 0:8 * ncols].rearrange("p (c f) -> p c f", c=8)
        wsrc = w_in.rearrange("(c p) f -> p c f", p=128)
        for c in range(8):
            P.dma('pool', WIv[:, c, :], wsrc[:, c, :], writes=[('WA', c)])
    hbs3 = [hb, hb2]

    def p3_load(blk):
        sl = blk % 2
        P.dma('sp', xbuf[sl][:], x_out[blk], reads=[('xout', blk)] if write_x else [], writes=xkeys(sl), semkey=('xbsem', sl))

    def p3_norm(blk):
        sl = blk % 2
        emit_norm_k(P, xbuf[sl], xkeys(sl), gn, 'gvec_n', hbs3[sl], [('hb', sl, c) for c in range(8)], ones_bf, sq, psn, rs)

    if nxt == 'final':
        for blk in range(NBLK):
            slot = blk % 2
            xb = xbuf[slot]
            xk = xkeys(slot)
            p3_load(blk)
            emit_norm_k(P, xb, xk, gn, 'gvec_n', None, [('st', c % 4) for c in range(8)], ones_bf, sq, psn, rs,
                        outs=lambda c: st[c % 4][:],
                        after=lambda c: P.dma('sp', y_out[blk][:, c, :], st[c % 4][:], reads=[('st', c % 4)], writes=[('yout', blk, c)],
                                              semkey=('stsem', c % 4)))
    else:
        p3_load(0)
        if NBLK > 1:
            p3_load(1)
        p3_norm(0)
        for blk in range(NBLK):
            slot = blk % 2
            hcur = hbs3[slot]
            hk = [('hb', slot, c) for c in range(8)]
            for cc in range(nch):
                if cc == nch // 2 and blk + 1 < NBLK:
                    p3_norm(blk + 1)
                m = min(128, ncols - cc * 128)
                pt, pk = psr.next()
                for kc in range(8):
                    P.op('pe', lambda e, pt=pt, kc=kc, cc=cc, m=m, hcur=hcur: e.matmul(
                        pt[0:m, :TB], lhsT=WIv[:, kc, cc * 128:cc * 128 + m], rhs=hcur[:, kc, :], start=(kc == 0), stop=(kc == 7)),
                        reads=[hk[kc], ('WA', kc)], writes=[pk])
                s_ = st[cc % 4]
                sk = ('st', cc % 4)
                if nxt == 'rg' and cc < 8:
                    P.op('act', lambda e, pt=pt, s_=s_: e.activation(out=s_[:], in_=pt[:, :TB], func=AF.Gelu_apprx_tanh), reads=[pk], writes=[sk])
                elif nxt == 'nsa' and m < 128:
                    P.op('act', lambda e, pt=pt, s_=s_, m=m: e.activation(out=s_[0:m, :], in_=pt[0:m, :TB], func=AF.Sigmoid, bias=bg[:]),
                         reads=[pk, 'bg'], writes=[sk])
                else:
                    P.op('dve', lambda e, pt=pt, s_=s_: e.tensor_copy(s_[:], pt[:, :TB]), reads=[pk], writes=[sk])
                P.dma('sp', proj[cc * 128:cc * 128 + m, blk * TB:(blk + 1) * TB], s_[0:m, :], reads=[sk], writes=[('proj', blk, cc)],
                      semkey=('stsem', cc % 4))
            if blk + 2 < NBLK:
                p3_load(blk + 2)
    P.phase_end()


def emit_norm_k(P, xb, xk, g_sb, gkey, hb, hk, ones_bf, sq, psn, rs, sfx=0, sqk=0, outs=None, after=None):
    P.op('act', lambda e: e.activation(out=sq[:], in_=xb[:], func=AF.Square), reads=xk, writes=[('sq', sqk)])
    for c in range(8):
        P.op('pe', lambda e, c=c: e.matmul(psn[:, :TB], lhsT=ones_bf[:], rhs=sq[:, c, :], start=(c == 0), stop=(c == 7)),
             reads=[('sq', sqk), 'ones'], writes=['psn'])
    P.op('act', lambda e: e.activation(out=rs[:], in_=psn[:, :TB], func=AF.Sqrt, bias=P.eps_tile[:], scale=1.0 / D),
         reads=['psn', 'epsc'], writes=[('rs', sfx)])
    P.op('dve', lambda e: e.reciprocal(rs[:], rs[:]), reads=[('rs', sfx)], writes=[('rs', sfx)])
    for c in range(8):
        dst = outs(c) if outs is not None else hb[:, c, :]
        P.op('dve', lambda e, c=c, dst=dst: e.scalar_tensor_tensor(out=dst, in0=xb[:, c, :], scalar=g_sb[:, c:c + 1], in1=rs[:],
                                                                 op0=ALU.mult, op1=ALU.mult),
             reads=[xk[c], ('rs', sfx), gkey], writes=[hk[c]])
        if after is not None:
            after(c)


SEG = 2048
NSEG = S // SEG


def build_scan_prog():
    P = Prog()
    ysrc = P.dram("ysrc", [2, 256, S], F32, "ExternalInput")
    xpsrc = P.dram("xpsrc", [2, 256, S], F32, "ExternalInput")
    cpar = P.dram("cpar", [2, 128, 2, 8], F32, "ExternalInput")
    wa = P.dram("wa", [2, 256, 256], F32, "ExternalInput")
    wx = P.dram("wx", [2, 256, 256], F32, "ExternalInput")
    mixo = P.dram("mixo", [2, 256, S], F32, "ExternalOutput")
    emit_scan(P, 2, lambda u, j, a, b: ysrc[u, j * 128:(j + 1) * 128, a:b], lambda u, j, a, b: xpsrc[u, j * 128:(j + 1) * 128, a:b],
              cpar, wa, wx, lambda u, j, a, b: mixo[u, j * 128:(j + 1) * 128, a:b])
    return P.build()


def emit_scan(P, NU, ysrc, xpsrc, cpar, wa, wx, mixo):
    P.phase_begin()
    cp = P.sb([128, NU, 2, 8], F32)
    cst = P.sb([128, NU, 2, 4], F32)
    wab = P.sb([128, NU, 2, 256], BF16)
    wxb = P.sb([128, NU, 2, 256], BF16)
    xp = [P.sb([128, 3 + SEG], F32) for _ in range(2)]
    xb = [P.sb([128, SEG], F32) for _ in range(2)]
    xbb = [P.sb([128, SEG], BF16) for _ in range(2)]
    rt = P.sb([128, SEG], F32)
    it = P.sb([128, SEG], F32)
    at = P.sb([128, SEG], F32)
    mt = P.sb([128, SEG], F32)
    ut = P.sb([128, SEG], F32)
    ht = P.sb([128, SEG], F32)
    yt = P.sb([128, SEG], F32)
    ot = P.sb([128, SEG], F32)
    hcar = P.sb([128, 2], F32)
    psr = PsumRot(P, 6, 'ps')

    for u in range(NU):
        P.dma('sp', cp[:, u], cpar[u], writes=[('cp', u)])
        P.dma('pool', wab[:, u], wa[u].rearrange("(j p) o -> p j o", p=128), writes=[('wab', u)])
        P.dma('pool', wxb[:, u], wx[u].rearrange("(j p) o -> p j o", p=128), writes=[('wxb', u)])
        for j in range(2):
            lam = cp[:, u, j, 7:8]
            P.op('act', lambda e, u=u, j=j, lam=lam: e.activation(out=cst[:, u, j, 0:1], in_=lam, func=AF.Exp, scale=-1.0),
                 reads=[('cp', u)], writes=[('cst', u, j)])
            P.op('act', lambda e, u=u, j=j: e.activation(out=cst[:, u, j, 1:2], in_=cst[:, u, j, 0:1], func=AF.Ln, bias=1.0),
                 reads=[('cst', u, j)], writes=[('cst', u, j)])
            P.op('dve', lambda e, u=u, j=j: e.tensor_scalar(out=cst[:, u, j, 2:3], in0=cst[:, u, j, 1:2], scalar1=-8.0, scalar2=None, op0=ALU.mult),
                 reads=[('cst', u, j)], writes=[('cst', u, j)])
            P.op('dve', lambda e, u=u, j=j: e.tensor_scalar(out=cst[:, u, j, 3:4], in0=cst[:, u, j, 1:2], scalar1=-16.0, scalar2=None, op0=ALU.mult),
                 reads=[('cst', u, j)], writes=[('cst', u, j)])

    for u in range(NU):
        for s in range(NSEG):
            t0 = s * SEG
            for j in range(2):
                rows = slice(j * 128, (j + 1) * 128)
                if s == 0:
                    P.op('pool', lambda e, j=j: e.memset(xp[j][:, 0:3], 0.0), writes=[('xp', j)])
                    P.dma('sp', xp[j][:, 3:3 + SEG], xpsrc(u, j, 0, SEG), writes=[('xp', j)], semkey=('xpsem', j))
                else:
                    P.dma('sp', xp[j][:, :], xpsrc(u, j, t0 - 3, t0 + SEG), writes=[('xp', j)], semkey=('xpsem', j))
                cw = lambda k, u=u, j=j: cp[:, u, j, k:k + 1]
                P.op('dve', lambda e, j=j, cw=cw: e.tensor_scalar(out=xb[j][:], in0=xp[j][:, 3:3 + SEG], scalar1=cw(3), scalar2=cw(4),
                                                                 op0=ALU.mult, op1=ALU.add),
                     reads=[('xp', j), ('cp', u)], writes=[('xb', j)])
                for k in (2, 1, 0):
                    P.op('dve', lambda e, j=j, k=k, cw=cw: e.scalar_tensor_tensor(out=xb[j][:], in0=xp[j][:, k:k + SEG], scalar=cw(k), in1=xb[j][:],
                                                                                 op0=ALU.mult, op1=ALU.add),
                         reads=[('xp', j), ('cp', u), ('xb', j)], writes=[('xb', j)])
                P.op('act', lambda e, j=j: e.copy(out=xbb[j][:], in_=xb[j][:]), reads=[('xb', j)], writes=[('xbb', j)])
            for j in range(2):
                rows = slice(j * 128, (j + 1) * 128)
                P.dma('sp', yt[:], ysrc(u, j, t0, t0 + SEG), writes=['yt'])
                for (wt, wkey, dst, dkey, bk) in ((wab, 'wab', rt, 'rt', 5), (wxb, 'wxb', it, 'it', 6)):
                    for tb in range(SEG // 512):
                        pt, pk = psr.next()
                        for jin in range(2):
                            P.op('pe', lambda e, pt=pt, wt=wt, jin=jin, j=j, tb=tb, u=u: e.matmul(
                                pt[:], lhsT=wt[:, u, jin, j * 128:(j + 1) * 128], rhs=xbb[jin][:, tb * 512:(tb + 1) * 512],
                                start=(jin == 0), stop=(jin == 1)),
                                reads=[('xbb', jin), (wkey, u)], writes=[pk])
                        P.op('act', lambda e, pt=pt, dst=dst, tb=tb, u=u, j=j, bk=bk: e.activation(
                            out=dst[:, tb * 512:(tb + 1) * 512], in_=pt[:], func=AF.Sigmoid, bias=cp[:, u, j, bk:bk + 1]),
                            reads=[pk, ('cp', u)], writes=[dkey])
                P.op('act', lambda e, u=u, j=j: e.activation(out=at[:], in_=rt[:], func=AF.Exp, scale=cst[:, u, j, 2:3]),
                     reads=['rt', ('cst', u, j)], writes=['at'])
                P.op('act', lambda e, u=u, j=j: e.activation(out=mt[:], in_=rt[:], func=AF.Exp, scale=cst[:, u, j, 3:4]),
                     reads=['rt', ('cst', u, j)], writes=['mt'])
                P.op('dve', lambda e: e.tensor_scalar(out=mt[:], in0=mt[:], scalar1=-1.0, scalar2=1.0, op0=ALU.mult, op1=ALU.add),
                     reads=['mt'], writes=['mt'])
                P.op('dve', lambda e: e.tensor_scalar_max(out=mt[:], in0=mt[:], scalar1=1e-20), reads=['mt'], writes=['mt'])
                P.op('act', lambda e: e.activation(out=mt[:], in_=mt[:], func=AF.Sqrt), reads=['mt'], writes=['mt'])
                P.op('pool', lambda e, j=j: e.tensor_tensor(out=ut[:], in0=it[:], in1=xb[j][:], op=ALU.mult),
                     reads=['it', ('xb', j)], writes=['ut'])
                P.op('dve', lambda e: e.tensor_tensor(out=ut[:], in0=ut[:], in1=mt[:], op=ALU.mult), reads=['ut', 'mt'], writes=['ut'])
                init = 0.0 if s == 0 else hcar[:, j:j + 1]
                P.op('dve', lambda e, init=init: e.tensor_tensor_scan(ht[:], at[:], ut[:], init, ALU.mult, ALU.add),
                     reads=['at', 'ut', ('hcar', j)], writes=['ht'])
                P.op('dve', lambda e, j=j: e.tensor_copy(hcar[:, j:j + 1], ht[:, SEG - 1:SEG]), reads=['ht'], writes=[('hcar', j)])
                P.op('pool', lambda e: e.tensor_tensor(out=ot[:], in0=ht[:], in1=yt[:], op=ALU.mult), reads=['ht', 'yt'], writes=['ot'])
                P.dma('sp', mixo(u, j, t0, t0 + SEG), ot[:], reads=['ot'], writes=[('mixo', u, j, s)], semkey=('osem',))
    P.phase_end()


_PROGS = {}


def _prog(key, fn):
    if key not in _PROGS:
        _PROGS[key] = fn()
    return _PROGS[key]


def _run(nc, in_maps):
    res = run_bass_kernel_spmd(nc, in_maps, core_ids=list(range(NCORES)))
    return res.results


def to_xl(xc):
    return np.ascontiguousarray(xc.reshape(NBLK, TB, 8, 128).transpose(0, 3, 2, 1))


def from_xl(xl):
    return np.ascontiguousarray(xl.transpose(0, 3, 2, 1).reshape(TPC, D))


def gvec(g):
    return np.ascontiguousarray(g.reshape(8, 128).T)


def run_token(xls, has_post, has_mlp, nxt, mixs=None, w_out=None, g_ffn=None, w_up=None, w_down=None, g_nxt=None, w_in=None, b_gate=None):
    nc = _prog(('tok', has_post, has_mlp, nxt), lambda: build_token_prog(has_post, has_mlp, nxt))
    maps = []
    for c in range(NCORES):
        m = {"x_in": xls[c], "g_nxt": gvec(g_nxt)}
        if has_post:
            m["mix"] = mixs[c]
            m["w_out"] = w_out
        if has_mlp:
            m["g_ffn"] = gvec(g_ffn)
            m["w_up"] = w_up
            m["w_down"] = w_down
        if nxt != 'final':
            m["w_in"] = w_in
        if nxt == 'nsa':
            m["b_gate"] = np.ascontiguousarray(b_gate.reshape(48, 1))
        maps.append(m)
    return _run(nc, maps)


def run_scan(projs, conv_w, conv_b, w_a, b_a, w_x, b_x, lam):
    nc = _prog(('scan',), build_scan_prog)
    maps = []
    for c in range(NCORES):
        b, hh = c // 2, c % 2
        ys, xs, cps, was, wxs = [], [], [], [], []
        for u in range(2):
            n = 2 * hh + u
            ch = slice(n * 256, (n + 1) * 256)
            ys.append(np.concatenate([projs[2 * b][ch], projs[2 * b + 1][ch]], axis=1))
            ch2 = slice(1024 + n * 256, 1024 + (n + 1) * 256)
            xs.append(np.concatenate([projs[2 * b][ch2], projs[2 * b + 1][ch2]], axis=1))
            par = np.stack([conv_w[0, ch], conv_w[1, ch], conv_w[2, ch], conv_w[3, ch], conv_b[ch], b_a[ch], b_x[ch], lam[ch]], axis=-1)
            cps.append(par.reshape(2, 128, 8).transpose(1, 0, 2))
            was.append(w_a[n])
            wxs.append(w_x[n])
        maps.append({"ysrc": np.ascontiguousarray(np.stack(ys)), "xpsrc": np.ascontiguousarray(np.stack(xs)),
                     "cpar": np.ascontiguousarray(np.stack(cps)), "wa": np.ascontiguousarray(np.stack(was)),
                     "wx": np.ascontiguousarray(np.stack(wxs))})
    res = _run(nc, maps)
    mixs = []
    for c in range(NCORES):
        b, hh = c // 2, c % 2
        parts = []
        for n in range(4):
            src = res[2 * b + n // 2]["mixo"][n % 2]
            parts.append(src[:, hh * TPC:(hh + 1) * TPC])
        mixs.append(np.ascontiguousarray(np.concatenate(parts, axis=0)))
    return mixs


QC = 512
NQC = S // QC
NCMP = 511
SCALE = 0.125


def att_tables(g):
    slopes = np.exp2(-8.0 * (np.arange(1, 17, dtype=np.float64)) / 16.0).reshape(4, 4)[g]
    tq = np.arange(QC, dtype=np.float64)
    aug = (-slopes[:, None] * tq[None, :] / SCALE).astype(np.float32)
    kk = np.arange(128, dtype=np.float64)
    dl = (np.arange(67, dtype=np.float64) - 63.0) * 128.0
    bias_sw = (slopes[None, :, None] * (dl[None, None, :] + kk[:, None, None])).astype(np.float32)
    kt = np.arange(4, dtype=np.float64)
    qc = np.arange(NQC, dtype=np.float64)
    pos = 16.0 * (128.0 * kt[None, :, None] + kk[:, None, None]) + 31.0 - QC * qc[None, None, :]
    bias_c = (slopes[None, :, None, None] * pos[:, None, :, :]).astype(np.float32)
    return aug, bias_sw, bias_c


def att_consts():
    n = np.arange(512)
    j = np.arange(128)
    ov = np.minimum(n[:, None] * 16 + 32, j[None, :] * 64 + 64) - np.maximum(n[:, None] * 16, j[None, :] * 64)
    cm = (np.maximum(ov, 0) / 16.0).astype(np.float32)
    cm[511] = 0.0
    cmpmap = np.ascontiguousarray(cm.reshape(4, 128, 128).transpose(1, 0, 2))
    jj = np.arange(128)[:, None, None]
    ktt = np.arange(64)[None, :, None]
    kk = np.arange(128)[None, None, :]
    eexp = (jj == 2 * ktt + kk // 64).astype(np.float32)
    kq = np.arange(128)[:, None, None]
    tq = np.arange(QC)[None, None, :]
    dw = (np.arange(8)[None, :, None] - 4) * 128
    d = tq - kq - dw
    winmask = ((d >= 0) & (d < 512)).astype(np.float32)
    dc = np.arange(4)[None, :, None] * 128
    causal = ((tq - kq - dc) >= 0).astype(np.float32)
    tt = np.arange(128)[:, None]
    w = np.arange(256)[None, :]
    jr = w - 126
    c = (tt >= 64).astype(np.int64)
    valid = jr <= c
    forced = (jr == c) | (jr == c - 1)
    tb = (300.0 - w) * 1e-35
    validW = valid.astype(np.float32)
    addW = np.where(valid, 1e4 * forced + tb, -1.0).astype(np.float32)
    ident = np.eye(128, dtype=np.float32)
    return dict(cmpmap=cmpmap, eexp=eexp, winmask=winmask, causal=causal, validW=validW, addW=addW, ident=ident)


def build_att_prog(dbg=False, nunits=2, nqc=NQC):
    P = Prog()
    A = {}
    qT = P.dram("qT", [2, 4, 64, S], F32, "ExternalInput")
    kT = P.dram("kT", [2, 3, 64, S], F32, "ExternalInput")
    vcT = P.dram("vcT", [2, 64, S], F32, "ExternalInput")
    vtok = P.dram("vtok", [2, 2, S, 64], F32, "ExternalInput")
    gates = P.dram("gates", [2, S, 12], F32, "ExternalInput")
    for nm, shp in (("w1k", [2048, 64]), ("w2k", [64, 64]), ("w1v", [2048, 64]), ("w2v", [64, 64]), ("pek", [128, 16]), ("pev", [128, 16]),
                    ("augrow", [2, 1, 4, QC]), ("bias_sw", [2, 128, 4, 67]), ("bias_c", [2, 128, 4, 4, NQC]), ("cmpmap", [128, 4, 128]),
                    ("eexp", [128, 64, 128]), ("winmask", [128, 8, QC]), ("causal", [128, 4, QC]), ("validW", [128, 256]), ("addW", [128, 256]),
                    ("ident", [128, 128])):
        A[nm] = P.dram(nm, shp, F32, "ExternalInput")
    o_d = P.dram("o", [2, 3, S, 256] if dbg else [2, S, 256], F32, "ExternalOutput")
    A['q_src'] = lambda u, q0: qT[u, :, :, q0:q0 + QC].rearrange("h d t -> d h t")
    A['k_src'] = lambda u, i: kT[u, i]
    A['vc_src'] = lambda u: vcT[u]
    A['vtok_src'] = lambda u, i: vtok[u, i]
    A['gates_src'] = lambda u, q0: gates[u, q0:q0 + QC, :]
    A['o_dst'] = (lambda u, br, q0: o_d[u, br, q0:q0 + QC, :]) if dbg else (lambda u, br, q0: o_d[u, q0:q0 + QC, :])
    emit_att(P, A, dbg=dbg, nunits=nunits, nqc=nqc, fused=False)
    return P.build()


def emit_att(P, A, dbg=False, nunits=2, nqc=NQC, fused=False, groups=None):
    P.phase_begin()
    ALIBI_CUT = 100.0

    def dcut(u, h):
        if groups is None:
            return 1e30
        return ALIBI_CUT / (2.0 ** (-(4 * groups[u] + h + 1) / 2.0))

    NU = nunits
    w1k, w2k, w1v, w2v, pek, pev = A['w1k'], A['w2k'], A['w1v'], A['w2v'], A['pek'], A['pev']
    augrow, bias_sw_d, bias_c_d = A['augrow'], A['bias_sw'], A['bias_c']
    cmpmap_d, eexp_d, winmask_d, causal_d = A['cmpmap'], A['eexp'], A['winmask'], A['causal']
    validW_d, addW_d, ident_d = A['validW'], A['addW'], A['ident']

    zk = P.sb([64, S], BF16)
    zv = P.sb([64, S], BF16)
    ksA = P.sb([65, S], BF16)
    kwA = P.sb([65, S], BF16)
    vsA = P.sb([128, 64, 65], BF16)
    vwA = P.sb([128, 64, 65], BF16)
    kcA = P.sb([65, 512], BF16)
    vcA = P.sb([128, 4, 65], BF16)
    eexp = P.sb([128, 64, 128], BF16)
    cmpb = P.sb([128, 4, 128], BF16)
    winm = P.sb([128, 8, QC], BF16)
    caus = P.sb([128, 4, QC], BF16)
    validW = P.sb([128, 256], F32)
    addW = P.sb([128, 256], F32)
    identf = P.sb([128, 128], F32)
    bsw = P.sb([128, NU, 4, 67], F32)
    bc = P.sb([128, NU, 4, 4, NQC], F32)
    w1d = [P.sb([64, 32, 64], BF16) for _ in range(2)]
    w1p = [P.sb([128, 16, 64], BF16) for _ in range(2)]
    w2b = [P.sb([64, 64], BF16) for _ in range(2)]
    pef = [P.sb([128, 16], BF16) for _ in range(2)]
    cb = P.sb([64, 2], F32)
    h1 = P.sb([64, 512], BF16)
    qbuf = [P.sb([65, 4, QC], BF16) for _ in range(2)]
    gt = [P.sb([128, 4, 12], F32) for _ in range(2)]
    acc = [[P.sb([128, 4, 256], F32) for _ in range(3 if dbg else 1)] for _ in range(2)]
    PTc = [P.sb([128, QC], BF16) for _ in range(4)]
    NPT = 9
    PT = [P.sb([128, QC], BF16) for _ in range(NPT)]
    PTm = [P.sb([128, QC], BF16) for _ in range(NPT)]
    msb = [P.sb([128, QC], BF16) for _ in range(3)]
    osb = [P.sb([65, QC], F32) for _ in range(4)]
    rl = [P.sb([128, 4], F32) for _ in range(4)]
    ff = [P.sb([128, 4], F32) for _ in range(4)]
    impacc = P.sb([128, 4, 128], F32)
    sc = P.sb([128, 128], F32)
    sc2s = [P.sb([128, 128], F32) for _ in range(4)]
    t8 = P.sb([128, 16], F32)
    selT = P.sb([128, QC], BF16)
    pso = [P.ps([128, 512], F32) for _ in range(4)]
    pss = PsumRot(P, 2, 'pss')
    psM = P.ps([128, 512], F32)
    pss3 = PsumRot(P, 0, 'pss3')
    pss3.tiles = pss.tiles + [psM]
    pss3.keys = pss.keys + ['psM']
    SK = 7
    psx = P.ps([128, 512], F32)
    if fused:
        vstage = P.sb([64, 2048], F32)
        gtf = [P.sb([12, QC], F32) for _ in range(2)]
        oT = [P.sb([128, QC], F32) for _ in range(2)]

    P.dma('pool', eexp[:], eexp_d, writes=['eexp'])
    P.dma('pool', cmpb[:], cmpmap_d, writes=['cmpb'])
    P.dma('pool', winm[:], winmask_d, writes=['winm'])
    P.dma('pool', caus[:], causal_d, writes=['caus'])
    P.dma('sp', validW[:], validW_d, writes=['validW'])
    P.dma('sp', addW[:], addW_d, writes=['addW'])
    P.dma('sp', identf[:], ident_d, writes=['identf'])
    for u in range(NU):
        P.dma('sp', bsw[:, u], bias_sw_d[u], writes=[('bsw', u)])
        P.dma('sp', bc[:, u], bias_c_d[u], writes=[('bc', u)])
    for i, (w1, w2, pe) in enumerate(((w1k, w2k, pek), (w1v, w2v, pev))):
        P.dma('pool', w1d[i][:], w1.rearrange("(l d) o -> d l o", d=64), writes=[('w1d', i)])
        P.dma('pool', w1p[i][:], w1.rearrange("(j p) o -> p j o", p=128), writes=[('w1p', i)])
        P.dma('pool', w2b[i][:], w2, writes=[('w2b', i)])
        P.dma('pool', pef[i][:], pe, writes=[('pef', i)])
    P.op('pool', lambda e: e.memset(ksA[64:65, :], 1.0), writes=['ksA1'])
    P.op('pool', lambda e: e.memset(kwA[64:65, :], 1.0), writes=['kwA1'])
    P.op('pool', lambda e: e.memset(kcA[64:65, :], 1.0), writes=['kcA1'])
    P.op('pool', lambda e: e.memset(vsA[:, :, 64:65], 1.0), writes=['vsA1'])
    P.op('pool', lambda e: e.memset(vwA[:, :, 64:65], 1.0), writes=['vwA1'])
    P.op('pool', lambda e: e.memset(vcA[:, :, 64:65], 1.0), writes=['vcA1'])
    P.op('pool', lambda e: e.memset(h1[:], 0.0), writes=['h1'])
    for i in range(2):
        for j in range(16):
            P.op('pe', lambda e, i=i, j=j: e.matmul(psx[0:64, i:i + 1], lhsT=w1p[i][:, j, :], rhs=pef[i][:, j:j + 1], start=(j == 0), stop=(j == 15)),
                 reads=[('w1p', i), ('pef', i)], writes=['psx'])
        P.op('act', lambda e, i=i: e.copy(out=cb[:, i:i + 1], in_=psx[0:64, i:i + 1]), reads=['psx'], writes=[('cb', i)])

    cnt = {'pt': 0, 'ptm': 0, 'msb': 0, 'osb': 0}

    def do_unit(u):
        P.dma('pool', zk[:], A['k_src'](u, 0), writes=['zk'])
        P.dma('pool', zv[:], A['vc_src'](u), writes=['zv'])
        P.dma('pool', ksA[0:64, :], A['k_src'](u, 1), writes=['ksA'])
        P.dma('pool', kwA[0:64, :], A['k_src'](u, 2), writes=['kwA'])
        if not fused:
            P.dma('pool', vsA[:, :, 0:64], A['vtok_src'](u, 0).rearrange("(kt p) d -> p kt d", p=128), writes=['vsA'])
            P.dma('pool', vwA[:, :, 0:64], A['vtok_src'](u, 1).rearrange("(kt p) d -> p kt d", p=128), writes=['vwA'])
        else:
            for i, (vA, vkey) in enumerate(((vsA, 'vsA'), (vwA, 'vwA'))):
                for pc in range(4):
                    P.dma('sp', vstage[:], A['vT_src'](u, i)[:, pc * 2048:(pc + 1) * 2048], writes=['vstage'])
                    for half in range(2):
                        for t8i in range(8):
                            c0 = half * 1024 + t8i * 128
                            P.op('pe', lambda e, t8i=t8i, c0=c0: e.transpose(psx[:, t8i * 64:(t8i + 1) * 64], vstage[0:64, c0:c0 + 128], identf[0:64, 0:64]),
                                 reads=['vstage', 'identf'], writes=['psx'])
                        kt0 = pc * 16 + half * 8
                        P.op('act', lambda e, vA=vA, kt0=kt0: e.copy(out=vA[:, kt0:kt0 + 8, 0:64], in_=psx[:].rearrange("p (a d) -> p a d", d=64)),
                             reads=['psx'], writes=[vkey])
        for slot in range(2):
            P.dma('pool', qbuf[slot][64:65, :, :], augrow[u], writes=[('qaug', slot)])
        for i, z in enumerate((zk, zv)):
            zkey = 'zk' if i == 0 else 'zv'
            pt, pk = pss.next()
            for l in range(32):
                P.op('pe', lambda e, pt=pt, i=i, l=l, z=z: e.matmul(pt[0:64, 0:NCMP], lhsT=w1d[i][:, l, :], rhs=z[:, l:l + 16 * (NCMP - 1) + 1:16],
                                                                  start=(l == 0), stop=(l == 31)),
                     reads=[zkey, ('w1d', i)], writes=[pk])
            P.op('act', lambda e, pt=pt, i=i: e.activation(out=h1[:, 0:NCMP], in_=pt[0:64, 0:NCMP], func=AF.Gelu_apprx_tanh, bias=cb[:, i:i + 1]),
                 reads=[pk, ('cb', i)], writes=['h1'])
            if i == 0:
                pt2, pk2 = pss.next()
                P.op('pe', lambda e, pt2=pt2: e.matmul(pt2[0:64, 0:NCMP], lhsT=w2b[0][:], rhs=h1[:, 0:NCMP], start=True, stop=True),
                     reads=['h1', ('w2b', 0)], writes=[pk2])
                P.op('act', lambda e, pt2=pt2: e.copy(out=kcA[0:64, 0:NCMP], in_=pt2[0:64, 0:NCMP]), reads=[pk2], writes=['kcA'])
            else:
                for kt in range(4):
                    nk = 127 if kt == 3 else 128
                    P.op('pe', lambda e, kt=kt, nk=nk: e.matmul(psx[0:nk, 0:64], lhsT=h1[:, kt * 128:kt * 128 + nk], rhs=w2b[1][:], start=True, stop=True),
                         reads=['h1', ('w2b', 1)], writes=['psx'])
                    P.op('act', lambda e, kt=kt, nk=nk: e.copy(out=vcA[0:nk, kt, 0:64], in_=psx[0:nk, 0:64]), reads=['psx'], writes=['vcA'])

    def do_chunk(u, qc):
        if True:
            q0 = qc * QC
            slot = qc % 2
            qa = qbuf[slot]
            def chunk_loads(qq):
                sl = qq % 2
                P.dma('pool', qbuf[sl][0:64, :, :], A['q_src'](u, qq * QC), writes=[('qa', sl)])
                if not fused:
                    P.dma('sp', gt[sl][:], A['gates_src'](u, qq * QC).rearrange("(ts p) k -> p ts k", p=128), writes=[('gt', sl)])
                else:
                    P.dma('sp', gtf[sl][:], A['gT_src'](u, qq * QC), writes=[('gtf', sl)])
            if qc == 0:
                chunk_loads(0)
            if qc + 1 < nqc:
                chunk_loads(qc + 1)
            if fused:
                for ts in range(4):
                    P.op('pe', lambda e, ts=ts: e.transpose(psx[:, ts * 12:(ts + 1) * 12], gtf[slot][0:12, ts * 128:(ts + 1) * 128], identf[0:12, 0:12]),
                         reads=[('gtf', slot), 'identf'], writes=['psx'])
                P.op('act', lambda e: e.copy(out=gt[slot][:], in_=psx[:, 0:48].rearrange("p (a k) -> p a k", k=12)), reads=['psx'], writes=[('gt', slot)])
            qkeys = [('qa', slot), ('qaug', slot)]

            def epilogue(h, br, with_imp=False):
                r = cnt['osb'] % 4
                cnt['osb'] += 1
                ob = osb[r]
                if br == 0:
                    pe_, pek_ = (psx, 'psx') if h % 2 == 0 else (psM, 'psM')
                    pm_, pmk_ = (psM, 'psM') if h % 2 == 0 else (psx, 'psx')
                else:
                    pe_, pek_ = [(psx, 'psx'), (pss.tiles[0], pss.keys[0]), (pss.tiles[1], pss.keys[1]), (psM, 'psM')][h]
                P.op('act', lambda e, ob=ob, h=h: e.copy(out=ob[:], in_=pso[h][0:65, :]), reads=[('pso', h)], writes=[('osb', r)])
                for ts in range(4):
                    P.op('pe', lambda e, ob=ob, ts=ts, pe_=pe_: e.transpose(pe_[:, ts * 65:(ts + 1) * 65], ob[0:65, ts * 128:(ts + 1) * 128], identf[0:65, 0:65]),
                         reads=[('osb', r), 'identf'], writes=[pek_])
                rr, fr = rl[r], ff[r]
                P.op('dve', lambda e, rr=rr, pe_=pe_: e.tensor_scalar_max(out=rr[:], in0=pe_[:, 64:64 + 65 * 4:65], scalar1=1e-30), reads=[pek_], writes=[('rl', r)])
                P.op('dve', lambda e, rr=rr: e.reciprocal(rr[:], rr[:]), reads=[('rl', r)], writes=[('rl', r)])
                gi = h * 3 + br
                P.op('dve', lambda e, rr=rr, fr=fr, gi=gi: e.tensor_tensor(out=fr[:], in0=rr[:], in1=gt[slot][:, :, gi], op=ALU.mult),
                     reads=[('rl', r), ('gt', slot)], writes=[('ff', r)])
                for ts in range(4):
                    dst = acc[slot][br if dbg else 0][:, ts, h * 64:(h + 1) * 64]
                    src = pe_[:, ts * 65:ts * 65 + 64]
                    if br == 0 or dbg:
                        P.op('dve', lambda e, dst=dst, src=src, fr=fr, ts=ts: e.tensor_scalar(out=dst, in0=src, scalar1=fr[:, ts:ts + 1], scalar2=None, op0=ALU.mult),
                             reads=[pek_, ('ff', r)], writes=[('acc', slot, br if dbg else 0, h)])
                    else:
                        P.op('dve', lambda e, dst=dst, src=src, fr=fr, ts=ts: e.scalar_tensor_tensor(out=dst, in0=src, scalar=fr[:, ts:ts + 1], in1=dst,
                                                                                                  op0=ALU.mult, op1=ALU.add),
                             reads=[pek_, ('ff', r), ('acc', slot, 0, h)], writes=[('acc', slot, 0, h)])
                if with_imp:
                    for ts in range(4):
                        dst = impacc[:, ts, :]
                        src = pm_[:, ts * 128:(ts + 1) * 128]
                        if h == 0:
                            P.op('dve', lambda e, dst=dst, src=src, rr=rr, ts=ts: e.tensor_scalar(out=dst, in0=src, scalar1=rr[:, ts:ts + 1], scalar2=None, op0=ALU.mult),
                                 reads=[pmk_, ('rl', r)], writes=['impacc'])
                        else:
                            P.op('dve', lambda e, dst=dst, src=src, rr=rr, ts=ts: e.scalar_tensor_tensor(out=dst, in0=src, scalar=rr[:, ts:ts + 1], in1=dst,
                                                                                                      op0=ALU.mult, op1=ALU.add),
                                 reads=[pmk_, ('rl', r), 'impacc'], writes=['impacc'])

            ktc = min(3, (32 * qc + 30) // 128)
            for h in range(4):
                k0h = 0
                while k0h < ktc and q0 - (16 * (128 * k0h + 127) + 31) > dcut(u, h):
                    k0h += 1
                for kt in range(k0h, ktc + 1):
                    nk = 127 if kt == 3 else 128
                    pt, pk = pss.next()
                    P.op('pe', lambda e, pt=pt, kt=kt, nk=nk, h=h: e.matmul(pt[0:nk, :], lhsT=kcA[0:65, kt * 128:kt * 128 + nk], rhs=qa[0:65, h, :], start=True, stop=True),
                         reads=['kcA', 'kcA1'] + qkeys, writes=[pk])
                    P.op('act', lambda e, pt=pt, kt=kt, nk=nk, h=h: e.activation(out=PTc[kt][0:nk, :], in_=pt[0:nk, :], func=AF.Exp,
                                                                              bias=bc[0:nk, u, h, kt, qc:qc + 1], scale=SCALE),
                         reads=[pk, ('bc', u)], writes=[('PTc', kt)])
                    if q0 - 16 * (128 * kt + nk - 1) - 31 < 0:
                        P.op('pool', lambda e, kt=kt, nk=nk: e.affine_select(out=PTc[kt][0:nk, :], in_=PTc[kt][0:nk, :], pattern=[[1, QC]],
                                                                           compare_op=ALU.is_ge, fill=0.0, base=q0 - 2048 * kt - 31, channel_multiplier=-16),
                             reads=[('PTc', kt)], writes=[('PTc', kt)])
                for kt in range(k0h, ktc + 1):
                    nk = 127 if kt == 3 else 128
                    P.op('pe', lambda e, kt=kt, nk=nk, h=h, k0h=k0h: e.matmul(pso[h][0:65, :], lhsT=vcA[0:nk, kt, 0:65], rhs=PTc[kt][0:nk, :],
                                                                  start=(kt == k0h), stop=(kt == ktc)),
                         reads=[('PTc', kt), 'vcA', 'vcA1'], writes=[('pso', h)])
                for ts in range(4):
                    for kt in range(k0h, ktc + 1):
                        nk = 127 if kt == 3 else 128
                        P.op('pe', lambda e, kt=kt, nk=nk, ts=ts, k0h=k0h, pmb=(psM if h % 2 == 0 else psx): e.matmul(pmb[:, ts * 128:(ts + 1) * 128], lhsT=PTc[kt][0:nk, ts * 128:(ts + 1) * 128],
                                                                        rhs=cmpb[0:nk, kt, :], start=(kt == k0h), stop=(kt == ktc)),
                             reads=[('PTc', kt), 'cmpb'], writes=['psM' if h % 2 == 0 else 'psx'])
                epilogue(h, 0, with_imp=True)

            for ts in range(4):
                w0 = 126 - 2 * (4 * qc + ts)
                P.op('dve', lambda e, ts=ts, w0=w0: e.tensor_tensor(out=sc[:], in0=impacc[:, ts, :], in1=validW[:, w0:w0 + 128], op=ALU.mult),
                     reads=['impacc', 'validW'], writes=['sc'])
                P.op('dve', lambda e, w0=w0, ts=ts: e.tensor_tensor(out=sc[:], in0=sc[:], in1=addW[:, w0:w0 + 128], op=ALU.add),
                     reads=['sc', 'addW'], writes=['sc'])
                P.op('dve', lambda e, ts=ts: e.tensor_scalar_add(out=sc[:, 0:1], in0=sc[:, 0:1], scalar1=1e4), reads=['sc'], writes=['sc'])
                P.op('dve', lambda e, ts=ts: e.max(t8[:, 0:8], sc[:]), reads=['sc'], writes=['t8'])
                P.op('dve', lambda e, ts=ts: e.match_replace(sc2s[ts][:], t8[:, 0:8], sc[:], -1e30), reads=['sc', 't8'], writes=[('sc2', ts)])
                P.op('dve', lambda e, ts=ts: e.max(t8[:, 8:16], sc2s[ts][:]), reads=[('sc2', ts)], writes=['t8'])
                P.op('dve', lambda e, ts=ts: e.tensor_scalar(out=sc2s[ts][:], in0=sc[:], scalar1=t8[:, 15:16], scalar2=None, op0=ALU.is_ge), reads=['sc', 't8'], writes=[('sc2', ts)])
                P.op('dve', lambda e, w0=w0, ts=ts: e.tensor_tensor(out=sc2s[ts][:], in0=sc2s[ts][:], in1=validW[:, w0:w0 + 128], op=ALU.mult), reads=[('sc2', ts), 'validW'], writes=[('sc2', ts)])

            kts = 4 * qc + 3

            def expand(kt):
                mi = cnt['msb'] % 3
                cnt['msb'] += 1
                mt_ = msb[mi]
                P.op('pe', lambda e, kt=kt: e.matmul(psx[:], lhsT=eexp[:, kt, :], rhs=selT[:], start=True, stop=True), reads=['eexp', 'selT'], writes=['psx'])
                P.op('act', lambda e, mt_=mt_: e.copy(out=mt_[:], in_=psx[:]), reads=['psx'], writes=[('msb', mi)])
                return mi
            pend = []

            def flush(n):
                while len(pend) > n:
                    (kt_, h_, pi_, vA_, vkeys_, first_, last_) = pend.pop(0)
                    P.op('pe', lambda e, pi_=pi_, kt_=kt_, h_=h_, vA_=vA_, first_=first_, last_=last_: e.matmul(
                        pso[h_][0:65, :], lhsT=vA_[:, kt_, 0:65], rhs=PTm[pi_][:], start=first_, stop=last_),
                        reads=[('PTm', pi_)] + vkeys_, writes=[('pso', h_)])
            kmin = [0] * 4
            for h in range(4):
                while kmin[h] < kts and q0 - (128 * kmin[h] + 127) > dcut(u, h):
                    kmin[h] += 1
            ktlo = min(kmin)

            kt0w = max(0, 4 * qc - 4)
            kminw = [max(kt0w, kmin[h]) for h in range(4)]
            for kt in range(kt0w, kts + 1):
                dl = 128 * kt - q0
                for h in range(4):
                    if kt < kminw[h]:
                        continue
                    pt, pk = pss3.next()
                    P.op('pe', lambda e, pt=pt, kt=kt, h=h: e.matmul(pt[:], lhsT=kwA[0:65, kt * 128:(kt + 1) * 128], rhs=qa[0:65, h, :], start=True, stop=True),
                         reads=['kwA', 'kwA1'] + qkeys, writes=[pk])
                    pi = cnt['pt'] % NPT
                    cnt['pt'] += 1
                    P.op('act', lambda e, pt=pt, pi=pi, h=h, dl=dl: e.activation(out=PT[pi][:], in_=pt[:], func=AF.Exp, bias=bsw[:, u, h, dl // 128 + 63:dl // 128 + 64], scale=SCALE),
                         reads=[pk, ('bsw', u)], writes=[('PT', pi)])
                    eng = 'pool' if h % 2 == 0 else 'pool'
                    if dl >= 0:
                        P.op(eng, lambda e, pi=pi, dl=dl: e.affine_select(out=PTm[pi][:], in_=PT[pi][:], pattern=[[1, QC]], compare_op=ALU.is_ge,
                                                                        fill=0.0, base=-dl, channel_multiplier=-1),
                             reads=[('PT', pi)], writes=[('PTm', pi)])
                    else:
                        P.op(eng, lambda e, pi=pi, dl=dl: e.affine_select(out=PTm[pi][:], in_=PT[pi][:], pattern=[[-1, QC]], compare_op=ALU.is_ge,
                                                                        fill=0.0, base=dl + 511, channel_multiplier=1),
                             reads=[('PT', pi)], writes=[('PTm', pi)])
                    pend.append((kt, h, pi, vwA, ['vwA', 'vwA1'], kt == kminw[h], kt == kts))
                    flush(SK)
            flush(0)
            for h in range(4):
                epilogue(h, 2)
            for ts in range(4):
                P.op('pe', lambda e, ts=ts: e.transpose(psx[:, ts * 128:(ts + 1) * 128], sc2s[ts][:], identf[:]), reads=[('sc2', ts), 'identf'], writes=['psx'])
            P.op('act', lambda e: e.copy(out=selT[:], in_=psx[:]), reads=['psx'], writes=['selT'])

            mi_next = expand(ktlo)
            for kt in range(ktlo, kts + 1):
                dl = 128 * kt - q0
                mi = mi_next
                mt_ = msb[mi]
                hs = [h for h in range(4) if kt >= kmin[h]]
                for h in hs:
                    pt, pk = pss3.next()
                    P.op('pe', lambda e, pt=pt, kt=kt, h=h: e.matmul(pt[:], lhsT=ksA[0:65, kt * 128:(kt + 1) * 128], rhs=qa[0:65, h, :], start=True, stop=True),
                         reads=['ksA', 'ksA1'] + qkeys, writes=[pk])
                    if h == hs[0] and kt < kts:
                        mi_next = expand(kt + 1)
                    pi = cnt['pt'] % NPT
                    cnt['pt'] += 1
                    P.op('act', lambda e, pt=pt, pi=pi, h=h, dl=dl: e.activation(out=PT[pi][:], in_=pt[:], func=AF.Exp, bias=bsw[:, u, h, dl // 128 + 63:dl // 128 + 64], scale=SCALE),
                         reads=[pk, ('bsw', u)], writes=[('PT', pi)])
                    if dl >= 0:
                        P.op('pool', lambda e, pi=pi, dl=dl: e.affine_select(out=PT[pi][:], in_=PT[pi][:], pattern=[[1, QC]], compare_op=ALU.is_ge,
                                                                           fill=0.0, base=-dl, channel_multiplier=-1),
                             reads=[('PT', pi)], writes=[('PT', pi)])
                        eng = 'dve'
                    else:
                        eng = 'dve' if h % 2 == 0 else 'pool'
                    P.op(eng, lambda e, pi=pi, mt_=mt_: e.tensor_tensor(out=PTm[pi][:], in0=PT[pi][:], in1=mt_[:], op=ALU.mult),
                         reads=[('PT', pi), ('msb', mi)], writes=[('PTm', pi)])
                    pend.append((kt, h, pi, vsA, ['vsA', 'vsA1'], kt == kmin[h], kt == kts))
                    flush(SK)
            flush(0)
            for h in range(4):
                epilogue(h, 1)

            for br in range(3 if dbg else 1):
                if not fused:
                    dst = A['o_dst'](u, br, q0).rearrange("(ts p) f -> p ts f", p=128)
                    P.dma('sp', dst, acc[slot][br][:], reads=[('acc', slot, br, h) for h in range(4)],
                          writes=[('o', u, qc, br)], semkey=('osem', slot, br))
                else:
                    for fb in range(2):
                        for ts in range(4):
                            P.op('pe', lambda e, ts=ts, fb=fb, br=br: e.transpose(psx[:, ts * 128:(ts + 1) * 128], acc[slot][br][:, ts, fb * 128:(fb + 1) * 128], identf[:]),
                                 reads=[('acc', slot, br, 2 * fb), ('acc', slot, br, 2 * fb + 1), 'identf'], writes=['psx'])
                        P.op('act', lambda e, fb=fb: e.copy(out=oT[fb][:], in_=psx[:]), reads=['psx'], writes=[('oT', fb)])
                        P.dma('sp', A['oT_dst'](u, fb, q0), oT[fb][:], reads=[('oT', fb)], writes=[('o', u, qc, fb)], semkey=('osem', fb))

    for u in range(nunits):
        do_unit(u)
        for qc in range(nqc):
            do_chunk(u, qc)
    P.phase_end()


def run_att(projs, w1k, w2k, w1v, w2v, pe_k, pe_v):
    nc = _prog(('att',), build_att_prog)
    consts = att_consts()
    maps = []
    for c in range(NCORES):
        b, hh = c // 2, c % 2
        full = np.concatenate([projs[2 * b], projs[2 * b + 1]], axis=1)
        qs, ks, vcs, vts, gs, augs, bsws, bcs = [], [], [], [], [], [], [], []
        for u in range(2):
            g = 2 * hh + u
            qs.append(full[g * 256:(g + 1) * 256].reshape(4, 64, S))
            kv = lambda i: full[1024 + i * 256 + g * 64:1024 + i * 256 + (g + 1) * 64]
            ks.append(np.stack([kv(0), kv(2), kv(4)]))
            vcs.append(kv(1))
            vts.append(np.stack([kv(3).T, kv(5).T]))
            gs.append(full[2560 + g * 12:2560 + (g + 1) * 12].T)
            aug, bsw, bc = att_tables(g)
            augs.append(aug[None])
            bsws.append(bsw)
            bcs.append(bc)
        m = {"qT": np.ascontiguousarray(np.stack(qs)), "kT": np.ascontiguousarray(np.stack(ks)), "vcT": np.ascontiguousarray(np.stack(vcs)),
             "vtok": np.ascontiguousarray(np.stack(vts)), "gates": np.ascontiguousarray(np.stack(gs)),
             "w1k": w1k, "w2k": w2k, "w1v": w1v, "w2v": w2v,
             "pek": np.ascontiguousarray(pe_k.reshape(16, 128).T), "pev": np.ascontiguousarray(pe_v.reshape(16, 128).T),
             "augrow": np.ascontiguousarray(np.stack(augs)), "bias_sw": np.ascontiguousarray(np.stack(bsws)), "bias_c": np.ascontiguousarray(np.stack(bcs))}
        m.update(consts)
        maps.append(m)
    res = _run(nc, maps)
    mixs = []
    for c in range(NCORES):
        b, hh = c // 2, c % 2
        parts = []
        for g in range(4):
            src = res[2 * b + g // 2]["o"][g % 2]
            parts.append(src[hh * TPC:(hh + 1) * TPC].T)
        mixs.append(np.ascontiguousarray(np.concatenate(parts, axis=0)))
    return mixs


def kernel_unfused(x, norm_mix, norm_ffn, norm_final,
           rg_w_in, rg_conv_w, rg_conv_b, rg_w_a, rg_b_a, rg_w_x, rg_b_x, rg_lambda, rg_w_out,
           nsa_w_in, nsa_b_gate, nsa_pe_k, nsa_pe_v, nsa_w1_k, nsa_w2_k, nsa_w1_v, nsa_w2_v, nsa_w_out,
           mlp_w_up, mlp_w_down):
    f = lambda a: np.ascontiguousarray(np.asarray(a, dtype=np.float32))
    x = f(x)
    xls = [to_xl(x[c // 2, (c % 2) * TPC:(c % 2 + 1) * TPC]) for c in range(NCORES)]
    r = run_token(xls, False, False, 'rg', g_nxt=f(norm_mix[0]), w_in=f(rg_w_in[0]))
    projs = [r[c]['proj'] for c in range(NCORES)]
    for i in range(4):
        j = i // 2
        if i % 2 == 0:
            mixs = run_scan(projs, f(rg_conv_w[j]), f(rg_conv_b[j]), f(rg_w_a[j]), f(rg_b_a[j]), f(rg_w_x[j]), f(rg_b_x[j]), f(rg_lambda[j]))
            w_out = f(rg_w_out[j])
        else:
            mixs = run_att(projs, f(nsa_w1_k[j]), f(nsa_w2_k[j]), f(nsa_w1_v[j]), f(nsa_w2_v[j]), f(nsa_pe_k[j]), f(nsa_pe_v[j]))
            w_out = f(nsa_w_out[j])
        if i == 3:
            r = run_token(xls, True, True, 'final', mixs=mixs, w_out=w_out, g_ffn=f(norm_ffn[i]), w_up=f(mlp_w_up[i]), w_down=f(mlp_w_down[i]),
                          g_nxt=f(norm_final))
            break
        if i % 2 == 0:
            r = run_token(xls, True, True, 'nsa', mixs=mixs, w_out=w_out, g_ffn=f(norm_ffn[i]), w_up=f(mlp_w_up[i]), w_down=f(mlp_w_down[i]),
                          g_nxt=f(norm_mix[i + 1]), w_in=f(nsa_w_in[(i + 1) // 2]), b_gate=f(nsa_b_gate[(i + 1) // 2]))
        else:
            r = run_token(xls, True, True, 'rg', mixs=mixs, w_out=w_out, g_ffn=f(norm_ffn[i]), w_up=f(mlp_w_up[i]), w_down=f(mlp_w_down[i]),
                          g_nxt=f(norm_mix[i + 1]), w_in=f(rg_w_in[(i + 1) // 2]))
        xls = [r[c]['x_out'] for c in range(NCORES)]
        projs = [r[c]['proj'] for c in range(NCORES)]
    out = np.empty((B, S, D), np.float32)
    for c in range(NCORES):
        out[c // 2, (c % 2) * TPC:(c % 2 + 1) * TPC] = from_xl(r[c]['y_out'])
    return out


NBLK_F = S // TB


def build_fused_prog():
    P = Prog()
    x_in = P.dram("x_in", [NBLK_F, 128, 8, TB], F32, "ExternalInput")
    y_out = P.dram("y_out", [NBLK_F, 128, 8, TB], F32, "ExternalOutput")
    gmix = P.dram("gmix", [4, 128, 8], F32, "ExternalInput")
    gffn = P.dram("gffn", [4, 128, 8], F32, "ExternalInput")
    gfin = P.dram("gfin", [128, 8], F32, "ExternalInput")
    rg_w_in = P.dram("rg_w_in", [2, D, 2048], F32, "ExternalInput")
    rg_cpar = P.dram("rg_cpar", [2, 4, 128, 2, 8], F32, "ExternalInput")
    rg_w_a = P.dram("rg_w_a", [2, 4, 256, 256], F32, "ExternalInput")
    rg_w_x = P.dram("rg_w_x", [2, 4, 256, 256], F32, "ExternalInput")
    rg_w_out = P.dram("rg_w_out", [2, D, D], F32, "ExternalInput")
    nsa_w_in = P.dram("nsa_w_in", [2, D, NSA_COLS], F32, "ExternalInput")
    nsa_b_gate = P.dram("nsa_b_gate", [2, 48, 1], F32, "ExternalInput")
    nsa_pek = P.dram("nsa_pek", [2, 128, 16], F32, "ExternalInput")
    nsa_pev = P.dram("nsa_pev", [2, 128, 16], F32, "ExternalInput")
    nsa_w1_k = P.dram("nsa_w1_k", [2, 2048, 64], F32, "ExternalInput")
    nsa_w2_k = P.dram("nsa_w2_k", [2, 64, 64], F32, "ExternalInput")
    nsa_w1_v = P.dram("nsa_w1_v", [2, 2048, 64], F32, "ExternalInput")
    nsa_w2_v = P.dram("nsa_w2_v", [2, 64, 64], F32, "ExternalInput")
    nsa_w_out = P.dram("nsa_w_out", [2, D, D], F32, "ExternalInput")
    mlp_w_up = P.dram("mlp_w_up", [4, D, DFF], F32, "ExternalInput")
    mlp_w_down = P.dram("mlp_w_down", [4, DFF, D], F32, "ExternalInput")
    T = {}
    for nm, shp in (("augrow", [4, 1, 4, QC]), ("bias_sw", [4, 128, 4, 67]), ("bias_c", [4, 128, 4, 4, NQC]), ("cmpmap", [128, 4, 128]),
                    ("eexp", [128, 64, 128]), ("winmask", [128, 8, QC]), ("causal", [128, 4, QC]), ("validW", [128, 256]), ("addW", [128, 256]),
                    ("ident", [128, 128])):
        T[nm] = P.dram(nm, shp, F32, "ExternalInput")
    X = P.dram("X_scr", [NBLK_F, 128, 8, TB], F32, "Internal")
    PROJ = P.dram("PROJ_scr", [NSA_COLS, S], F32, "Internal")
    MIX = P.dram("MIX_scr", [D, S], F32, "Internal")

    def scan_phase(j):
        emit_scan(P, 4,
                  lambda u, jj, a, b: PROJ[u * 256 + jj * 128:u * 256 + (jj + 1) * 128, a:b],
                  lambda u, jj, a, b: PROJ[1024 + u * 256 + jj * 128:1024 + u * 256 + (jj + 1) * 128, a:b],
                  rg_cpar[j], rg_w_a[j], rg_w_x[j],
                  lambda u, jj, a, b: MIX[u * 256 + jj * 128:u * 256 + (jj + 1) * 128, a:b])

    def att_phase(j):
        A = dict(T)
        A.update(w1k=nsa_w1_k[j], w2k=nsa_w2_k[j], w1v=nsa_w1_v[j], w2v=nsa_w2_v[j], pek=nsa_pek[j], pev=nsa_pev[j])
        A['q_src'] = lambda u, q0: PROJ[u * 256:(u + 1) * 256, q0:q0 + QC].rearrange("(h d) t -> d h t", d=64)
        A['k_src'] = lambda u, i: PROJ[1024 + (2 * i) * 256 + u * 64:1024 + (2 * i) * 256 + (u + 1) * 64, :]
        A['vc_src'] = lambda u: PROJ[1024 + 256 + u * 64:1024 + 256 + (u + 1) * 64, :]
        A['vT_src'] = lambda u, i: PROJ[1024 + (3 + 2 * i) * 256 + u * 64:1024 + (3 + 2 * i) * 256 + (u + 1) * 64, :]
        A['gT_src'] = lambda u, q0: PROJ[2560 + u * 12:2560 + (u + 1) * 12, q0:q0 + QC]
        A['oT_dst'] = lambda u, fb, q0: MIX[u * 256 + fb * 128:u * 256 + (fb + 1) * 128, q0:q0 + QC]
        emit_att(P, A, dbg=False, nunits=4, nqc=NQC, fused=True, groups=[0, 1, 2, 3])

    emit_token(P, NBLK_F, x_in, x_in, False, False, 'rg', dict(g_nxt=gmix[0], w_in=rg_w_in[0], proj=PROJ))
    xsrc = x_in
    for i in range(4):
        j = i // 2
        if i % 2 == 0:
            scan_phase(j)
            w_out = rg_w_out[j]
        else:
            att_phase(j)
            w_out = nsa_w_out[j]
        A = dict(mix=MIX, w_out=w_out, g_ffn=gffn[i], w_up=mlp_w_up[i], w_down=mlp_w_down[i])
        if i == 3:
            A.update(g_nxt=gfin, y_out=y_out)
            emit_token(P, NBLK_F, xsrc, X, True, True, 'final', A)
        elif i % 2 == 0:
            A.update(g_nxt=gmix[i + 1], w_in=nsa_w_in[(i + 1) // 2], b_gate=nsa_b_gate[(i + 1) // 2], proj=PROJ)
            emit_token(P, NBLK_F, xsrc, X, True, True, 'nsa', A)
        else:
            A.update(g_nxt=gmix[i + 1], w_in=rg_w_in[(i + 1) // 2], proj=PROJ)
            emit_token(P, NBLK_F, xsrc, X, True, True, 'rg', A)
        xsrc = X
    return P.build()


def kernel(x, norm_mix, norm_ffn, norm_final,
                 rg_w_in, rg_conv_w, rg_conv_b, rg_w_a, rg_b_a, rg_w_x, rg_b_x, rg_lambda, rg_w_out,
                 nsa_w_in, nsa_b_gate, nsa_pe_k, nsa_pe_v, nsa_w1_k, nsa_w2_k, nsa_w1_v, nsa_w2_v, nsa_w_out,
                 mlp_w_up, mlp_w_down):
    f = lambda a: np.ascontiguousarray(np.asarray(a, dtype=np.float32))
    x = f(x)
    nc = _prog(('fused',), build_fused_prog)
    common = {
        "gmix": np.stack([gvec(f(norm_mix[i])) for i in range(4)]),
        "gffn": np.stack([gvec(f(norm_ffn[i])) for i in range(4)]),
        "gfin": gvec(f(norm_final)),
        "rg_w_in": f(rg_w_in), "rg_w_a": f(rg_w_a), "rg_w_x": f(rg_w_x), "rg_w_out": f(rg_w_out),
        "nsa_w_in": f(nsa_w_in), "nsa_b_gate": f(nsa_b_gate).reshape(2, 48, 1),
        "nsa_pek": np.ascontiguousarray(f(nsa_pe_k).reshape(2, 16, 128).transpose(0, 2, 1)),
        "nsa_pev": np.ascontiguousarray(f(nsa_pe_v).reshape(2, 16, 128).transpose(0, 2, 1)),
        "nsa_w1_k": f(nsa_w1_k), "nsa_w2_k": f(nsa_w2_k), "nsa_w1_v": f(nsa_w1_v), "nsa_w2_v": f(nsa_w2_v), "nsa_w_out": f(nsa_w_out),
        "mlp_w_up": f(mlp_w_up), "mlp_w_down": f(mlp_w_down),
    }
    cps = []
    for j in range(2):
        par = np.stack([f(rg_conv_w[j])[0], f(rg_conv_w[j])[1], f(rg_conv_w[j])[2], f(rg_conv_w[j])[3], f(rg_conv_b[j]), f(rg_b_a[j]), f(rg_b_x[j]),
                        f(rg_lambda[j])], axis=-1)
        cps.append(par.reshape(4, 2, 128, 8).transpose(0, 2, 1, 3))
    common["rg_cpar"] = np.ascontiguousarray(np.stack(cps))
    tabs = [att_tables(g) for g in range(4)]
    common["augrow"] = np.ascontiguousarray(np.stack([t[0][None] for t in tabs]))
    common["bias_sw"] = np.ascontiguousarray(np.stack([t[1] for t in tabs]))
    common["bias_c"] = np.ascontiguousarray(np.stack([t[2] for t in tabs]))
    common.update(att_consts())
    maps = []
    for c in range(NCORES):
        m = dict(common)
        xb = x[c % B]
        m["x_in"] = np.ascontiguousarray(xb.reshape(NBLK_F, TB, 8, 128).transpose(0, 3, 2, 1))
        maps.append(m)
    res = _run(nc, maps)
    out = np.empty((B, S, D), np.float32)
    for b in range(B):
        out[b] = res[b]["y_out"].transpose(0, 3, 2, 1).reshape(S, D)
    return out
```
